# Optimizing a Trainium2 kernel written in Bass

```python
import math
import jax, jax.numpy as jnp
from jax import lax
import numpy as np

D_MODEL = 1024
BATCH = 4
SEQ = 8192
DEPTH = 2

D_MIX = D_MODEL
SSM_WIDTH = D_MIX // 4
SSM_GROUP = 16
SSM_GROUPS = SSM_WIDTH // SSM_GROUP
SSM_STATE = 64
DIFF_HEAD_DIM = 64
DIFF_V_DIM = 2 * DIFF_HEAD_DIM
DIFF_WIDTH = D_MIX // 2
DIFF_HEADS = DIFF_WIDTH // DIFF_V_DIM
MOBA_HEAD_DIM = 64
MOBA_WIDTH = D_MIX // 4
MOBA_HEADS = MOBA_WIDTH // MOBA_HEAD_DIM
IN_WIDTH = SSM_WIDTH + 3 * DIFF_WIDTH + 3 * MOBA_WIDTH
D_FF = 4 * D_MODEL
ROPE_THETA = 500000.0
ROPE_FRACTION = 4
MOBA_BLOCK = 256
MOBA_TOPK = 3
MOBA_Q_CHUNK = 64
ATTN_Q_BLOCK = 128
EPS = 1e-6
NEG = -1e30

kernel_name = "hybrid_s5_diffattn_moba_adaln_block"


def rmsnorm(x, g):
    xf = x.astype(jnp.float32)
    y = xf * lax.rsqrt(jnp.mean(xf * xf, axis=-1, keepdims=True) + EPS)
    return (y * g.astype(jnp.float32)).astype(x.dtype)


def rope_tables(positions, head_dim):
    rot = head_dim // ROPE_FRACTION
    inv = ROPE_THETA ** (-jnp.arange(0, rot, 2, dtype=jnp.float32) / rot)
    ang = positions.astype(jnp.float32)[..., None] * inv
    return jnp.cos(ang), jnp.sin(ang)


def apply_partial_rope(x, cos, sin):
    half = cos.shape[-1]
    x1 = x[..., :half].astype(jnp.float32)
    x2 = x[..., half:2 * half].astype(jnp.float32)
    r1 = x1 * cos - x2 * sin
    r2 = x2 * cos + x1 * sin
    return jnp.concatenate([r1.astype(x.dtype), r2.astype(x.dtype), x[..., 2 * half:]], axis=-1)


def _complex_affine_combine(e1, e2):
    ar1, ai1, br1, bi1 = e1
    ar2, ai2, br2, bi2 = e2
    ar = ar2 * ar1 - ai2 * ai1
    ai = ar2 * ai1 + ai2 * ar1
    br = ar2 * br1 - ai2 * bi1 + br2
    bi = ar2 * bi1 + ai2 * br1 + bi2
    return (ar, ai, br, bi)


def s5_mixer(u, a_re, a_im, log_dt, b_re, b_im, c_re, c_im, d_skip, glu_w, glu_b, norm_g):
    bsz, s, _ = u.shape
    uf = u.astype(jnp.float32)
    ug = uf.reshape(bsz, s, SSM_GROUPS, SSM_GROUP)
    dt = jnp.exp(log_dt.astype(jnp.float32))[:, None]
    ar = a_re.astype(jnp.float32)
    ai = a_im.astype(jnp.float32)
    mag = jnp.exp(ar * dt)
    abar_re = mag * jnp.cos(ai * dt)
    abar_im = mag * jnp.sin(ai * dt)
    den = ar * ar + ai * ai
    nr = abar_re - 1.0
    ni = abar_im
    kr = ((nr * ar + ni * ai) / den)[..., None]
    ki = ((ni * ar - nr * ai) / den)[..., None]
    br = b_re.astype(jnp.float32)
    bi = b_im.astype(jnp.float32)
    bbar_re = kr * br - ki * bi
    bbar_im = kr * bi + ki * br
    bu_re = jnp.einsum('bsgc,gpc->bsgp', ug, bbar_re)
    bu_im = jnp.einsum('bsgc,gpc->bsgp', ug, bbar_im)
    shp = bu_re.shape
    elems = (jnp.broadcast_to(abar_re, shp), jnp.broadcast_to(abar_im, shp), bu_re, bu_im)
    _, _, h_re, h_im = lax.associative_scan(_complex_affine_combine, elems, axis=1)
    y = (jnp.einsum('bsgp,gcp->bsgc', h_re, c_re.astype(jnp.float32))
         - jnp.einsum('bsgp,gcp->bsgc', h_im, c_im.astype(jnp.float32)))
    y = y.reshape(bsz, s, SSM_WIDTH) + d_skip.astype(jnp.float32) * uf
    y = jax.nn.gelu(y)
    y = y * jax.nn.sigmoid(y @ glu_w.astype(jnp.float32) + glu_b.astype(jnp.float32))
    return rmsnorm(y, norm_g).astype(u.dtype)


def diff_attention(q, k, v, lam, lam_init, subln_g):
    bsz, s, nh, _, dh = q.shape
    dv = v.shape[-1]
    nq = s // ATTN_Q_BLOCK
    kh = jnp.transpose(k, (0, 2, 3, 1, 4))
    vh = jnp.transpose(v, (0, 2, 1, 3))
    qb = jnp.transpose(q, (0, 2, 3, 1, 4)).reshape(bsz, nh, 2, nq, ATTN_Q_BLOCK, dh)
    qb = jnp.moveaxis(qb, 3, 0)
    kpos = jnp.arange(s)
    scale = dh ** -0.5

    def block(args):
        qblk, i = args
        qpos = i * ATTN_Q_BLOCK + jnp.arange(ATTN_Q_BLOCK)
        sc = jnp.einsum('bhcqd,bhckd->bhcqk', qblk, kh).astype(jnp.float32) * scale
        sc = jnp.where(kpos[None, :] <= qpos[:, None], sc, NEG)
        p = jax.nn.softmax(sc, axis=-1)
        a = p[:, :, 0] - lam * p[:, :, 1]
        return jnp.einsum('bhqk,bhkd->bhqd', a, vh.astype(jnp.float32))

    o = lax.map(block, (qb, jnp.arange(nq)))
    o = jnp.moveaxis(o, 0, 2).reshape(bsz, nh, s, dv)
    o = jnp.transpose(o, (0, 2, 1, 3))
    o = rmsnorm(o, subln_g) * (1.0 - lam_init)
    return o.reshape(bsz, s, nh * dv).astype(v.dtype)


def moba_attention(q, k, v):
    bsz, s, nh, dh = q.shape
    nb = -(-s // MOBA_BLOCK)
    s_pad = nb * MOBA_BLOCK
    pad = ((0, 0), (0, s_pad - s), (0, 0), (0, 0))
    qh = jnp.transpose(jnp.pad(q, pad), (0, 2, 1, 3))
    kh = jnp.transpose(jnp.pad(k, pad), (0, 2, 1, 3))
    vh = jnp.transpose(jnp.pad(v, pad), (0, 2, 1, 3))
    kb = kh.reshape(bsz, nh, nb, MOBA_BLOCK, dh)
    vb = vh.reshape(bsz, nh, nb, MOBA_BLOCK, dh)
    kmean = jnp.mean(kb.astype(jnp.float32), axis=3)
    gate = jnp.einsum('bhsd,bhnd->bhsn', qh.astype(jnp.float32), kmean)
    qblock = jnp.arange(s_pad) // MOBA_BLOCK
    past = jnp.arange(nb)[None, :] < qblock[:, None]
    gate = jnp.where(past, gate, NEG)
    topk = min(MOBA_TOPK, nb)
    _, sel = lax.top_k(gate, topk)
    nc = s_pad // MOBA_Q_CHUNK
    q_chunks = jnp.moveaxis(qh.reshape(bsz, nh, nc, MOBA_Q_CHUNK, dh), 2, 0)
    sel_chunks = jnp.moveaxis(sel.reshape(bsz, nh, nc, MOBA_Q_CHUNK, topk), 2, 0)
    bi = jnp.arange(bsz)[:, None, None, None]
    hi = jnp.arange(nh)[None, :, None, None]
    scale = dh ** -0.5

    def chunk(args):
        qc, selc, ci = args
        qpos = ci * MOBA_Q_CHUNK + jnp.arange(MOBA_Q_CHUNK)
        own = (ci * MOBA_Q_CHUNK) // MOBA_BLOCK
        k_sel = kb[bi, hi, selc]
        v_sel = vb[bi, hi, selc]
        s_sel = jnp.einsum('bhqd,bhqnkd->bhqnk', qc, k_sel).astype(jnp.float32) * scale
        s_sel = jnp.where((selc < own)[..., None], s_sel, NEG)
        k_own = lax.dynamic_index_in_dim(kb, own, axis=2, keepdims=False)
        v_own = lax.dynamic_index_in_dim(vb, own, axis=2, keepdims=False)
        s_own = jnp.einsum('bhqd,bhkd->bhqk', qc, k_own).astype(jnp.float32) * scale
        kpos = own * MOBA_BLOCK + jnp.arange(MOBA_BLOCK)
        s_own = jnp.where(kpos[None, :] <= qpos[:, None], s_own, NEG)
        nsel = topk * MOBA_BLOCK
        s_all = jnp.concatenate([s_sel.reshape(bsz, nh, MOBA_Q_CHUNK, nsel), s_own], axis=-1)
        p = jax.nn.softmax(s_all, axis=-1)
        p_sel = p[..., :nsel].reshape(bsz, nh, MOBA_Q_CHUNK, topk, MOBA_BLOCK)
        p_own = p[..., nsel:]
        return (jnp.einsum('bhqnk,bhqnkd->bhqd', p_sel, v_sel.astype(jnp.float32))
                + jnp.einsum('bhqk,bhkd->bhqd', p_own, v_own.astype(jnp.float32)))

    o = lax.map(chunk, (q_chunks, sel_chunks, jnp.arange(nc)))
    o = jnp.moveaxis(o, 0, 2).reshape(bsz, nh, s_pad, dh)[:, :, :s]
    return jnp.transpose(o, (0, 2, 1, 3)).astype(v.dtype)


def setup_inputs(seed: int = 0) -> dict:
    key = jax.random.key(seed)
    ks = jax.random.split(key, 32)
    f32 = jnp.float32
    L, D, G, P = DEPTH, D_MODEL, SSM_GROUPS, SSM_STATE

    def nrm(k, shape, scale):
        return jax.random.normal(k, shape, f32) * scale

    def gain(k, shape):
        return 1.0 + 0.05 * jax.random.normal(k, shape, f32)

    x = jax.random.normal(ks[0], (BATCH, SEQ, D), f32)
    c = jax.random.normal(ks[1], (BATCH, D), f32)
    offs = jax.random.randint(ks[2], (BATCH, 1), 0, 4096, dtype=jnp.int32)
    positions = jnp.arange(SEQ, dtype=jnp.int32)[None, :] + offs
    a_re = -0.5 * jnp.exp(0.05 * jax.random.normal(ks[3], (L, G, P), f32))
    a_im = (math.pi * jnp.arange(P, dtype=f32))[None, None, :] + 0.01 * jax.random.normal(ks[4], (L, G, P), f32)
    log_dt = jax.random.uniform(ks[5], (L, G), f32, math.log(0.001), math.log(0.1))
    return {
        "x": x,
        "c": c,
        "positions": positions,
        "norm1_g": gain(ks[6], (L, D)),
        "norm2_g": gain(ks[7], (L, D)),
        "w_ada": nrm(ks[8], (L, D, 6 * D), 0.5 * D ** -0.5),
        "b_ada": nrm(ks[9], (L, 6 * D), 0.02),
        "w_in": nrm(ks[10], (L, D, IN_WIDTH), D ** -0.5),
        "w_out": nrm(ks[11], (L, D_MIX, D), D_MIX ** -0.5),
        "ssm_a_re": a_re,
        "ssm_a_im": a_im,
        "ssm_log_dt": log_dt,
        "ssm_b_re": nrm(ks[12], (L, G, P, SSM_GROUP), (2.0 * SSM_GROUP) ** -0.5),
        "ssm_b_im": nrm(ks[13], (L, G, P, SSM_GROUP), (2.0 * SSM_GROUP) ** -0.5),
        "ssm_c_re": nrm(ks[14], (L, G, SSM_GROUP, P), (2.0 * P) ** -0.5),
        "ssm_c_im": nrm(ks[15], (L, G, SSM_GROUP, P), (2.0 * P) ** -0.5),
        "ssm_d": nrm(ks[16], (L, SSM_WIDTH), 1.0),
        "ssm_glu_w": nrm(ks[17], (L, SSM_WIDTH, SSM_WIDTH), SSM_WIDTH ** -0.5),
        "ssm_glu_b": nrm(ks[18], (L, SSM_WIDTH), 0.02),
        "ssm_norm_g": gain(ks[19], (L, SSM_WIDTH)),
        "diff_lq1": nrm(ks[20], (L, DIFF_HEAD_DIM), 0.1),
        "diff_lk1": nrm(ks[21], (L, DIFF_HEAD_DIM), 0.1),
        "diff_lq2": nrm(ks[22], (L, DIFF_HEAD_DIM), 0.1),
        "diff_lk2": nrm(ks[23], (L, DIFF_HEAD_DIM), 0.1),
        "diff_subln_g": gain(ks[24], (L, DIFF_V_DIM)),
        "moba_norm_g": gain(ks[25], (L, MOBA_HEAD_DIM)),
        "mlp_w1": nrm(ks[26], (L, D, D_FF), D ** -0.5),
        "mlp_w2": nrm(ks[27], (L, D_FF, D), D_FF ** -0.5),
        "final_g": gain(ks[28], (D,)),
    }


def reference(x, c, positions, norm1_g, norm2_g, w_ada, b_ada, w_in, w_out,
              ssm_a_re, ssm_a_im, ssm_log_dt, ssm_b_re, ssm_b_im, ssm_c_re, ssm_c_im,
              ssm_d, ssm_glu_w, ssm_glu_b, ssm_norm_g,
              diff_lq1, diff_lk1, diff_lq2, diff_lk2, diff_subln_g, moba_norm_g,
              mlp_w1, mlp_w2, final_g):
    bsz, s, _ = x.shape
    silu_c = jax.nn.silu(c)
    cos_d, sin_d = rope_tables(positions, DIFF_HEAD_DIM)
    cos_m, sin_m = rope_tables(positions, MOBA_HEAD_DIM)
    splits = [SSM_WIDTH,
              SSM_WIDTH + DIFF_WIDTH,
              SSM_WIDTH + 2 * DIFF_WIDTH,
              SSM_WIDTH + 3 * DIFF_WIDTH,
              SSM_WIDTH + 3 * DIFF_WIDTH + MOBA_WIDTH,
              SSM_WIDTH + 3 * DIFF_WIDTH + 2 * MOBA_WIDTH]
    for l in range(DEPTH):
        mod = silu_c @ w_ada[l] + b_ada[l]
        sh1, sc1, g1, sh2, sc2, g2 = jnp.split(mod[:, None, :], 6, axis=-1)

        h = rmsnorm(x, norm1_g[l]) * (1.0 + sc1) + sh1
        proj = h @ w_in[l]
        u, dq, dk, dv, mq, mk, mv = jnp.split(proj, splits, axis=-1)

        y_ssm = s5_mixer(u, ssm_a_re[l], ssm_a_im[l], ssm_log_dt[l], ssm_b_re[l], ssm_b_im[l],
                         ssm_c_re[l], ssm_c_im[l], ssm_d[l], ssm_glu_w[l], ssm_glu_b[l], ssm_norm_g[l])

        lam_init = 0.8 - 0.6 * math.exp(-0.3 * l)
        lam = (jnp.exp(jnp.sum(diff_lq1[l].astype(jnp.float32) * diff_lk1[l].astype(jnp.float32)))
               - jnp.exp(jnp.sum(diff_lq2[l].astype(jnp.float32) * diff_lk2[l].astype(jnp.float32)))
               + lam_init)
        rd = (cos_d[:, :, None, None, :], sin_d[:, :, None, None, :])
        dq = apply_partial_rope(dq.reshape(bsz, s, DIFF_HEADS, 2, DIFF_HEAD_DIM), *rd)
        dk = apply_partial_rope(dk.reshape(bsz, s, DIFF_HEADS, 2, DIFF_HEAD_DIM), *rd)
        dv = dv.reshape(bsz, s, DIFF_HEADS, DIFF_V_DIM)
        y_diff = diff_attention(dq, dk, dv, lam, lam_init, diff_subln_g[l])

        rm = (cos_m[:, :, None, :], sin_m[:, :, None, :])
        mq = apply_partial_rope(mq.reshape(bsz, s, MOBA_HEADS, MOBA_HEAD_DIM), *rm)
        mk = apply_partial_rope(mk.reshape(bsz, s, MOBA_HEADS, MOBA_HEAD_DIM), *rm)
        mv = mv.reshape(bsz, s, MOBA_HEADS, MOBA_HEAD_DIM)
        y_moba = rmsnorm(moba_attention(mq, mk, mv), moba_norm_g[l]).reshape(bsz, s, MOBA_WIDTH)

        mix = jnp.concatenate([y_ssm, y_diff, y_moba], axis=-1)
        x = x + g1 * (mix @ w_out[l])

        h = rmsnorm(x, norm2_g[l]) * (1.0 + sc2) + sh2
        x = x + g2 * (jnp.square(jax.nn.relu(h @ mlp_w1[l])) @ mlp_w2[l])
    return rmsnorm(x, final_g)
```

```python
import math
from contextlib import ExitStack

import numpy as np
import concourse.bass as bass
import concourse.mybir as mybir
from concourse.bass_utils import run_bass_kernel_spmd

F32 = mybir.dt.float32
BF16 = mybir.dt.bfloat16
I32 = mybir.dt.int32
ALU = mybir.AluOpType
AF = mybir.ActivationFunctionType
AX = mybir.AxisListType

D = 1024
SEQ = 8192
DEPTH = 2
DFF = 4096
INW = 2560
TB = 512
EPS = 1e-6
ROPE_THETA = 500000.0

ENGS = ("pe", "act", "dve", "pool", "sp")
PSUM_PREFIX = ("pa", "pb", "pn", "pg", "pst", "modps", "pbu", "py", "psS", "acc", "pmv", "pnm", "pc", "pd", "pt")


class Prog:
    NDMA = 6

    def __init__(self, nc, sems):
        self.nc = nc
        self.sems = sems
        self.cnt = {e: 0 for e in ENGS}
        self.dcnt = {}
        self.drot = {e: 0 for e in ENGS}
        self.reset()

    def reset(self):
        self.ops = {e: [] for e in ENGS}
        self.last_w = {}
        self.readers = {}

    def add(self, eng, fn, r=(), w=(), dma=False):
        deps = []
        for x in r:
            if x in self.last_w:
                deps.append(self.last_w[x])
            if x.startswith(PSUM_PREFIX):
                deps.extend(o for o in self.readers.get(x, ()) if o["eng"] != eng)
        for x in w:
            if x in self.last_w:
                deps.append(self.last_w[x])
            deps.extend(self.readers.get(x, ()))
        op = {"fn": fn, "deps": [], "dma": dma, "sig": False, "eng": eng, "val": None, "sem": None}
        for d in deps:
            if d is op:
                continue
            if d["eng"] == eng and not d["dma"] and not dma and eng == "pe":
                continue
            if not any(d is x for x in op["deps"]):
                op["deps"].append(d)
                d["sig"] = True
        self.ops[eng].append(op)
        for x in r:
            self.readers.setdefault(x, []).append(op)
        for x in w:
            self.last_w[x] = op
            self.readers[x] = []
        return op

    def emit(self, block):
        nc = self.nc
        for e in ENGS:
            for op in self.ops[e]:
                if op["dma"]:
                    k = ("dma", e, self.drot[e] % self.NDMA)
                    self.drot[e] += 1
                    op["sem"] = k
                    op["prev"] = self.dcnt.get(k, 0)
                    self.dcnt[k] = op["prev"] + 16
                    op["val"] = self.dcnt[k]
                elif op["sig"]:
                    self.cnt[e] += 1
                    op["sem"] = e
                    op["val"] = self.cnt[e]
        final_dma = dict(self.dcnt)
        sems = self.sems
        ops = self.ops

        def run(e, eng):
            waited = {}
            for op in ops[e]:
                need = {}
                for d in op["deps"]:
                    k = d["sem"]
                    if d["val"] > need.get(k, 0):
                        need[k] = d["val"]
                if op["dma"] and op["prev"] > 0:
                    k = op["sem"]
                    need[k] = max(need.get(k, 0), op["prev"])
                for k, v in need.items():
                    if waited.get(k, 0) < v:
                        eng.wait_ge(sems[k], v)
                        waited[k] = v
                inst = op["fn"](eng)
                if op["dma"]:
                    inst.then_inc(sems[op["sem"]], 16)
                elif op["sig"]:
                    inst.then_inc(sems[op["sem"]], 1)
            if e == "sp":
                for k, v in final_dma.items():
                    if v > 0 and waited.get(k, 0) < v:
                        eng.wait_ge(sems[k], v)

        for e, starter in (("pe", block.tensor), ("act", block.scalar), ("dve", block.vector), ("pool", block.gpsimd), ("sp", block.sync)):
            if ops[e] or e == "sp":
                starter(lambda eng, e=e: run(e, eng))

        self.reset()

    def dma(self, out, in_, r=(), w=(), q="sp", **kw):
        return self.add(q, lambda eng: eng.dma_start(out=out, in_=in_, **kw), r=r, w=w, dma=True)

    def mm(self, out, lhsT, rhs, start=True, stop=True, r=(), w=(), **kw):
        return self.add("pe", lambda eng: eng.matmul(out, lhsT, rhs, start=start, stop=stop, **kw), r=r, w=w)

    def tr(self, out, in_, ident, r=(), w=()):
        return self.add("pe", lambda eng: eng.transpose(out, in_, ident), r=r, w=w)

    def act(self, out, in_, func, r=(), w=(), **kw):
        return self.add("act", lambda eng: eng.activation(out=out, in_=in_, func=func, **kw), r=r, w=w)

    def tt(self, out, in0, in1, op, r=(), w=(), eng="dve"):
        return self.add(eng, lambda e: e.tensor_tensor(out=out, in0=in0, in1=in1, op=op), r=r, w=w)

    def ts(self, out, in0, s1, s2, op0, op1=None, r=(), w=(), eng="dve", **kw):
        if op1 is None:
            return self.add(eng, lambda e: e.tensor_scalar(out=out, in0=in0, scalar1=s1, scalar2=None, op0=op0, **kw), r=r, w=w)
        return self.add(eng, lambda e: e.tensor_scalar(out=out, in0=in0, scalar1=s1, scalar2=s2, op0=op0, op1=op1, **kw), r=r, w=w)

    def stt(self, out, in0, scalar, in1, op0, op1, r=(), w=()):
        return self.add("dve", lambda e: e.scalar_tensor_tensor(out=out, in0=in0, scalar=scalar, in1=in1, op0=op0, op1=op1), r=r, w=w)

    def copy(self, out, in_, r=(), w=(), eng="dve"):
        if eng == "act":
            return self.add("act", lambda e: e.copy(out=out, in_=in_), r=r, w=w)
        return self.add(eng, lambda e: e.tensor_copy(out=out, in_=in_), r=r, w=w)

    def memset(self, ap, val, r=(), w=(), eng="dve"):
        return self.add(eng, lambda e: e.memset(ap, val), r=r, w=w)


PI = math.pi
SKIP = set()
TWO_PI = 2.0 * math.pi
CW1 = 6.28125
CW2 = TWO_PI - 6.28125
PI_LO = 3.141592


def sincos(P, ang, n, out_sin, out_cos, t1, t2, t3, ti, tag, rsin, rcos, np_=128):
    a = lambda nm: tag + nm
    sl = lambda t: t[0:np_, 0:n]
    P.ts(sl(t1), ang, 1.0 / TWO_PI, None, ALU.mult, r=[a("ang")], w=[a("t1")])
    P.copy(sl(ti), sl(t1), r=[a("t1")], w=[a("ti")])
    P.copy(sl(t1), sl(ti), r=[a("ti")], w=[a("t1")])
    P.stt(sl(t2), sl(t1), -CW1, ang, ALU.mult, ALU.add, r=[a("t1"), a("ang")], w=[a("t2")])
    P.stt(sl(t3), sl(t1), -CW2, sl(t2), ALU.mult, ALU.add, r=[a("t1"), a("t2")], w=[a("t3")])
    P.ts(sl(t1), sl(t3), PI, -TWO_PI, ALU.is_gt, ALU.mult, r=[a("t3")], w=[a("t1")])
    P.tt(sl(t2), sl(t3), sl(t1), ALU.add, r=[a("t3"), a("t1")], w=[a("t2")])
    P.ts(sl(t1), sl(t2), -PI, TWO_PI, ALU.is_lt, ALU.mult, r=[a("t2")], w=[a("t1")])
    P.tt(sl(t3), sl(t2), sl(t1), ALU.add, r=[a("t2"), a("t1")], w=[a("t3")])
    P.ts(sl(t1), sl(t3), PI_LO, -PI_LO, ALU.min, ALU.max, r=[a("t3")], w=[a("t1")])
    P.act(out_sin, sl(t1), AF.Sin, r=[a("t1")], w=[rsin])
    P.ts(sl(t2), sl(t3), PI / 2, None, ALU.add, r=[a("t3")], w=[a("t2")])
    P.ts(sl(t1), sl(t2), PI, -TWO_PI, ALU.is_gt, ALU.mult, r=[a("t2"), a("t1")], w=[a("t1")])
    P.tt(sl(t3), sl(t2), sl(t1), ALU.add, r=[a("t2"), a("t1")], w=[a("t3")])
    P.ts(sl(t2), sl(t3), PI_LO, -PI_LO, ALU.min, ALU.max, r=[a("t3")], w=[a("t2")])
    P.act(out_cos, sl(t2), AF.Sin, r=[a("t2")], w=[rcos])


def build(S, dbg=False, nl=DEPTH, stop=None):
    NB = S // TB
    NKT = S // 128
    nc = bass.Bass("TRN2", target_bir_lowering=False)
    _uid = [0]

    def U(n):
        _uid[0] += 1
        return f"{n}_u{_uid[0]}"

    din = lambda n, s, d=F32: nc.dram_tensor(n, list(s), d, kind="ExternalInput").ap()
    dscr = lambda n, s, d=F32: nc.dram_tensor(n, list(s), d, kind=("ExternalOutput" if dbg else "Internal")).ap()
    x_d = din("x", [S, D])
    out_d = nc.dram_tensor("out", [S, D], F32, kind="ExternalOutput").ap()
    pos_d = din("pos", [1, S], I32)
    cT_d = din("cT", [128, 8])
    cst_d = din("cst", [128, 2])
    wada_d = din("w_ada", [DEPTH, D, 6 * D])
    bada_d = din("b_adaT", [DEPTH, 128, 48])
    n1g_d = din("n1gT", [DEPTH, 128, 8])
    n2g_d = din("n2gT", [DEPTH, 128, 8])
    fg_d = din("fgT", [128, 8])
    win_d = din("w_in", [DEPTH, D, INW])
    wout_d = din("w_out", [DEPTH, D, D])
    w1_d = din("mlp_w1", [DEPTH, D, DFF])
    w2_d = din("mlp_w2", [DEPTH, DFF, D])
    sare_d = din("s_are", [DEPTH, 128, 8])
    saim_d = din("s_aim", [DEPTH, 128, 8])
    sldt_d = din("s_ldt", [DEPTH, 128, 8])
    sbre_d = din("s_bre", [DEPTH, 128, 8, 16])
    sbim_d = din("s_bim", [DEPTH, 128, 8, 16])
    scre_d = din("s_cre", [DEPTH, 128, 8, 16])
    scim_d = din("s_cim", [DEPTH, 128, 8, 16])
    sd_d = din("s_d", [DEPTH, 128, 2])
    sgw_d = din("s_gw", [DEPTH, 256, 256])
    sgb_d = din("s_gb", [DEPTH, 128, 2])
    sng_d = din("s_ng", [DEPTH, 128, 2])
    lam_d = din("lamv", [DEPTH, 256])
    dsg_d = din("dsg", [DEPTH, 128, 1])
    mng_d = din("mng", [DEPTH, 64, 1])
    xT_d = dscr("xT", [D, S])
    cosT_d = dscr("cosT", [128, S])
    sinT_d = dscr("sinT", [128, S])
    uT_d = dscr("uT", [256, S], BF16)
    dqT_d = dscr("dqT", [4, 128, S], BF16)
    dkT_d = dscr("dkT", [4, 128, S], BF16)
    dV_d = dscr("dV", [S, 512], BF16)
    mqT_d = dscr("mqT", [4, 96, S], BF16)
    mkT_d = dscr("mkT", [4, 64, S], BF16)
    mV_d = dscr("mV", [S, 256], BF16)
    mixT_d = dscr("mixT", [D, S], BF16)
    dbgk_d = dscr("dbgk", [128, 32]); dbgg_d = dscr("dbgg", [128, 32]); dbgm_d = dscr("dbgm", [128, 8]); dbgs_d = dscr("dbgs", [128, 32])

    with ExitStack() as top:
        sems = {}
        for e in ENGS:
            sems[e] = top.enter_context(nc.semaphore("s_" + e))
            for i in range(Prog.NDMA):
                sems[("dma", e, i)] = top.enter_context(nc.semaphore(f"d_{e}_{i}"))
        P = Prog(nc, sems)
        gsb = lambda n, s, d=F32: top.enter_context(nc.sbuf_tensor(U(n), list(s), d))
        ident = gsb("ident", [128, 128])
        identb = gsb("identb", [128, 128], BF16)
        onesb = gsb("onesb", [128, 128], BF16)
        tri = gsb("tri", [128, 128], BF16)
        pswap = gsb("pswap", [128, 128], BF16)
        epsc = gsb("epsc", [128, 1])
        cst = gsb("cstc", [128, 2])
        modv = gsb("modv", [128, DEPTH, 48])
        A1 = gsb("A1", [128, DEPTH, 8])
        A2 = gsb("A2", [128, DEPTH, 8])
        fgT = gsb("fgTs", [128, 8])

        with ExitStack() as es:
            sb = lambda n, s, d=F32: es.enter_context(nc.sbuf_tensor(U(n), list(s), d))
            onesf = sb("onesf", [128, 128])
            b1 = sb("b1", [128, 128])
            b2 = sb("b2", [128, 128])
            cT = sb("cTs", [128, 8])
            scT = sb("scT", [128, 8])
            tmp8 = sb("tmp8", [128, 8])
            wa = [sb(f"wa{i}", [128, 8, 512]) for i in range(2)]
            bada = sb("bada", [128, DEPTH, 48])
            ng1 = sb("ng1", [128, DEPTH, 8])
            ng2 = sb("ng2", [128, DEPTH, 8])
            posi = [sb(f"posi{i}", [128, TB], I32) for i in range(2)]
            ang = sb("ang", [128, TB])
            t1 = sb("t1", [128, TB]); t2 = sb("t2", [128, TB]); t3 = sb("t3", [128, TB])
            ti = sb("ti", [128, TB], I32)
            sn = [sb(f"sn{i}", [128, TB]) for i in range(2)]
            cs = [sb(f"cs{i}", [128, TB]) for i in range(2)]
            modps = es.enter_context(nc.psum_tensor(U("modps"), [128, DEPTH * 48], F32))
            with nc.Block() as block:
                P.memset(onesf[:], 1.0, w=["onesf"], eng="pool")
                P.memset(onesb[:], 1.0, w=["onesb"], eng="pool")
                P.memset(epsc[:], EPS, w=["epsc"], eng="pool")
                P.memset(pswap[:], 0.0, w=["pswap"], eng="pool")
                P.add("pool", lambda e: e.affine_select(out=ident[:], in_=onesf[:], pattern=[[-1, 128]], compare_op=ALU.is_equal, fill=0.0, base=0, channel_multiplier=1), r=["onesf"], w=["ident"])
                P.copy(identb[:], ident[:], r=["ident"], w=["identb"], eng="pool")
                P.add("pool", lambda e: e.affine_select(out=tri[:], in_=onesb[:], pattern=[[1, 128]], compare_op=ALU.is_ge, fill=0.0, base=0, channel_multiplier=-1), r=["onesb"], w=["tri"])
                P.add("pool", lambda e: e.affine_select(out=b1[:], in_=onesf[:], pattern=[[-1, 128]], compare_op=ALU.is_equal, fill=0.0, base=-8, channel_multiplier=1), r=["onesf"], w=["b1"])
                P.add("pool", lambda e: e.affine_select(out=b2[:], in_=onesf[:], pattern=[[-1, 128]], compare_op=ALU.is_equal, fill=0.0, base=8, channel_multiplier=1), r=["onesf"], w=["b2"])
                for s0 in (0, 64):
                    P.copy(pswap[:, s0:s0 + 8], b1[:, s0:s0 + 8], r=["b1", "pswap"], w=["pswap"], eng="pool")
                    P.copy(pswap[:, s0 + 8:s0 + 16], b2[:, s0 + 8:s0 + 16], r=["b2", "pswap"], w=["pswap"], eng="pool")
                P.dma(cT[:], cT_d, w=["cT"])
                P.dma(cst[:], cst_d, w=["cst"])
                P.dma(bada[:], bada_d.rearrange("l p j -> p l j"), w=["bada"])
                P.dma(ng1[:], n1g_d.rearrange("l p j -> p l j"), w=["ng1"])
                P.dma(ng2[:], n2g_d.rearrange("l p j -> p l j"), w=["ng2"])
                P.dma(fgT[:], fg_d, w=["fgT"])
                P.act(tmp8[:], cT[:], AF.Exp, r=["cT"], w=["tmp8"], scale=-1.0)
                P.ts(tmp8[:], tmp8[:], 1.0, None, ALU.add, r=["tmp8"], w=["tmp8"])
                P.add("dve", lambda e: e.reciprocal(out=scT[:], in_=tmp8[:]), r=["tmp8"], w=["scT"])
                P.tt(scT[:], scT[:], cT[:], ALU.mult, r=["scT", "cT"], w=["scT"])
                k = 0
                for l in range(DEPTH):
                    wv = wada_d[l].rearrange("(kt p) n -> p kt n", p=128)
                    for cb in range(12):
                        wb_ = wa[k % 2]; wn = f"wa{k % 2}"; k += 1
                        P.dma(wb_[:], wv[:, :, cb * 512:(cb + 1) * 512], w=[wn])
                        for j in range(4):
                            col = l * 48 + cb * 4 + j
                            for kt in range(8):
                                P.mm(modps[:, col:col + 1], wb_[:, kt, j * 128:(j + 1) * 128], scT[:, kt:kt + 1],
                                     start=(kt == 0), stop=(kt == 7), r=[wn, "scT"], w=["modps"])
                P.tt(modv[:].rearrange("p l j -> p (l j)"), modps[:], bada[:].rearrange("p l j -> p (l j)"), ALU.add, r=["modps", "bada"], w=["modv"])
                for l in range(DEPTH):
                    P.stt(A1[:, l, :], modv[:, l, 8:16], 1.0, ng1[:, l, :], ALU.add, ALU.mult, r=["modv", "ng1"], w=["A1"])
                    P.stt(A2[:, l, :], modv[:, l, 32:40], 1.0, ng2[:, l, :], ALU.add, ALU.mult, r=["modv", "ng2"], w=["A2"])
                for i in range(NB):
                    pi_ = posi[i % 2]; pn_ = f"posi{i % 2}"
                    P.dma(pi_[:], pos_d[0:1, i * TB:(i + 1) * TB].to_broadcast([128, TB]), w=[pn_])
                    P.copy(t1[:], pi_[:], r=[pn_], w=["rt1"])
                    P.ts(ang[:], t1[:], cst[:, 0:1], None, ALU.mult, r=["rt1", "cst"], w=["rang"])
                    sincos(P, ang[:], TB, sn[i % 2][:], cs[i % 2][:], t1, t2, t3, ti, "r", f"sn{i % 2}", f"cs{i % 2}")
                    P.ts(sn[i % 2][:], sn[i % 2][:], cst[:, 1:2], None, ALU.mult, r=[f"sn{i % 2}", "cst"], w=[f"sn{i % 2}"])
                    P.dma(sinT_d[:, i * TB:(i + 1) * TB], sn[i % 2][:], r=[f"sn{i % 2}"], w=["sinT"])
                    P.dma(cosT_d[:, i * TB:(i + 1) * TB], cs[i % 2][:], r=[f"cs{i % 2}"], w=["cosT"])
                P.emit(block)

        if stop == "0":
            return nc
        with ExitStack() as es:
            sb = lambda n, s, d=F32: es.enter_context(nc.sbuf_tensor(U(n), list(s), d))
            xin = [sb(f"xin{i}", [128, D]) for i in range(3)]
            xo = [sb(f"xo{i}", [128, 8, TB]) for i in range(2)]
            pst = [es.enter_context(nc.psum_tensor(U(f"pst{i}"), [128, TB], F32)) for i in range(8)]
            with nc.Block() as block:
                k = 0
                for i in range(NB):
                    o_ = xo[i % 2]; on = f"xo{i % 2}"
                    for tt in range(4):
                        xi = xin[k % 3]; xn = f"xin{k % 3}"; k += 1
                        P.dma(xi[:], x_d[i * TB + tt * 128:i * TB + (tt + 1) * 128, :], w=[xn])
                        for ft in range(8):
                            P.tr(pst[ft][:, tt * 128:(tt + 1) * 128], xi[:, ft * 128:(ft + 1) * 128], ident[:], r=[xn], w=[f"pst{ft}"])
                    for ft in range(8):
                        P.copy(o_[:, ft, :], pst[ft][:], r=[f"pst{ft}"], w=[f"{on}_{ft}"], eng=("act" if ft % 2 else "dve"))
                    P.dma(xT_d.rearrange("(ft p) t -> p ft t", p=128)[:, :, i * TB:(i + 1) * TB], o_[:], r=[f"{on}_{ft}" for ft in range(8)], w=["xTd"])
                P.emit(block)

        if stop == "T0":
            return nc
        xTv = xT_d.rearrange("(ft p) t -> p ft t", p=128)

        def load_cast(P, dst3, src2, nk, ncol, name, q="pool"):
            sv = src2.rearrange("(kt p) n -> p kt n", p=128)
            step = 2048
            for kt in range(nk):
                for c0 in range(0, ncol, step):
                    c1 = min(ncol, c0 + step)
                    P.dma(dst3[:, kt, c0:c1], sv[:, kt, c0:c1], w=[name], q=q)

        def norm_mod(P, xb, xn, hT, hn, A, B, l, pn, tmps, sqs, rstd, lnv, pool_share=True):
            for kt in range(8):
                sq = sqs[kt % 2]; sqn = f"sq{kt % 2}"
                P.act(sq[:], xb[:, kt, :], AF.Square, r=[xn], w=[sqn])
                P.mm(pn[:], onesb[:], sq[:], start=(kt == 0), stop=(kt == 7), r=[sqn, "onesb"], w=["pn"])
            P.act(lnv[:], pn[:], AF.Ln, r=["pn", "epsc"], w=["lnv"], scale=1.0 / D, bias=epsc[:, 0:1])
            P.act(rstd[:], lnv[:], AF.Exp, r=["lnv"], w=["rstd"], scale=-0.5)
            for kt in range(8):
                tm = tmps[kt % 2]; tn = f"nt{kt % 2}"
                P.tt(tm[:], xb[:, kt, :], rstd[:], ALU.mult, r=[xn, "rstd"], w=[tn], eng=("pool" if (pool_share and kt % 2) else "dve"))
                P.act(hT[:, kt, :], tm[:], AF.Identity, r=[tn, "A", "modv"], w=[f"{hn}_{kt}"], scale=A[:, l, kt:kt + 1], bias=B(kt))

        for l in range(nl):
            with ExitStack() as es:
                sb = lambda n, s, d=F32: es.enter_context(nc.sbuf_tensor(U(n), list(s), d))
                win = sb("win", [128, 8, INW], BF16)
                xbs = [sb(f"xb{i}", [128, 8, TB]) for i in range(2)]
                hTs = [sb(f"hT{i}", [128, 8, TB], BF16) for i in range(2)]
                sqs = [sb(f"sq{i}", [128, TB], BF16) for i in range(2)]
                tmps = [sb(f"nt{i}", [128, TB]) for i in range(2)]
                rstd = sb("rstd", [128, TB]); lnv = sb("lnv", [128, TB])
                cosb = [sb(f"cosb{i}", [128, TB]) for i in range(2)]
                sinb = [sb(f"sinb{i}", [128, TB]) for i in range(2)]
                qb = [sb(f"qb{i}", [128, TB], BF16) for i in range(2)]
                r1 = [sb(f"r1_{i}", [128, TB]) for i in range(2)]
                r2 = [sb(f"r2_{i}", [128, TB]) for i in range(2)]
                rf = [sb(f"rf{i}", [128, TB]) for i in range(2)]
                stg = [sb(f"stg{i}", [128, TB], BF16) for i in range(4)]
                vst = [sb(f"vst{i}", [128, 768], BF16) for i in range(2)]
                kmT = [sb(f"kmT{j}", [128, 32]) for j in range(2)]
                gm = sb("gm", [128, 32]); m8 = sb("m8", [128, 8]); selv = sb("selv", [128, 32])
                negm = sb("negm", [128, 32], BF16)
                nst = [sb(f"nst{i}", [32, TB], BF16) for i in range(2)]
                pa = [es.enter_context(nc.psum_tensor(U(f"pa{i}"), [128, TB], F32)) for i in range(4)]
                pb = [es.enter_context(nc.psum_tensor(U(f"pb{i}"), [128, TB], F32)) for i in range(2)]
                pn = es.enter_context(nc.psum_tensor(U("pn"), [128, TB], F32))
                pg = es.enter_context(nc.psum_tensor(U("pg"), [128, TB], F32))
                pgb = pg[:].bitcast(BF16)
                with nc.Block() as block:
                    load_cast(P, win, win_d[l], 8, INW, "win")
                    for j in range(2):
                        P.memset(kmT[j][:], 0.0, w=[f"kmT{j}"], eng="pool")
                    cnt = {"pa": 0, "pb": 0, "stg": 0, "qb": 0, "r": 0, "v": 0, "ns": 0}

                    def load_blk(i):
                        P.dma(xbs[i % 2][:], xTv[:, :, i * TB:(i + 1) * TB], w=[f"xb{i % 2}"])
                        P.dma(cosb[i % 2][:], cosT_d[:, i * TB:(i + 1) * TB], w=[f"cosb{i % 2}"])
                        P.dma(sinb[i % 2][:], sinT_d[:, i * TB:(i + 1) * TB], w=[f"sinb{i % 2}"])

                    load_blk(0)
                    for i in range(NB):
                        t0 = i * TB
                        if i + 1 < NB:
                            load_blk(i + 1)
                        xb = xbs[i % 2]; xn = f"xb{i % 2}"; hT = hTs[i % 2]; hn = f"hT{i % 2}"
                        cb_, cbn = cosb[i % 2], f"cosb{i % 2}"
                        sb_, sbn = sinb[i % 2], f"sinb{i % 2}"
                        norm_mod(P, xb, xn, hT, hn, A1, lambda kt: modv[:, l, kt:kt + 1], l, pn, tmps, sqs, rstd, lnv)
                        hres = [f"{hn}_{kt}" for kt in range(8)]

                        def proj(c0):
                            p_ = pa[cnt["pa"] % 4]; pn_ = f"pa{cnt['pa'] % 4}"; cnt["pa"] += 1
                            for kt in range(8):
                                P.mm(p_[:], win[:, kt, c0:c0 + 128], hT[:, kt, :], start=(kt == 0), stop=(kt == 7), r=["win", hres[kt]], w=[pn_])
                            return p_, pn_

                        RENG = "dve" if "nopool" in SKIP else "pool"

                        def rope(p_, pn_, want_f32):
                            q_ = qb[cnt["qb"] % 2]; qn = f"qb{cnt['qb'] % 2}"; cnt["qb"] += 1
                            P.copy(q_[:], p_[:], r=[pn_], w=[qn], eng="act")
                            s_ = pb[cnt["pb"] % 2]; sn_ = f"pb{cnt['pb'] % 2}"; cnt["pb"] += 1
                            if "nosw" not in SKIP:
                                P.mm(s_[:], pswap[:], q_[:], r=[qn, "pswap"], w=[sn_])
                            else:
                                s_, sn_ = p_, pn_
                            k_ = cnt["r"] % 2; cnt["r"] += 1
                            P.tt(r1[k_][:], p_[:], cb_[:], ALU.mult, r=[pn_, cbn, qn], w=[f"r1_{k_}"])
                            P.tt(r2[k_][:], s_[:], sb_[:], ALU.mult, r=[sn_, sbn], w=[f"r2_{k_}"])
                            g_ = stg[cnt["stg"] % 4]; gn = f"stg{cnt['stg'] % 4}"; cnt["stg"] += 1
                            if want_f32:
                                P.tt(rf[k_][:], r1[k_][:], r2[k_][:], ALU.add, r=[f"r1_{k_}", f"r2_{k_}"], w=[f"rf{k_}"], eng=RENG)
                                P.copy(g_[:], rf[k_][:], r=[f"rf{k_}"], w=[gn], eng="act")
                                return rf[k_], f"rf{k_}", g_, gn
                            P.tt(g_[:], r1[k_][:], r2[k_][:], ALU.add, r=[f"r1_{k_}", f"r2_{k_}"], w=[gn], eng=RENG)
                            return None, None, g_, gn

                        for j in range(2):
                            p_, pn_ = proj(j * 128)
                            g_ = stg[cnt["stg"] % 4]; gn = f"stg{cnt['stg'] % 4}"; cnt["stg"] += 1
                            P.copy(g_[:], p_[:], r=[pn_], w=[gn], eng="act")
                            P.dma(uT_d[j * 128:(j + 1) * 128, t0:t0 + TB], g_[:], r=[gn], w=["uTd"])
                        for h in range(0 if "rope" in SKIP else 4):
                            p_, pn_ = proj(256 + h * 128)
                            _, _, g_, gn = rope(p_, pn_, False)
                            if "nodma" not in SKIP:
                                P.dma(dqT_d[h, :, t0:t0 + TB], g_[:], r=[gn], w=["dqTd"])
                        for h in range(0 if "rope" in SKIP else 4):
                            p_, pn_ = proj(768 + h * 128)
                            _, _, g_, gn = rope(p_, pn_, False)
                            if "nodma" not in SKIP:
                                P.dma(dkT_d[h, :, t0:t0 + TB], g_[:], r=[gn], w=["dkTd"])
                        for j in range(0 if "mk" in SKIP else 2):
                            p_, pn_ = proj(2048 + j * 128)
                            f_, fn_, g_, gn = rope(p_, pn_, True)
                            P.add("dve", lambda e, f_=f_, j=j, i=i: e.tensor_reduce(out=kmT[j][:, 2 * i:2 * i + 2], in_=f_[:].rearrange("p (b k) -> p b k", k=256), axis=AX.X, op=ALU.add),
                                  r=[fn_], w=[f"kmT{j}"])
                            P.ts(kmT[j][:, 2 * i:2 * i + 2], kmT[j][:, 2 * i:2 * i + 2], 1.0 / 256, None, ALU.mult, r=[f"kmT{j}"], w=[f"kmT{j}"])
                            for hh in range(2):
                                P.dma(mkT_d[2 * j + hh, :, t0:t0 + TB], g_[hh * 64:(hh + 1) * 64, :], r=[gn], w=["mkTd"])
                        for j in range(0 if "mq" in SKIP else 2):
                            p_, pn_ = proj(1792 + j * 128)
                            f_, fn_, g_, gn = rope(p_, pn_, True)
                            for hh in range(2):
                                P.dma(mqT_d[2 * j + hh, 0:64, t0:t0 + TB], g_[hh * 64:(hh + 1) * 64, :], r=[gn], w=["mqTd"])
                            for hh in range(2):
                                hs = slice(hh * 64, (hh + 1) * 64)
                                ns_ = nst[cnt["ns"] % 2]; nsn = f"nst{cnt['ns'] % 2}"; cnt["ns"] += 1
                                for tt in range(4):
                                    own = (t0 + tt * 128) // 256
                                    gcol = (hh * 4 + tt) * 32
                                    P.mm(pg[:, gcol:gcol + 32], f_[hs, tt * 128:(tt + 1) * 128], kmT[j][hs, :], r=[fn_, f"kmT{j}"], w=["pg"])
                                    P.memset(gm[:], -1e30, r=["gm"], w=["gm"])
                                    if own > 0:
                                        P.copy(gm[:, 0:own], pg[:, gcol:gcol + own], r=["gm"], w=["gm", "pg"])
                                    P.add("dve", lambda e: e.max(out=m8[:], in_=gm[:]), r=["gm"], w=["m8"])
                                    P.tt(selv[:], gm[:], m8[:, 2:3].to_broadcast([128, 32]), ALU.is_ge, r=["gm", "m8"], w=["selv"])
                                    P.ts(negm[:], selv[:], 30000.0, -30000.0, ALU.mult, ALU.add, r=["selv"], w=["negm"])
                                    if own < 32:
                                        P.memset(negm[:, own:own + 1], 0.0, r=["negm"], w=["negm"])
                                    tcol = 512 + tt * 128
                                    P.tr(pgb[0:32, tcol:tcol + 128], negm[:], identb[:], r=["negm", "identb"], w=["pg"])
                                    P.copy(ns_[:, tt * 128:(tt + 1) * 128], pgb[0:32, tcol:tcol + 128], r=[], w=[nsn, "pg"])
                                P.dma(mqT_d[2 * j + hh, 64:96, t0:t0 + TB], ns_[:], r=[nsn], w=["mqTd"])
                        for tt in range(0 if "v" in SKIP else 4):
                            v_ = vst[cnt["v"] % 2]; vn = f"vst{cnt['v'] % 2}"; cnt["v"] += 1
                            p_ = pa[cnt["pa"] % 4]; pn_ = f"pa{cnt['pa'] % 4}"; cnt["pa"] += 1
                            for kt in range(8):
                                P.mm(p_[:], hT[:, kt, tt * 128:(tt + 1) * 128], win[:, kt, 1280:1792], start=(kt == 0), stop=(kt == 7), r=["win", hres[kt]], w=[pn_])
                            P.copy(v_[:, 0:512], p_[:], r=[pn_], w=[vn + "a"], eng="act")
                            p2 = pa[cnt["pa"] % 4]; pn2 = f"pa{cnt['pa'] % 4}"; cnt["pa"] += 1
                            for kt in range(8):
                                P.mm(p2[:, 0:256], hT[:, kt, tt * 128:(tt + 1) * 128], win[:, kt, 2304:2560], start=(kt == 0), stop=(kt == 7), r=["win", hres[kt]], w=[pn2])
                            P.copy(v_[:, 512:768], p2[:, 0:256], r=[pn2], w=[vn + "b"], eng="dve")
                            P.dma(dV_d[t0 + tt * 128:t0 + (tt + 1) * 128, :], v_[:, 0:512], r=[vn + "a"], w=["dVd"])
                            P.dma(mV_d[t0 + tt * 128:t0 + (tt + 1) * 128, :], v_[:, 512:768], r=[vn + "b"], w=["mVd"])
                    if dbg:
                        P.dma(dbgk_d, kmT[0][:], r=["kmT0"], w=["dbgk"])
                        P.dma(dbgg_d, gm[:], r=["gm"], w=["dbgg"])
                        P.dma(dbgm_d, m8[:], r=["m8"], w=["dbgm"])
                        P.dma(dbgs_d, selv[:], r=["selv"], w=["dbgs"])
                    P.emit(block)

            if stop == "A":
                return nc
            with ExitStack() as es:
                sb = lambda n, s, d=F32: es.enter_context(nc.sbuf_tensor(U(n), list(s), d))
                uTs = [sb(f"uTs{h}", [128, S], BF16) for h in range(2)]
                are = sb("are", [128, 8]); aim = sb("aim", [128, 8]); ldt = sb("ldt", [128, 8])
                dt_ = sb("dt_", [128, 8]); mag = sb("mag", [128, 8]); th = sb("th", [128, 8])
                sth = sb("sth", [128, 8]); cth = sb("cth", [128, 8])
                s1 = sb("s1", [128, 8]); s2 = sb("s2", [128, 8]); s3 = sb("s3", [128, 8]); si = sb("si", [128, 8], I32)
                abr = sb("abr", [128, 8]); abi = sb("abi", [128, 8]); rden = sb("rden", [128, 8])
                kr = sb("kr", [128, 8]); ki = sb("ki", [128, 8]); nki = sb("nki", [128, 8])
                bre = sb("bre", [128, 8, 16]); bim = sb("bim", [128, 8, 16])
                cre = sb("cre", [128, 8, 16]); cim = sb("cim", [128, 8, 16])
                bbr = sb("bbr", [128, 8, 16]); bbi = sb("bbi", [128, 8, 16]); tma = sb("tma", [128, 16])
                X = [sb(f"X{i}", [128, 128]) for i in range(2)]
                BD = sb("BD", [128, 16, 128], BF16)
                CDm = sb("CD", [128, 16, 128], BF16)
                sdv = sb("sdv", [128, 2]); sgb = sb("sgb", [128, 2]); nsgb = sb("nsgb", [128, 2]); sng = sb("sng", [128, 2])
                gw = sb("gw", [128, 2, 256], BF16)
                jidi = sb("jidi", [128, TB], I32); jid = sb("jid", [128, TB])
                tA = sb("tA", [128, TB]); q1 = sb("q1", [128, TB]); q2 = sb("q2", [128, TB]); q3 = sb("q3", [128, TB]); qi = sb("qi", [128, TB], I32)
                Cs = sb("Cs", [128, 8, TB]); Sn = sb("Sn", [128, 8, TB])
                Hre = sb("Hre", [128, 8]); Him = sb("Him", [128, 8])
                NW = 2
                mt = [[sb(f"m{k}_{w}", [128, TB]) for k in range(4)] for w in range(NW)]
                bp = [[sb(f"bp{k}_{w}", [128, TB]) for k in range(2)] for w in range(NW)]
                gg = [[sb(f"g{k}_{w}", [128, TB]) for k in range(2)] for w in range(NW)]
                pp = [[sb(f"p{k}_{w}", [128, TB]) for k in range(4)] for w in range(NW)]
                hb = [[sb(f"hb{k}_{w}", [128, TB], BF16) for k in range(2)] for w in range(NW)]
                yv = [sb(f"yv{h}", [128, TB]) for h in range(2)]
                ya = sb("ya", [128, TB]); yb = sb("yb", [128, TB]); yc = sb("yc", [128, TB])
                gl = [sb(f"gl{h}", [128, TB]) for h in range(2)]
                glb = [sb(f"glb{h}", [128, TB], BF16) for h in range(2)]
                zz = [sb(f"zz{h}", [128, TB]) for h in range(2)]
                zsq = [sb(f"zsq{h}", [128, TB], BF16) for h in range(2)]
                rs_ = sb("rs_", [128, TB]); ln_ = sb("ln_", [128, TB])
                mo = [sb(f"mo{h}", [128, TB], BF16) for h in range(2)]
                pbu = [es.enter_context(nc.psum_tensor(U(f"pbu{i}"), [128, TB], F32)) for i in range(4)]
                py = [es.enter_context(nc.psum_tensor(U(f"py{i}"), [128, TB], F32)) for i in range(2)]
                pgx = es.enter_context(nc.psum_tensor(U("pgx"), [128, TB], F32))
                pn2 = es.enter_context(nc.psum_tensor(U("pn2"), [128, TB], F32))
                with nc.Block() as block:
                    for h in range(2):
                        P.dma(uTs[h][:], uT_d[h * 128:(h + 1) * 128, :], w=[f"uTs{h}"])
                    for t_, d_, n_ in ((are, sare_d, "are"), (aim, saim_d, "aim"), (ldt, sldt_d, "ldt"), (bre, sbre_d, "bre"), (bim, sbim_d, "bim"),
                                       (cre, scre_d, "cre"), (cim, scim_d, "cim"), (sdv, sd_d, "sdv"), (sgb, sgb_d, "sgb"), (sng, sng_d, "sng")):
                        P.dma(t_[:], d_[l], w=[n_])
                    load_cast(P, gw, sgw_d[l], 2, 256, "gw")
                    P.act(dt_[:], ldt[:], AF.Exp, r=["ldt"], w=["dt_"])
                    P.tt(s1[:], are[:], dt_[:], ALU.mult, r=["are", "dt_"], w=["s1"])
                    P.act(mag[:], s1[:], AF.Exp, r=["s1"], w=["mag"])
                    P.tt(th[:], aim[:], dt_[:], ALU.mult, r=["aim", "dt_"], w=["sang"])
                    sincos(P, th[:], 8, sth[:], cth[:], s1, s2, s3, si, "s", "sth", "cth")
                    P.tt(abr[:], mag[:], cth[:], ALU.mult, r=["mag", "cth"], w=["abr"])
                    P.tt(abi[:], mag[:], sth[:], ALU.mult, r=["mag", "sth"], w=["abi"])
                    P.tt(s1[:], are[:], are[:], ALU.mult, r=["are", "st1"], w=["s1b"])
                    P.tt(s2[:], aim[:], aim[:], ALU.mult, r=["aim", "st2"], w=["s2b"])
                    P.tt(s1[:], s1[:], s2[:], ALU.add, r=["s1b", "s2b"], w=["s1b"])
                    P.add("dve", lambda e: e.reciprocal(out=rden[:], in_=s1[:]), r=["s1b"], w=["rden"])
                    P.ts(s3[:], abr[:], -1.0, None, ALU.add, r=["abr", "st3"], w=["nr"])
                    P.tt(s1[:], s3[:], are[:], ALU.mult, r=["nr", "are", "rden"], w=["s1c"])
                    P.tt(s2[:], abi[:], aim[:], ALU.mult, r=["abi", "aim", "s2b"], w=["s2c"])
                    P.tt(s1[:], s1[:], s2[:], ALU.add, r=["s1c", "s2c"], w=["s1c"])
                    P.tt(kr[:], s1[:], rden[:], ALU.mult, r=["s1c", "rden"], w=["kr"])
                    P.tt(s1[:], abi[:], are[:], ALU.mult, r=["abi", "are", "kr"], w=["s1d"])
                    P.tt(s2[:], s3[:], aim[:], ALU.mult, r=["nr", "aim", "s1c"], w=["s2d"])
                    P.tt(s1[:], s1[:], s2[:], ALU.subtract, r=["s1d", "s2d"], w=["s1d"])
                    P.tt(ki[:], s1[:], rden[:], ALU.mult, r=["s1d", "rden"], w=["ki"])
                    P.ts(nki[:], ki[:], -1.0, None, ALU.mult, r=["ki"], w=["nki"])
                    P.ts(nsgb[:], sgb[:], -1.0, None, ALU.mult, r=["sgb"], w=["nsgb"])
                    P.memset(BD[:], 0.0, w=["BD"], eng="pool")
                    P.memset(CDm[:], 0.0, w=["CD"], eng="pool")
                    P.memset(Hre[:], 0.0, w=["Hre"], eng="pool")
                    P.memset(Him[:], 0.0, w=["Him"], eng="pool")
                    for st in range(8):
                        P.ts(tma[:], bre[:, st, :], kr[:, st:st + 1], None, ALU.mult, r=["bre", "kr", "bbr"], w=["tma"])
                        P.stt(bbr[:, st, :], bim[:, st, :], nki[:, st:st + 1], tma[:], ALU.mult, ALU.add, r=["bim", "nki", "tma"], w=["bbr"])
                        P.ts(tma[:], bim[:, st, :], kr[:, st:st + 1], None, ALU.mult, r=["bim", "kr", "bbr"], w=["tma"])
                        P.stt(bbi[:, st, :], bre[:, st, :], ki[:, st:st + 1], tma[:], ALU.mult, ALU.add, r=["bre", "ki", "tma"], w=["bbi"])
                    kx = 0
                    for st in range(8):
                        gl0 = (2 * st) % 8
                        for ri, bsrc, bn in ((0, bbr, "bbr"), (1, bbi, "bbi")):
                            X_ = X[kx % 2]; Xn = f"X{kx % 2}"; kx += 1
                            P.memset(X_[:], 0.0, w=[Xn], eng="pool")
                            P.copy(X_[0:64, gl0 * 16:(gl0 + 1) * 16], bsrc[0:64, st, :], r=[bn, Xn], w=[Xn])
                            P.copy(X_[64:128, (gl0 + 1) * 16:(gl0 + 2) * 16], bsrc[64:128, st, :], r=[bn, Xn], w=[Xn])
                            pt_ = pbu[kx % 4]; ptn = f"pbu{kx % 4}"
                            P.tr(pt_[:, 0:128], X_[:], ident[:], r=[Xn], w=[ptn])
                            P.copy(BD[:, st * 2 + ri, :], pt_[:, 0:128], r=[ptn, "BD"], w=["BD"], eng="act")
                        P.copy(CDm[0:64, st * 2, gl0 * 16:(gl0 + 1) * 16], cre[0:64, st, :], r=["cre", "CD"], w=["CD"])
                        P.copy(CDm[64:128, st * 2, (gl0 + 1) * 16:(gl0 + 2) * 16], cre[64:128, st, :], r=["cre", "CD"], w=["CD"])
                        P.ts(CDm[0:64, st * 2 + 1, gl0 * 16:(gl0 + 1) * 16], cim[0:64, st, :], -1.0, None, ALU.mult, r=["cim", "CD"], w=["CD"])
                        P.ts(CDm[64:128, st * 2 + 1, (gl0 + 1) * 16:(gl0 + 2) * 16], cim[64:128, st, :], -1.0, None, ALU.mult, r=["cim", "CD"], w=["CD"])
                    P.add("pool", lambda e: e.iota(jidi[:], pattern=[[1, TB]], base=1, channel_multiplier=0), w=["jidi"])
                    P.copy(jid[:], jidi[:], r=["jidi"], w=["jid"])
                    for st in range(8):
                        P.ts(tA[:], jid[:], th[:, st:st + 1], None, ALU.mult, r=["jid", "sang", "tsin", "tcos"], w=["tang"])
                        sincos(P, tA[:], TB, Sn[:, st, :], Cs[:, st, :], q1, q2, q3, qi, "t", "Sn", "Cs")
                    it = 0
                    for i in range(NB):
                        t0 = i * TB
                        for st in range(8):
                            w_ = it % NW; it += 1
                            half = st // 4
                            m = mt[w_]; mn = [f"m{k}_{w_}" for k in range(4)]
                            b_ = bp[w_]; bn_ = [f"bp{k}_{w_}" for k in range(2)]
                            g_ = gg[w_]; gn_ = [f"g{k}_{w_}" for k in range(2)]
                            p_ = pp[w_]; pn_ = [f"p{k}_{w_}" for k in range(4)]
                            h_ = hb[w_]; hn_ = [f"hb{k}_{w_}" for k in range(2)]
                            pr, prn = pbu[(2 * it) % 4], f"pbu{(2 * it) % 4}"
                            pi2, pin = pbu[(2 * it + 1) % 4], f"pbu{(2 * it + 1) % 4}"
                            P.mm(pr[:], BD[:, st * 2, :], uTs[half][:, t0:t0 + TB], r=["BD", f"uTs{half}"], w=[prn])
                            P.mm(pi2[:], BD[:, st * 2 + 1, :], uTs[half][:, t0:t0 + TB], r=["BD", f"uTs{half}"], w=[pin])
                            cs_, sn_ = Cs[:, st, :], Sn[:, st, :]
                            P.tt(m[0][:], pr[:], cs_, ALU.mult, r=[prn, "Cs"], w=[mn[0]])
                            P.tt(m[1][:], pi2[:], sn_, ALU.mult, r=[pin, "Sn"], w=[mn[1]])
                            P.tt(m[2][:], pi2[:], cs_, ALU.mult, r=[pin, "Cs"], w=[mn[2]])
                            P.tt(m[3][:], pr[:], sn_, ALU.mult, r=[prn, "Sn"], w=[mn[3]])
                            P.tt(b_[0][:], m[0][:], m[1][:], ALU.add, r=[mn[0], mn[1]], w=[bn_[0]], eng="pool")
                            P.tt(b_[1][:], m[2][:], m[3][:], ALU.subtract, r=[mn[2], mn[3]], w=[bn_[1]], eng="pool")
                            rbc = mag[:, st:st + 1].to_broadcast([128, TB])
                            P.add("dve", lambda e, o=g_[0], d1=b_[0], ini=Hre[:, st:st + 1], rbc=rbc: e.tensor_tensor_scan(out=o[:], data0=rbc, data1=d1[:], initial=ini, op0=ALU.mult, op1=ALU.add),
                                  r=[bn_[0], "mag", "Hre"], w=[gn_[0]])
                            P.add("dve", lambda e, o=g_[1], d1=b_[1], ini=Him[:, st:st + 1], rbc=rbc: e.tensor_tensor_scan(out=o[:], data0=rbc, data1=d1[:], initial=ini, op0=ALU.mult, op1=ALU.add),
                                  r=[bn_[1], "mag", "Him"], w=[gn_[1]])
                            P.tt(p_[0][:], g_[0][:], cs_, ALU.mult, r=[gn_[0], "Cs"], w=[pn_[0]])
                            P.tt(p_[1][:], g_[1][:], sn_, ALU.mult, r=[gn_[1], "Sn"], w=[pn_[1]])
                            P.tt(p_[2][:], g_[0][:], sn_, ALU.mult, r=[gn_[0], "Sn"], w=[pn_[2]], eng="pool")
                            P.tt(p_[3][:], g_[1][:], cs_, ALU.mult, r=[gn_[1], "Cs"], w=[pn_[3]], eng="pool")
                            P.tt(h_[0][:], p_[0][:], p_[1][:], ALU.subtract, r=[pn_[0], pn_[1]], w=[hn_[0]], eng="pool")
                            P.tt(h_[1][:], p_[2][:], p_[3][:], ALU.add, r=[pn_[2], pn_[3]], w=[hn_[1]], eng="pool")
                            P.tt(Hre[:, st:st + 1], p_[0][:, TB - 1:TB], p_[1][:, TB - 1:TB], ALU.subtract, r=[pn_[0], pn_[1], "Hre"], w=["Hre"])
                            P.tt(Him[:, st:st + 1], p_[2][:, TB - 1:TB], p_[3][:, TB - 1:TB], ALU.add, r=[pn_[2], pn_[3], "Him"], w=["Him"])
                            P.mm(py[half][:], CDm[:, st * 2, :], h_[0][:], start=(st % 4 == 0), stop=False, r=["CD", hn_[0]], w=[f"py{half}"])
                            P.mm(py[half][:], CDm[:, st * 2 + 1, :], h_[1][:], start=False, stop=(st % 4 == 3), r=["CD", hn_[1]], w=[f"py{half}"])
                        for h in range(2):
                            P.stt(yv[h][:], uTs[h][:, t0:t0 + TB], sdv[:, h:h + 1], py[h][:], ALU.mult, ALU.add, r=[f"uTs{h}", "sdv", f"py{h}"], w=[f"yv{h}"])
                            P.act(ya[:], yv[h][:], AF.Square, r=[f"yv{h}"], w=["ya"])
                            P.ts(yb[:], ya[:], 0.044715, 1.0, ALU.mult, ALU.add, r=["ya"], w=["yb"])
                            P.tt(yc[:], yb[:], yv[h][:], ALU.mult, r=["yb", f"yv{h}"], w=["yc"], eng="pool")
                            P.act(ya[:], yc[:], AF.Exp, r=["yc"], w=["ya"], scale=-1.5957691216)
                            P.ts(yb[:], ya[:], 1.0, None, ALU.add, r=["ya"], w=["yb"])
                            P.add("dve", lambda e: e.reciprocal(out=yc[:], in_=yb[:]), r=["yb"], w=["yc"])
                            P.tt(gl[h][:], yv[h][:], yc[:], ALU.mult, r=[f"yv{h}", "yc"], w=[f"gl{h}"], eng="pool")
                            P.copy(glb[h][:], gl[h][:], r=[f"gl{h}"], w=[f"glb{h}"], eng="act")
                        for oh in range(2):
                            for kh in range(2):
                                P.mm(pgx[:], gw[:, kh, oh * 128:(oh + 1) * 128], glb[kh][:], start=(kh == 0), stop=(kh == 1), r=["gw", f"glb{kh}"], w=["pgx"])
                            P.act(ya[:], pgx[:], AF.Exp, r=["pgx", "nsgb"], w=["ya"], scale=-1.0, bias=nsgb[:, oh:oh + 1])
                            P.ts(yb[:], ya[:], 1.0, None, ALU.add, r=["ya"], w=["yb"])
                            P.add("dve", lambda e: e.reciprocal(out=yc[:], in_=yb[:]), r=["yb"], w=["yc"])
                            P.tt(zz[oh][:], gl[oh][:], yc[:], ALU.mult, r=[f"gl{oh}", "yc"], w=[f"zz{oh}"], eng="pool")
                            P.act(zsq[oh][:], zz[oh][:], AF.Square, r=[f"zz{oh}"], w=[f"zsq{oh}"])
                            P.mm(pn2[:], onesb[:], zsq[oh][:], start=(oh == 0), stop=(oh == 1), r=[f"zsq{oh}"], w=["pn2"])
                        P.act(ln_[:], pn2[:], AF.Ln, r=["pn2"], w=["ln_"], scale=1.0 / 256, bias=epsc[:, 0:1])
                        P.act(rs_[:], ln_[:], AF.Exp, r=["ln_"], w=["rs_"], scale=-0.5)
                        for oh in range(2):
                            P.stt(mo[oh][:], zz[oh][:], sng[:, oh:oh + 1], rs_[:], ALU.mult, ALU.mult, r=[f"zz{oh}", "sng", "rs_"], w=[f"mo{oh}"])
                            P.dma(mixT_d[oh * 128:(oh + 1) * 128, t0:t0 + TB], mo[oh][:], r=[f"mo{oh}"], w=["mixTd"])
                    P.emit(block)

            if stop == "S":
                return nc
            lam_init = 0.8 - 0.6 * math.exp(-0.3 * l)

            def attention(P, streams, nq_blocks, epilogue, psS, psAcc, Pt, cnt):
                for qb_ in range(nq_blocks):
                    q0 = qb_ * TB
                    nkt = (q0 + TB) // 128
                    for kt in range(nkt):
                        a_ = kt - q0 // 128
                        qs = max(a_, 0) * 128
                        for si_, st_ in enumerate(streams):
                            bi = cnt["s"] % len(psS); cnt["s"] += 1
                            ps_, psn = psS[bi], f"psS{bi}"
                            pt_, ptn = Pt[bi], f"Pt{bi}"
                            rows = st_["rows"]
                            P.mm(ps_[:, qs:TB], st_["kT"][rows, kt * 128:(kt + 1) * 128], st_["qT"][rows, q0 + qs:q0 + TB], r=[st_["kn"], st_["qn"]], w=[psn])
                            P.act(pt_[:, qs:TB], ps_[:, qs:TB], AF.Exp, r=[psn], w=[ptn], scale=0.125)
                            if a_ >= 0:
                                P.tt(pt_[:, qs:qs + 128], pt_[:, qs:qs + 128], tri[:], ALU.mult, r=[ptn, "tri"], w=[ptn], eng=("pool" if si_ % 2 else "dve"))
                            for lf, ai in st_["vals"]:
                                P.mm(psAcc[ai][:, qs:TB], lf(kt), pt_[:, qs:TB], start=(kt == 0), stop=(kt == nkt - 1), r=[ptn, st_["vn"]], w=[f"acc{ai}"])
                    epilogue(qb_, q0)

            with ExitStack() as es:
                sb = lambda n, s, d=F32: es.enter_context(nc.sbuf_tensor(U(n), list(s), d))
                qT = sb("qT", [128, S], BF16); kT = sb("kT", [128, S], BF16)
                Vt = sb("Vt", [128, NKT, 128], BF16)
                Pt = [sb(f"Pt{i}", [128, TB], BF16) for i in range(4)]
                lv = sb("lv", [128, 256]); lp = sb("lp", [128, 64]); ls = sb("ls", [128, 2]); nlam = sb("nlam", [128, 1])
                dsg = sb("dsg", [128, 1])
                rc0 = sb("rc0", [128, TB]); rc1 = sb("rc1", [128, TB]); o0 = sb("o0", [128, TB]); o1 = sb("o1", [128, TB])
                osq = sb("osq", [128, TB], BF16); dl = sb("dl", [128, TB]); dr = sb("dr", [128, TB])
                do_ = [sb(f"do{i}", [128, TB], BF16) for i in range(2)]
                psS = [es.enter_context(nc.psum_tensor(U(f"psS{i}"), [128, TB], F32)) for i in range(4)]
                psAcc = [es.enter_context(nc.psum_tensor(U(f"acc{i}"), [128, TB], F32)) for i in range(4)]
                with nc.Block() as block:
                    P.dma(lv[:], lam_d[l:l + 1, :].to_broadcast([128, 256]), w=["lv"])
                    P.dma(dsg[:], dsg_d[l], w=["dsg"])
                    for k2 in range(2):
                        P.tt(lp[:], lv[:, 128 * k2:128 * k2 + 64], lv[:, 128 * k2 + 64:128 * k2 + 128], ALU.mult, r=["lv", "ls"], w=["lp"])
                        P.add("dve", lambda e, k2=k2: e.tensor_reduce(out=ls[:, k2:k2 + 1], in_=lp[:], axis=AX.X, op=ALU.add), r=["lp"], w=["ls"])
                    P.act(ls[:], ls[:], AF.Exp, r=["ls"], w=["ls"])
                    P.stt(nlam[:], ls[:, 1:2], -lam_init, ls[:, 0:1], ALU.add, ALU.subtract, r=["ls"], w=["nlam"])
                    P.ts(dsg[:], dsg[:], 1.0 - lam_init, None, ALU.mult, r=["dsg"], w=["dsg"])
                    cnt = {"s": 0, "o": 0}
                    for h in range(4):
                        P.dma(qT[:], dqT_d[h], w=["qT"])
                        P.dma(kT[:], dkT_d[h], w=["kT"])
                        P.dma(Vt[:], dV_d.rearrange("(kt p) c -> p kt c", p=128)[:, :, h * 128:(h + 1) * 128], w=["Vt"])
                        streams = [dict(kT=kT, qT=qT, rows=slice(c * 64, (c + 1) * 64), kn="kT", qn="qT", vn="Vt",
                                        vals=[(lambda kt: Vt[:, kt, :], 2 * c), (lambda kt: onesb[:], 2 * c + 1)]) for c in range(2)]

                        def epi(qb_, q0, h=h):
                            P.add("dve", lambda e: e.reciprocal(out=rc0[:], in_=psAcc[1][:]), r=["acc1"], w=["rc0"])
                            P.add("dve", lambda e: e.reciprocal(out=rc1[:], in_=psAcc[3][:]), r=["acc3"], w=["rc1"])
                            P.tt(o0[:], psAcc[0][:], rc0[:], ALU.mult, r=["acc0", "rc0"], w=["o0"])
                            P.tt(o1[:], psAcc[2][:], rc1[:], ALU.mult, r=["acc2", "rc1"], w=["o1"])
                            P.stt(o0[:], o1[:], nlam[:, 0:1], o0[:], ALU.mult, ALU.add, r=["o1", "o0", "nlam"], w=["o0"])
                            P.act(osq[:], o0[:], AF.Square, r=["o0"], w=["osq"])
                            P.mm(psS[0][:], onesb[:], osq[:], r=["osq"], w=["psS0"])
                            P.act(dl[:], psS[0][:], AF.Ln, r=["psS0"], w=["dl"], scale=1.0 / 128, bias=epsc[:, 0:1])
                            P.act(dr[:], dl[:], AF.Exp, r=["dl"], w=["dr"], scale=-0.5)
                            d_ = do_[cnt["o"] % 2]; dn = f"do{cnt['o'] % 2}"; cnt["o"] += 1
                            P.stt(d_[:], o0[:], dsg[:, 0:1], dr[:], ALU.mult, ALU.mult, r=["o0", "dsg", "dr"], w=[dn])
                            P.dma(mixT_d[256 + h * 128:256 + (h + 1) * 128, q0:q0 + TB], d_[:], r=[dn], w=["mixTd"])

                        attention(P, streams, NB, epi, psS, psAcc, Pt, cnt)
                    P.emit(block)

            if stop == "D":
                return nc
            with ExitStack() as es:
                sb = lambda n, s, d=F32: es.enter_context(nc.sbuf_tensor(U(n), list(s), d))
                qT = sb("mqTs", [96, S], BF16); kT = sb("mkTs", [96, S], BF16)
                Vt = sb("mVt", [128, NKT, 128], BF16)
                onesS = sb("onesS", [96, S], BF16); tmpS = sb("tmpS", [96, S], BF16)
                Pt = [sb(f"Pt{i}", [128, TB], BF16) for i in range(4)]
                mng = sb("mngs", [64, 1])
                osb = sb("osb", [128, TB]); rc0 = sb("mrc", [64, TB]); o0 = sb("mo0", [64, TB])
                osq = sb("mosq", [64, TB], BF16); dl = sb("mdl", [64, TB]); dr = sb("mdr", [64, TB])
                do_ = [sb(f"mdo{i}", [64, TB], BF16) for i in range(2)]
                psS = [es.enter_context(nc.psum_tensor(U(f"psS{i}"), [128, TB], F32)) for i in range(4)]
                psAcc = [es.enter_context(nc.psum_tensor(U(f"acc{i}"), [128, TB], F32)) for i in range(2)]
                pmv = es.enter_context(nc.psum_tensor(U("pmv"), [128, TB], F32))
                pnm = es.enter_context(nc.psum_tensor(U("pnm"), [128, TB], F32))
                with nc.Block() as block:
                    P.dma(mng[:], mng_d[l], w=["mng"])
                    P.memset(Vt[:], 1.0, w=["Vt"], eng="pool")
                    P.memset(onesS[64:96, :], 1.0, w=["onesS"], eng="pool")
                    P.add("pool", lambda e: e.affine_select(out=tmpS[64:96, :], in_=onesS[64:96, :], pattern=[[1, S]], compare_op=ALU.is_ge, fill=0.0, base=0, channel_multiplier=-256), r=["onesS"], w=["tmpS"])
                    P.add("pool", lambda e: e.affine_select(out=kT[64:96, :], in_=tmpS[64:96, :], pattern=[[-1, S]], compare_op=ALU.is_ge, fill=0.0, base=255, channel_multiplier=256), r=["tmpS"], w=["kT1h"])
                    cnt = {"s": 0, "o": 0, "a": 0}
                    for h in range(4):
                        P.dma(qT[:], mqT_d[h], w=["qT"])
                        P.dma(kT[0:64, :], mkT_d[h], r=["kT1h"], w=["kT"])
                        P.dma(Vt[:, :, 0:64], mV_d.rearrange("(kt p) c -> p kt c", p=128)[:, :, h * 64:(h + 1) * 64], w=["Vt"])
                        streams = [dict(kT=kT, qT=qT, rows=slice(0, 96), kn="kT", qn="qT", vn="Vt", vals=[(lambda kt: Vt[:, kt, :], 0)])]

                        def epi(qb_, q0, h=h):
                            P.copy(osb[:], psAcc[0][:], r=["acc0"], w=["osb"], eng="act")
                            P.mm(pmv[0:64, :], ident[:, 64:128], osb[:], r=["osb"], w=["pmv"])
                            P.add("dve", lambda e: e.reciprocal(out=rc0[:], in_=pmv[0:64, :]), r=["pmv"], w=["rc0"])
                            P.tt(o0[:], osb[0:64, :], rc0[:], ALU.mult, r=["osb", "rc0"], w=["o0"])
                            P.act(osq[:], o0[:], AF.Square, r=["o0"], w=["osq"])
                            P.mm(pnm[0:64, :], onesb[0:64, 0:64], osq[:], r=["osq"], w=["pnm"])
                            P.act(dl[:], pnm[0:64, :], AF.Ln, r=["pnm"], w=["dl"], scale=1.0 / 64, bias=epsc[0:64, 0:1])
                            P.act(dr[:], dl[:], AF.Exp, r=["dl"], w=["dr"], scale=-0.5)
                            d_ = do_[cnt["o"] % 2]; dn = f"mdo{cnt['o'] % 2}"; cnt["o"] += 1
                            P.stt(d_[:], o0[:], mng[:, 0:1], dr[:], ALU.mult, ALU.mult, r=["o0", "mng", "dr"], w=[dn])
                            P.dma(mixT_d[768 + h * 64:768 + (h + 1) * 64, q0:q0 + TB], d_[:], r=[dn], w=["mixTd"])

                        attention(P, streams, NB, epi, psS, psAcc, Pt, cnt)
                    P.emit(block)

            if stop == "M":
                return nc
            with ExitStack() as es:
                sb = lambda n, s, d=F32: es.enter_context(nc.sbuf_tensor(U(n), list(s), d))
                wo = sb("wo", [128, 8, D], BF16)
                xbs = [sb(f"xb{i}", [128, 8, TB]) for i in range(2)]
                mxs = [sb(f"mx{i}", [128, 8, TB], BF16) for i in range(2)]
                pc = [es.enter_context(nc.psum_tensor(U(f"pc{i}"), [128, TB], F32)) for i in range(4)]
                with nc.Block() as block:
                    load_cast(P, wo, wout_d[l], 8, D, "wo")
                    mixv = mixT_d.rearrange("(ft p) t -> p ft t", p=128)

                    def ld(i):
                        P.dma(xbs[i % 2][:], xTv[:, :, i * TB:(i + 1) * TB], r=["xTd"], w=[f"xb{i % 2}_{ft}" for ft in range(8)])
                        P.dma(mxs[i % 2][:], mixv[:, :, i * TB:(i + 1) * TB], w=[f"mx{i % 2}"])
                    ld(0)
                    k = 0
                    for i in range(NB):
                        if i + 1 < NB:
                            ld(i + 1)
                        xb = xbs[i % 2]; mx = mxs[i % 2]
                        for fo in range(8):
                            p_ = pc[k % 4]; pn_ = f"pc{k % 4}"; k += 1
                            for kt in range(8):
                                P.mm(p_[:], wo[:, kt, fo * 128:(fo + 1) * 128], mx[:, kt, :], start=(kt == 0), stop=(kt == 7), r=["wo", f"mx{i % 2}"], w=[pn_])
                            P.stt(xb[:, fo, :], p_[:], modv[:, l, 16 + fo:17 + fo], xb[:, fo, :], ALU.mult, ALU.add, r=[pn_, f"xb{i % 2}_{fo}"], w=[f"xb{i % 2}_{fo}"])
                        P.dma(xTv[:, :, i * TB:(i + 1) * TB], xb[:], r=[f"xb{i % 2}_{ft}" for ft in range(8)], w=["xTd"])
                    P.emit(block)

            if stop == "C1":
                return nc
            with ExitStack() as es:
                sb = lambda n, s, d=F32: es.enter_context(nc.sbuf_tensor(U(n), list(s), d))
                w1 = sb("w1", [128, 8, DFF], BF16)
                w2 = sb("w2", [128, 32, D], BF16)
                xb = sb("xb", [128, 8, TB])
                hT = sb("hT", [128, 8, TB], BF16)
                hid = sb("hid", [128, 32, TB], BF16)
                sqs = [sb(f"sq{i}", [128, TB], BF16) for i in range(2)]
                tmps = [sb(f"nt{i}", [128, TB]) for i in range(2)]
                rstd = sb("rstd", [128, TB]); lnv = sb("lnv", [128, TB])
                rl = [sb(f"rl{i}", [128, TB]) for i in range(2)]
                pc = [es.enter_context(nc.psum_tensor(U(f"pc{i}"), [128, TB], F32)) for i in range(5)]
                pd = [es.enter_context(nc.psum_tensor(U(f"pd{i}"), [128, TB], F32)) for i in range(2)]
                pn = es.enter_context(nc.psum_tensor(U("pn"), [128, TB], F32))
                with nc.Block() as block:
                    load_cast(P, w1, w1_d[l], 8, DFF, "w1")
                    load_cast(P, w2, w2_d[l], 32, D, "w2")
                    k = 0; k2 = 0
                    for i in range(NB):
                        P.dma(xb[:], xTv[:, :, i * TB:(i + 1) * TB], r=["xTd"], w=[f"xb_{ft}" for ft in range(8)] + ["xb"])
                        norm_mod(P, xb, "xb", hT, "hT", A2, lambda kt: modv[:, l, 24 + kt:25 + kt], l, pn, tmps, sqs, rstd, lnv)
                        hres = [f"hT_{kt}" for kt in range(8)]
                        for ft in range(32):
                            p_ = pc[k % 5]; pn_ = f"pc{k % 5}"; r_ = rl[k % 2]; rn = f"rl{k % 2}"; k += 1
                            for kt in range(8):
                                P.mm(p_[:], w1[:, kt, ft * 128:(ft + 1) * 128], hT[:, kt, :], start=(kt == 0), stop=(kt == 7), r=["w1", hres[kt]], w=[pn_])
                            P.act(r_[:], p_[:], AF.Relu, r=[pn_], w=[rn])
                            P.tt(hid[:, ft, :], r_[:], r_[:], ALU.mult, r=[rn], w=[f"hid{ft}"], eng=("pool" if ft % 2 else "dve"))
                        for fo in range(8):
                            p_ = pd[k2 % 2]; pn_ = f"pd{k2 % 2}"; k2 += 1
                            for ft in range(32):
                                P.mm(p_[:], w2[:, ft, fo * 128:(fo + 1) * 128], hid[:, ft, :], start=(ft == 0), stop=(ft == 31), r=["w2", f"hid{ft}"], w=[pn_])
                            P.stt(xb[:, fo, :], p_[:], modv[:, l, 40 + fo:41 + fo], xb[:, fo, :], ALU.mult, ALU.add, r=[pn_, "xb", f"xb_{fo}"], w=[f"xb_{fo}"])
                        P.dma(xTv[:, :, i * TB:(i + 1) * TB], xb[:], r=[f"xb_{ft}" for ft in range(8)], w=["xTd", "xb"])
                    P.emit(block)

        if stop == "C2":
            return nc
        with ExitStack() as es:
            sb = lambda n, s, d=F32: es.enter_context(nc.sbuf_tensor(U(n), list(s), d))
            xbs = [sb(f"xb{i}", [128, 8, TB]) for i in range(2)]
            yn = sb("yn", [128, 8, TB])
            sqs = [sb(f"sq{i}", [128, TB], BF16) for i in range(2)]
            rstd = sb("rstd", [128, TB]); lnv = sb("lnv", [128, TB])
            ost = [sb(f"ost{i}", [128, D]) for i in range(2)]
            pt = [es.enter_context(nc.psum_tensor(U(f"pt{i}"), [128, TB], F32)) for i in range(6)]
            pn = es.enter_context(nc.psum_tensor(U("pn"), [128, TB], F32))
            with nc.Block() as block:
                P.dma(xbs[0][:], xTv[:, :, 0:TB], w=["xb0"])
                k = 0; ko = 0
                for i in range(NB):
                    if i + 1 < NB:
                        P.dma(xbs[(i + 1) % 2][:], xTv[:, :, (i + 1) * TB:(i + 2) * TB], w=[f"xb{(i + 1) % 2}"])
                    xb = xbs[i % 2]; xn = f"xb{i % 2}"
                    for kt in range(8):
                        sq = sqs[kt % 2]; sqn = f"sq{kt % 2}"
                        P.act(sq[:], xb[:, kt, :], AF.Square, r=[xn], w=[sqn])
                        P.mm(pn[:], onesb[:], sq[:], start=(kt == 0), stop=(kt == 7), r=[sqn], w=["pn"])
                    P.act(lnv[:], pn[:], AF.Ln, r=["pn"], w=["lnv"], scale=1.0 / D, bias=epsc[:, 0:1])
                    P.act(rstd[:], lnv[:], AF.Exp, r=["lnv"], w=["rstd"], scale=-0.5)
                    for kt in range(8):
                        P.stt(yn[:, kt, :], xb[:, kt, :], fgT[:, kt:kt + 1], rstd[:], ALU.mult, ALU.mult, r=[xn, "rstd"], w=[f"yn{kt}"])
                    for tt in range(4):
                        o_ = ost[ko % 2]; on = f"ost{ko % 2}"; ko += 1
                        for hf in range(2):
                            p_ = pt[k % 6]; pn_ = f"pt{k % 6}"; k += 1
                            for kk in range(4):
                                kt = hf * 4 + kk
                                P.tr(p_[:, kk * 128:(kk + 1) * 128], yn[:, kt, tt * 128:(tt + 1) * 128], ident[:], r=[f"yn{kt}"], w=[pn_])
                            P.copy(o_[:, hf * 512:(hf + 1) * 512], p_[:], r=[pn_], w=[f"{on}_{hf}"], eng=("act" if hf else "dve"))
                        P.dma(out_d[i * TB + tt * 128:i * TB + (tt + 1) * 128, :], o_[:], r=[f"{on}_0", f"{on}_1"], w=["outd"])
                P.emit(block)
    return nc


def _layout_inputs(inp, S):
    f = lambda a: np.ascontiguousarray(a, dtype=np.float32)
    col8 = lambda v: f(np.asarray(v).reshape(8, 128).T)
    cst = np.zeros((128, 2), np.float32)
    inv = (ROPE_THETA ** (-np.arange(0, 16, 2, dtype=np.float32) / 16)).astype(np.float32)
    for s0 in (0, 64):
        for d in range(16):
            cst[s0 + d, 0] = inv[d % 8]
            cst[s0 + d, 1] = -1.0 if d < 8 else 1.0
    L = DEPTH
    shared = {
        "cst": cst,
        "w_ada": f(inp["w_ada"]),
        "b_adaT": f(np.asarray(inp["b_ada"]).reshape(L, 48, 128).transpose(0, 2, 1)),
        "n1gT": f(np.asarray(inp["norm1_g"]).reshape(L, 8, 128).transpose(0, 2, 1)),
        "n2gT": f(np.asarray(inp["norm2_g"]).reshape(L, 8, 128).transpose(0, 2, 1)),
        "fgT": col8(inp["final_g"]),
        "w_in": f(inp["w_in"]), "w_out": f(inp["w_out"]), "mlp_w1": f(inp["mlp_w1"]), "mlp_w2": f(inp["mlp_w2"]),
        "s_are": f(np.asarray(inp["ssm_a_re"]).reshape(L, 8, 128).transpose(0, 2, 1)),
        "s_aim": f(np.asarray(inp["ssm_a_im"]).reshape(L, 8, 128).transpose(0, 2, 1)),
        "s_ldt": f(np.repeat(np.asarray(inp["ssm_log_dt"]).reshape(L, 8, 2), 64, axis=2).transpose(0, 2, 1)),
        "s_bre": f(np.asarray(inp["ssm_b_re"]).reshape(L, 8, 2, 64, 16).transpose(0, 2, 3, 1, 4).reshape(L, 128, 8, 16)),
        "s_bim": f(np.asarray(inp["ssm_b_im"]).reshape(L, 8, 2, 64, 16).transpose(0, 2, 3, 1, 4).reshape(L, 128, 8, 16)),
        "s_cre": f(np.asarray(inp["ssm_c_re"]).reshape(L, 8, 2, 16, 64).transpose(0, 2, 4, 1, 3).reshape(L, 128, 8, 16)),
        "s_cim": f(np.asarray(inp["ssm_c_im"]).reshape(L, 8, 2, 16, 64).transpose(0, 2, 4, 1, 3).reshape(L, 128, 8, 16)),
        "s_d": f(np.asarray(inp["ssm_d"]).reshape(L, 2, 128).transpose(0, 2, 1)),
        "s_gw": f(inp["ssm_glu_w"]),
        "s_gb": f(np.asarray(inp["ssm_glu_b"]).reshape(L, 2, 128).transpose(0, 2, 1)),
        "s_ng": f(np.asarray(inp["ssm_norm_g"]).reshape(L, 2, 128).transpose(0, 2, 1)),
        "lamv": f(np.concatenate([np.asarray(inp[k]) for k in ("diff_lq1", "diff_lk1", "diff_lq2", "diff_lk2")], axis=1)),
        "dsg": f(np.asarray(inp["diff_subln_g"]).reshape(L, 128, 1)),
        "mng": f(np.asarray(inp["moba_norm_g"]).reshape(L, 64, 1)),
    }
    x = np.asarray(inp["x"]); c = np.asarray(inp["c"]); pos = np.asarray(inp["positions"])
    maps = []
    for core in range(8):
        b = core % x.shape[0]
        m = dict(shared)
        m["x"] = f(x[b, :S])
        m["pos"] = np.ascontiguousarray(pos[b:b + 1, :S], dtype=np.int32)
        m["cT"] = col8(c[b])
        maps.append(m)
    return maps


def kernel(**inputs):
    S = SEQ
    nc = build(S)
    maps = _layout_inputs(inputs, S)
    res = run_bass_kernel_spmd(nc, maps, core_ids=list(range(8)))
    B = np.asarray(inputs["x"]).shape[0]
    return np.stack([np.asarray(res.results[b]["out"], dtype=np.float32) for b in range(B)], axis=0)
```

```python
import math
from contextlib import ExitStack

import numpy as np
import concourse.bass as bass
import concourse.mybir as mybir
from concourse.bass_utils import run_bass_kernel_spmd

F32 = mybir.dt.float32
BF16 = mybir.dt.bfloat16
I32 = mybir.dt.int32
ALU = mybir.AluOpType
AF = mybir.ActivationFunctionType
AX = mybir.AxisListType

D = 1024
SEQ = 8192
DEPTH = 2
DFF = 4096
INW = 2560
TB = 512
EPS = 1e-6
ROPE_THETA = 500000.0

ENGS = ("pe", "act", "dve", "pool", "sp")
PSUM_PREFIX = ("pa", "pb", "pn", "pg", "pst", "modps", "pbu", "py", "psS", "acc", "pmv", "pnm", "pc", "pd", "pt")


class Prog:
    NDMA = 6

    def __init__(self, nc, sems):
        self.nc = nc
        self.sems = sems
        self.cnt = {e: 0 for e in ENGS}
        self.dcnt = {}
        self.drot = {e: 0 for e in ENGS}
        self.reset()

    def reset(self):
        self.ops = {e: [] for e in ENGS}
        self.last_w = {}
        self.readers = {}

    def add(self, eng, fn, r=(), w=(), dma=False):
        deps = []
        for x in r:
            if x in self.last_w:
                deps.append(self.last_w[x])
            if x.startswith(PSUM_PREFIX):
                deps.extend(o for o in self.readers.get(x, ()) if o["eng"] != eng)
        for x in w:
            if x in self.last_w:
                deps.append(self.last_w[x])
            deps.extend(self.readers.get(x, ()))
        op = {"fn": fn, "deps": [], "dma": dma, "sig": False, "eng": eng, "val": None, "sem": None}
        for d in deps:
            if d is op:
                continue
            if d["eng"] == eng and not d["dma"] and not dma and eng == "pe":
                continue
            if not any(d is x for x in op["deps"]):
                op["deps"].append(d)
                d["sig"] = True
        self.ops[eng].append(op)
        for x in r:
            self.readers.setdefault(x, []).append(op)
        for x in w:
            self.last_w[x] = op
            self.readers[x] = []
        return op

    def emit(self, block):
        nc = self.nc
        for e in ENGS:
            for op in self.ops[e]:
                if op["dma"]:
                    k = ("dma", e, self.drot[e] % self.NDMA)
                    self.drot[e] += 1
                    op["sem"] = k
                    op["prev"] = self.dcnt.get(k, 0)
                    self.dcnt[k] = op["prev"] + 16
                    op["val"] = self.dcnt[k]
                elif op["sig"]:
                    self.cnt[e] += 1
                    op["sem"] = e
                    op["val"] = self.cnt[e]
        final_dma = dict(self.dcnt)
        sems = self.sems
        ops = self.ops

        def run(e, eng):
            waited = {}
            for op in ops[e]:
                need = {}
                for d in op["deps"]:
                    k = d["sem"]
                    if d["val"] > need.get(k, 0):
                        need[k] = d["val"]
                if op["dma"] and op["prev"] > 0:
                    k = op["sem"]
                    need[k] = max(need.get(k, 0), op["prev"])
                for k, v in need.items():
                    if waited.get(k, 0) < v:
                        eng.wait_ge(sems[k], v)
                        waited[k] = v
                inst = op["fn"](eng)
                if op["dma"]:
                    inst.then_inc(sems[op["sem"]], 16)
                elif op["sig"]:
                    inst.then_inc(sems[op["sem"]], 1)
            if e == "sp":
                for k, v in final_dma.items():
                    if v > 0 and waited.get(k, 0) < v:
                        eng.wait_ge(sems[k], v)

        for e, starter in (("pe", block.tensor), ("act", block.scalar), ("dve", block.vector), ("pool", block.gpsimd), ("sp", block.sync)):
            if ops[e] or e == "sp":
                starter(lambda eng, e=e: run(e, eng))

        self.reset()

    def dma(self, out, in_, r=(), w=(), q="sp", **kw):
        return self.add(q, lambda eng: eng.dma_start(out=out, in_=in_, **kw), r=r, w=w, dma=True)

    def mm(self, out, lhsT, rhs, start=True, stop=True, r=(), w=(), **kw):
        return self.add("pe", lambda eng: eng.matmul(out, lhsT, rhs, start=start, stop=stop, **kw), r=r, w=w)

    def tr(self, out, in_, ident, r=(), w=()):
        return self.add("pe", lambda eng: eng.transpose(out, in_, ident), r=r, w=w)

    def act(self, out, in_, func, r=(), w=(), **kw):
        return self.add("act", lambda eng: eng.activation(out=out, in_=in_, func=func, **kw), r=r, w=w)

    def tt(self, out, in0, in1, op, r=(), w=(), eng="dve"):
        return self.add(eng, lambda e: e.tensor_tensor(out=out, in0=in0, in1=in1, op=op), r=r, w=w)

    def ts(self, out, in0, s1, s2, op0, op1=None, r=(), w=(), eng="dve", **kw):
        if op1 is None:
            return self.add(eng, lambda e: e.tensor_scalar(out=out, in0=in0, scalar1=s1, scalar2=None, op0=op0, **kw), r=r, w=w)
        return self.add(eng, lambda e: e.tensor_scalar(out=out, in0=in0, scalar1=s1, scalar2=s2, op0=op0, op1=op1, **kw), r=r, w=w)

    def stt(self, out, in0, scalar, in1, op0, op1, r=(), w=()):
        return self.add("dve", lambda e: e.scalar_tensor_tensor(out=out, in0=in0, scalar=scalar, in1=in1, op0=op0, op1=op1), r=r, w=w)

    def copy(self, out, in_, r=(), w=(), eng="dve"):
        if eng == "act":
            return self.add("act", lambda e: e.copy(out=out, in_=in_), r=r, w=w)
        return self.add(eng, lambda e: e.tensor_copy(out=out, in_=in_), r=r, w=w)

    def memset(self, ap, val, r=(), w=(), eng="dve"):
        return self.add(eng, lambda e: e.memset(ap, val), r=r, w=w)


PI = math.pi
SKIP = set()
TWO_PI = 2.0 * math.pi
CW1 = 6.28125
CW2 = TWO_PI - 6.28125
PI_LO = 3.141592


def sincos(P, ang, n, out_sin, out_cos, t1, t2, t3, ti, tag, rsin, rcos, np_=128):
    a = lambda nm: tag + nm
    sl = lambda t: t[0:np_, 0:n]
    P.ts(sl(t1), ang, 1.0 / TWO_PI, None, ALU.mult, r=[a("ang")], w=[a("t1")])
    P.copy(sl(ti), sl(t1), r=[a("t1")], w=[a("ti")])
    P.copy(sl(t1), sl(ti), r=[a("ti")], w=[a("t1")])
    P.stt(sl(t2), sl(t1), -CW1, ang, ALU.mult, ALU.add, r=[a("t1"), a("ang")], w=[a("t2")])
    P.stt(sl(t3), sl(t1), -CW2, sl(t2), ALU.mult, ALU.add, r=[a("t1"), a("t2")], w=[a("t3")])
    P.ts(sl(t1), sl(t3), PI, -TWO_PI, ALU.is_gt, ALU.mult, r=[a("t3")], w=[a("t1")])
    P.tt(sl(t2), sl(t3), sl(t1), ALU.add, r=[a("t3"), a("t1")], w=[a("t2")])
    P.ts(sl(t1), sl(t2), -PI, TWO_PI, ALU.is_lt, ALU.mult, r=[a("t2")], w=[a("t1")])
    P.tt(sl(t3), sl(t2), sl(t1), ALU.add, r=[a("t2"), a("t1")], w=[a("t3")])
    P.ts(sl(t1), sl(t3), PI_LO, -PI_LO, ALU.min, ALU.max, r=[a("t3")], w=[a("t1")])
    P.act(out_sin, sl(t1), AF.Sin, r=[a("t1")], w=[rsin])
    P.ts(sl(t2), sl(t3), PI / 2, None, ALU.add, r=[a("t3")], w=[a("t2")])
    P.ts(sl(t1), sl(t2), PI, -TWO_PI, ALU.is_gt, ALU.mult, r=[a("t2"), a("t1")], w=[a("t1")])
    P.tt(sl(t3), sl(t2), sl(t1), ALU.add, r=[a("t2"), a("t1")], w=[a("t3")])
    P.ts(sl(t2), sl(t3), PI_LO, -PI_LO, ALU.min, ALU.max, r=[a("t3")], w=[a("t2")])
    P.act(out_cos, sl(t2), AF.Sin, r=[a("t2")], w=[rcos])


def build(S, dbg=False, nl=DEPTH, stop=None):
    NB = S // TB
    NKT = S // 128
    nc = bass.Bass("TRN2", target_bir_lowering=False)
    _uid = [0]

    def U(n):
        _uid[0] += 1
        return f"{n}_u{_uid[0]}"

    din = lambda n, s, d=F32: nc.dram_tensor(n, list(s), d, kind="ExternalInput").ap()
    dscr = lambda n, s, d=F32: nc.dram_tensor(n, list(s), d, kind=("ExternalOutput" if dbg else "Internal")).ap()
    x_d = din("x", [S, D])
    out_d = nc.dram_tensor("out", [S, D], F32, kind="ExternalOutput").ap()
    pos_d = din("pos", [1, S], I32)
    cT_d = din("cT", [128, 8])
    cst_d = din("cst", [128, 2])
    wada_d = din("w_ada", [DEPTH, D, 6 * D])
    bada_d = din("b_adaT", [DEPTH, 128, 48])
    n1g_d = din("n1gT", [DEPTH, 128, 8])
    n2g_d = din("n2gT", [DEPTH, 128, 8])
    fg_d = din("fgT", [128, 8])
    win_d = din("w_in", [DEPTH, D, INW])
    wout_d = din("w_out", [DEPTH, D, D])
    w1_d = din("mlp_w1", [DEPTH, D, DFF])
    w2_d = din("mlp_w2", [DEPTH, DFF, D])
    sare_d = din("s_are", [DEPTH, 128, 8])
    saim_d = din("s_aim", [DEPTH, 128, 8])
    sldt_d = din("s_ldt", [DEPTH, 128, 8])
    sbre_d = din("s_bre", [DEPTH, 128, 8, 16])
    sbim_d = din("s_bim", [DEPTH, 128, 8, 16])
    scre_d = din("s_cre", [DEPTH, 128, 8, 16])
    scim_d = din("s_cim", [DEPTH, 128, 8, 16])
    sd_d = din("s_d", [DEPTH, 128, 2])
    sgw_d = din("s_gw", [DEPTH, 256, 256])
    sgb_d = din("s_gb", [DEPTH, 128, 2])
    sng_d = din("s_ng", [DEPTH, 128, 2])
    lam_d = din("lamv", [DEPTH, 256])
    dsg_d = din("dsg", [DEPTH, 128, 1])
    mng_d = din("mng", [DEPTH, 64, 1])
    xT_d = dscr("xT", [D, S])
    cosT_d = dscr("cosT", [128, S])
    sinT_d = dscr("sinT", [128, S])
    uT_d = dscr("uT", [256, S], BF16)
    dqT_d = dscr("dqT", [4, 128, S], BF16)
    dkT_d = dscr("dkT", [4, 128, S], BF16)
    dV_d = dscr("dV", [S, 512], BF16)
    mqT_d = dscr("mqT", [4, 96, S], BF16)
    mkT_d = dscr("mkT", [4, 64, S], BF16)
    mV_d = dscr("mV", [S, 256], BF16)
    mixT_d = dscr("mixT", [D, S], BF16)
    dbgk_d = dscr("dbgk", [128, 32]); dbgg_d = dscr("dbgg", [128, 32]); dbgm_d = dscr("dbgm", [128, 8]); dbgs_d = dscr("dbgs", [128, 32])

    with ExitStack() as top:
        sems = {}
        for e in ENGS:
            sems[e] = top.enter_context(nc.semaphore("s_" + e))
            for i in range(Prog.NDMA):
                sems[("dma", e, i)] = top.enter_context(nc.semaphore(f"d_{e}_{i}"))
        P = Prog(nc, sems)
        gsb = lambda n, s, d=F32: top.enter_context(nc.sbuf_tensor(U(n), list(s), d))
        ident = gsb("ident", [128, 128])
        identb = gsb("identb", [128, 128], BF16)
        onesb = gsb("onesb", [128, 128], BF16)
        tri = gsb("tri", [128, 128], BF16)
        pswap = gsb("pswap", [128, 128], BF16)
        epsc = gsb("epsc", [128, 1])
        cst = gsb("cstc", [128, 2])
        modv = gsb("modv", [128, DEPTH, 48])
        A1 = gsb("A1", [128, DEPTH, 8])
        A2 = gsb("A2", [128, DEPTH, 8])
        fgT = gsb("fgTs", [128, 8])

        with ExitStack() as es:
            sb = lambda n, s, d=F32: es.enter_context(nc.sbuf_tensor(U(n), list(s), d))
            onesf = sb("onesf", [128, 128])
            b1 = sb("b1", [128, 128])
            b2 = sb("b2", [128, 128])
            cT = sb("cTs", [128, 8])
            scT = sb("scT", [128, 8])
            tmp8 = sb("tmp8", [128, 8])
            wa = [sb(f"wa{i}", [128, 8, 512]) for i in range(2)]
            bada = sb("bada", [128, DEPTH, 48])
            ng1 = sb("ng1", [128, DEPTH, 8])
            ng2 = sb("ng2", [128, DEPTH, 8])
            posi = [sb(f"posi{i}", [128, TB], I32) for i in range(2)]
            ang = sb("ang", [128, TB])
            t1 = sb("t1", [128, TB]); t2 = sb("t2", [128, TB]); t3 = sb("t3", [128, TB])
            ti = sb("ti", [128, TB], I32)
            sn = [sb(f"sn{i}", [128, TB]) for i in range(2)]
            cs = [sb(f"cs{i}", [128, TB]) for i in range(2)]
            modps = es.enter_context(nc.psum_tensor(U("modps"), [128, DEPTH * 48], F32))
            with nc.Block() as block:
                P.memset(onesf[:], 1.0, w=["onesf"], eng="pool")
                P.memset(onesb[:], 1.0, w=["onesb"], eng="pool")
                P.memset(epsc[:], EPS, w=["epsc"], eng="pool")
                P.memset(pswap[:], 0.0, w=["pswap"], eng="pool")
                P.add("pool", lambda e: e.affine_select(out=ident[:], in_=onesf[:], pattern=[[-1, 128]], compare_op=ALU.is_equal, fill=0.0, base=0, channel_multiplier=1), r=["onesf"], w=["ident"])
                P.copy(identb[:], ident[:], r=["ident"], w=["identb"], eng="pool")
                P.add("pool", lambda e: e.affine_select(out=tri[:], in_=onesb[:], pattern=[[1, 128]], compare_op=ALU.is_ge, fill=0.0, base=0, channel_multiplier=-1), r=["onesb"], w=["tri"])
                P.add("pool", lambda e: e.affine_select(out=b1[:], in_=onesf[:], pattern=[[-1, 128]], compare_op=ALU.is_equal, fill=0.0, base=-8, channel_multiplier=1), r=["onesf"], w=["b1"])
                P.add("pool", lambda e: e.affine_select(out=b2[:], in_=onesf[:], pattern=[[-1, 128]], compare_op=ALU.is_equal, fill=0.0, base=8, channel_multiplier=1), r=["onesf"], w=["b2"])
                for s0 in (0, 64):
                    P.copy(pswap[:, s0:s0 + 8], b1[:, s0:s0 + 8], r=["b1", "pswap"], w=["pswap"], eng="pool")
                    P.copy(pswap[:, s0 + 8:s0 + 16], b2[:, s0 + 8:s0 + 16], r=["b2", "pswap"], w=["pswap"], eng="pool")
                P.dma(cT[:], cT_d, w=["cT"])
                P.dma(cst[:], cst_d, w=["cst"])
                P.dma(bada[:], bada_d.rearrange("l p j -> p l j"), w=["bada"])
                P.dma(ng1[:], n1g_d.rearrange("l p j -> p l j"), w=["ng1"])
                P.dma(ng2[:], n2g_d.rearrange("l p j -> p l j"), w=["ng2"])
                P.dma(fgT[:], fg_d, w=["fgT"])
                P.act(tmp8[:], cT[:], AF.Exp, r=["cT"], w=["tmp8"], scale=-1.0)
                P.ts(tmp8[:], tmp8[:], 1.0, None, ALU.add, r=["tmp8"], w=["tmp8"])
                P.add("dve", lambda e: e.reciprocal(out=scT[:], in_=tmp8[:]), r=["tmp8"], w=["scT"])
                P.tt(scT[:], scT[:], cT[:], ALU.mult, r=["scT", "cT"], w=["scT"])
                k = 0
                for l in range(DEPTH):
                    wv = wada_d[l].rearrange("(kt p) n -> p kt n", p=128)
                    for cb in range(12):
                        wb_ = wa[k % 2]; wn = f"wa{k % 2}"; k += 1
                        P.dma(wb_[:], wv[:, :, cb * 512:(cb + 1) * 512], w=[wn])
                        for j in range(4):
                            col = l * 48 + cb * 4 + j
                            for kt in range(8):
                                P.mm(modps[:, col:col + 1], wb_[:, kt, j * 128:(j + 1) * 128], scT[:, kt:kt + 1],
                                     start=(kt == 0), stop=(kt == 7), r=[wn, "scT"], w=["modps"])
                P.tt(modv[:].rearrange("p l j -> p (l j)"), modps[:], bada[:].rearrange("p l j -> p (l j)"), ALU.add, r=["modps", "bada"], w=["modv"])
                for l in range(DEPTH):
                    P.stt(A1[:, l, :], modv[:, l, 8:16], 1.0, ng1[:, l, :], ALU.add, ALU.mult, r=["modv", "ng1"], w=["A1"])
                    P.stt(A2[:, l, :], modv[:, l, 32:40], 1.0, ng2[:, l, :], ALU.add, ALU.mult, r=["modv", "ng2"], w=["A2"])
                for i in range(NB):
                    pi_ = posi[i % 2]; pn_ = f"posi{i % 2}"
                    P.dma(pi_[:], pos_d[0:1, i * TB:(i + 1) * TB].to_broadcast([128, TB]), w=[pn_])
                    P.copy(t1[:], pi_[:], r=[pn_], w=["rt1"])
                    P.ts(ang[:], t1[:], cst[:, 0:1], None, ALU.mult, r=["rt1", "cst"], w=["rang"])
                    sincos(P, ang[:], TB, sn[i % 2][:], cs[i % 2][:], t1, t2, t3, ti, "r", f"sn{i % 2}", f"cs{i % 2}")
                    P.ts(sn[i % 2][:], sn[i % 2][:], cst[:, 1:2], None, ALU.mult, r=[f"sn{i % 2}", "cst"], w=[f"sn{i % 2}"])
                    P.dma(sinT_d[:, i * TB:(i + 1) * TB], sn[i % 2][:], r=[f"sn{i % 2}"], w=["sinT"])
                    P.dma(cosT_d[:, i * TB:(i + 1) * TB], cs[i % 2][:], r=[f"cs{i % 2}"], w=["cosT"])
                P.emit(block)

        if stop == "0":
            return nc
        with ExitStack() as es:
            sb = lambda n, s, d=F32: es.enter_context(nc.sbuf_tensor(U(n), list(s), d))
            xin = [sb(f"xin{i}", [128, D]) for i in range(3)]
            xo = [sb(f"xo{i}", [128, 8, TB]) for i in range(2)]
            pst = [es.enter_context(nc.psum_tensor(U(f"pst{i}"), [128, TB], F32)) for i in range(8)]
            with nc.Block() as block:
                k = 0
                for i in range(NB):
                    o_ = xo[i % 2]; on = f"xo{i % 2}"
                    for tt in range(4):
                        xi = xin[k % 3]; xn = f"xin{k % 3}"; k += 1
                        P.dma(xi[:], x_d[i * TB + tt * 128:i * TB + (tt + 1) * 128, :], w=[xn])
                        for ft in range(8):
                            P.tr(pst[ft][:, tt * 128:(tt + 1) * 128], xi[:, ft * 128:(ft + 1) * 128], ident[:], r=[xn], w=[f"pst{ft}"])
                    for ft in range(8):
                        P.copy(o_[:, ft, :], pst[ft][:], r=[f"pst{ft}"], w=[f"{on}_{ft}"], eng=("act" if ft % 2 else "dve"))
                    P.dma(xT_d.rearrange("(ft p) t -> p ft t", p=128)[:, :, i * TB:(i + 1) * TB], o_[:], r=[f"{on}_{ft}" for ft in range(8)], w=["xTd"])
                P.emit(block)

        if stop == "T0":
            return nc
        xTv = xT_d.rearrange("(ft p) t -> p ft t", p=128)

        def load_cast(P, dst3, src2, nk, ncol, name, q="pool"):
            sv = src2.rearrange("(kt p) n -> p kt n", p=128)
            step = 2048
            for kt in range(nk):
                for c0 in range(0, ncol, step):
                    c1 = min(ncol, c0 + step)
                    P.dma(dst3[:, kt, c0:c1], sv[:, kt, c0:c1], w=[name], q=q)

        def norm_mod(P, xb, xn, hT, hn, A, B, l, pn, tmps, sqs, rstd, lnv, pool_share=True):
            for kt in range(8):
                sq = sqs[kt % 2]; sqn = f"sq{kt % 2}"
                P.act(sq[:], xb[:, kt, :], AF.Square, r=[xn], w=[sqn])
                P.mm(pn[:], onesb[:], sq[:], start=(kt == 0), stop=(kt == 7), r=[sqn, "onesb"], w=["pn"])
            P.act(lnv[:], pn[:], AF.Ln, r=["pn", "epsc"], w=["lnv"], scale=1.0 / D, bias=epsc[:, 0:1])
            P.act(rstd[:], lnv[:], AF.Exp, r=["lnv"], w=["rstd"], scale=-0.5)
            for kt in range(8):
                tm = tmps[kt % 2]; tn = f"nt{kt % 2}"
                P.tt(tm[:], xb[:, kt, :], rstd[:], ALU.mult, r=[xn, "rstd"], w=[tn], eng=("pool" if (pool_share and kt % 2) else "dve"))
                P.act(hT[:, kt, :], tm[:], AF.Identity, r=[tn, "A", "modv"], w=[f"{hn}_{kt}"], scale=A[:, l, kt:kt + 1], bias=B(kt))

        for l in range(nl):
            with ExitStack() as es:
                sb = lambda n, s, d=F32: es.enter_context(nc.sbuf_tensor(U(n), list(s), d))
                win = sb("win", [128, 8, INW], BF16)
                xbs = [sb(f"xb{i}", [128, 8, TB]) for i in range(2)]
                hTs = [sb(f"hT{i}", [128, 8, TB], BF16) for i in range(2)]
                sqs = [sb(f"sq{i}", [128, TB], BF16) for i in range(2)]
                tmps = [sb(f"nt{i}", [128, TB]) for i in range(2)]
                rstd = sb("rstd", [128, TB]); lnv = sb("lnv", [128, TB])
                cosb = [sb(f"cosb{i}", [128, TB]) for i in range(2)]
                sinb = [sb(f"sinb{i}", [128, TB]) for i in range(2)]
                qb = [sb(f"qb{i}", [128, TB], BF16) for i in range(2)]
                r1 = [sb(f"r1_{i}", [128, TB]) for i in range(2)]
                r2 = [sb(f"r2_{i}", [128, TB]) for i in range(2)]
                rf = [sb(f"rf{i}", [128, TB]) for i in range(2)]
                stg = [sb(f"stg{i}", [128, TB], BF16) for i in range(4)]
                vst = [sb(f"vst{i}", [128, 768], BF16) for i in range(2)]
                kmT = [sb(f"kmT{j}", [128, 32]) for j in range(2)]
                gm8 = [sb(f"gm8_{j}", [128, 8, 32]) for j in range(2)]
                m8a = [sb(f"m8a{j}", [128, 8, 8]) for j in range(2)]
                sel8 = [sb(f"sel8_{j}", [128, 8, 32]) for j in range(2)]
                neg8 = [sb(f"neg8_{j}", [128, 8, 32], BF16) for j in range(2)]
                nst = [sb(f"nst{i}", [32, TB], BF16) for i in range(4)]
                NPA = 3
                pa = [es.enter_context(nc.psum_tensor(U(f"pa{i}"), [128, TB], F32)) for i in range(NPA)]
                pb = [es.enter_context(nc.psum_tensor(U(f"pb{i}"), [128, TB], F32)) for i in range(2)]
                pn = es.enter_context(nc.psum_tensor(U("pn"), [128, TB], F32))
                pgs = [es.enter_context(nc.psum_tensor(U(f"pg{i}"), [128, TB], F32)) for i in range(2)]
                with nc.Block() as block:
                    load_cast(P, win, win_d[l], 8, INW, "win")
                    for j in range(2):
                        P.memset(kmT[j][:], 0.0, w=[f"kmT{j}"], eng="pool")
                    cnt = {"pa": 0, "pb": 0, "stg": 0, "qb": 0, "r": 0, "v": 0, "ns": 0}

                    def load_blk(i):
                        P.dma(xbs[i % 2][:], xTv[:, :, i * TB:(i + 1) * TB], w=[f"xb{i % 2}"])
                        P.dma(cosb[i % 2][:], cosT_d[:, i * TB:(i + 1) * TB], w=[f"cosb{i % 2}"])
                        P.dma(sinb[i % 2][:], sinT_d[:, i * TB:(i + 1) * TB], w=[f"sinb{i % 2}"])

                    load_blk(0)
                    for i in range(NB):
                        t0 = i * TB
                        if i + 1 < NB:
                            load_blk(i + 1)
                        xb = xbs[i % 2]; xn = f"xb{i % 2}"; hT = hTs[i % 2]; hn = f"hT{i % 2}"
                        cb_, cbn = cosb[i % 2], f"cosb{i % 2}"
                        sb_, sbn = sinb[i % 2], f"sinb{i % 2}"
                        norm_mod(P, xb, xn, hT, hn, A1, lambda kt: modv[:, l, kt:kt + 1], l, pn, tmps, sqs, rstd, lnv)
                        hres = [f"{hn}_{kt}" for kt in range(8)]

                        def proj(c0):
                            p_ = pa[cnt["pa"] % NPA]; pn_ = f"pa{cnt['pa'] % NPA}"; cnt["pa"] += 1
                            for kt in range(8):
                                P.mm(p_[:], win[:, kt, c0:c0 + 128], hT[:, kt, :], start=(kt == 0), stop=(kt == 7), r=["win", hres[kt]], w=[pn_])
                            return p_, pn_

                        RENG = "dve" if "nopool" in SKIP else "pool"

                        def rope(p_, pn_, want_f32):
                            q_ = qb[cnt["qb"] % 2]; qn = f"qb{cnt['qb'] % 2}"; cnt["qb"] += 1
                            P.copy(q_[:], p_[:], r=[pn_], w=[qn], eng="act")
                            s_ = pb[cnt["pb"] % 2]; sn_ = f"pb{cnt['pb'] % 2}"; cnt["pb"] += 1
                            if "nosw" not in SKIP:
                                P.mm(s_[:], pswap[:], q_[:], r=[qn, "pswap"], w=[sn_])
                            else:
                                s_, sn_ = p_, pn_
                            k_ = cnt["r"] % 2; cnt["r"] += 1
                            P.tt(r1[k_][:], p_[:], cb_[:], ALU.mult, r=[pn_, cbn, qn], w=[f"r1_{k_}"])
                            P.tt(r2[k_][:], s_[:], sb_[:], ALU.mult, r=[sn_, sbn], w=[f"r2_{k_}"])
                            g_ = stg[cnt["stg"] % 4]; gn = f"stg{cnt['stg'] % 4}"; cnt["stg"] += 1
                            if want_f32:
                                P.tt(rf[k_][:], r1[k_][:], r2[k_][:], ALU.add, r=[f"r1_{k_}", f"r2_{k_}"], w=[f"rf{k_}"], eng=RENG)
                                P.copy(g_[:], rf[k_][:], r=[f"rf{k_}"], w=[gn], eng="act")
                                return rf[k_], f"rf{k_}", g_, gn
                            P.tt(g_[:], r1[k_][:], r2[k_][:], ALU.add, r=[f"r1_{k_}", f"r2_{k_}"], w=[gn], eng=RENG)
                            return None, None, g_, gn

                        for j in range(2):
                            p_, pn_ = proj(j * 128)
                            g_ = stg[cnt["stg"] % 4]; gn = f"stg{cnt['stg'] % 4}"; cnt["stg"] += 1
                            P.copy(g_[:], p_[:], r=[pn_], w=[gn], eng="act")
                            P.dma(uT_d[j * 128:(j + 1) * 128, t0:t0 + TB], g_[:], r=[gn], w=["uTd"])
                        for h in range(0 if "rope" in SKIP else 4):
                            p_, pn_ = proj(256 + h * 128)
                            _, _, g_, gn = rope(p_, pn_, False)
                            if "nodma" not in SKIP:
                                P.dma(dqT_d[h, :, t0:t0 + TB], g_[:], r=[gn], w=["dqTd"])
                        for h in range(0 if "rope" in SKIP else 4):
                            p_, pn_ = proj(768 + h * 128)
                            _, _, g_, gn = rope(p_, pn_, False)
                            if "nodma" not in SKIP:
                                P.dma(dkT_d[h, :, t0:t0 + TB], g_[:], r=[gn], w=["dkTd"])
                        for j in range(0 if "mk" in SKIP else 2):
                            p_, pn_ = proj(2048 + j * 128)
                            f_, fn_, g_, gn = rope(p_, pn_, True)
                            P.add("dve", lambda e, f_=f_, j=j, i=i: e.tensor_reduce(out=kmT[j][:, 2 * i:2 * i + 2], in_=f_[:].rearrange("p (b k) -> p b k", k=256), axis=AX.X, op=ALU.add),
                                  r=[fn_], w=[f"kmT{j}"])
                            P.ts(kmT[j][:, 2 * i:2 * i + 2], kmT[j][:, 2 * i:2 * i + 2], 1.0 / 256, None, ALU.mult, r=[f"kmT{j}"], w=[f"kmT{j}"])
                            for hh in range(2):
                                P.dma(mkT_d[2 * j + hh, :, t0:t0 + TB], g_[hh * 64:(hh + 1) * 64, :], r=[gn], w=["mkTd"])
                        pend = []
                        for j in range(0 if "mq" in SKIP else 2):
                            p_, pn_ = proj(1792 + j * 128)
                            f_, fn_, g_, gn = rope(p_, pn_, True)
                            for hh in range(2):
                                P.dma(mqT_d[2 * j + hh, 0:64, t0:t0 + TB], g_[hh * 64:(hh + 1) * 64, :], r=[gn], w=["mqTd"])
                            for hh in range(0 if "nogate" in SKIP else 2):
                                hs = slice(hh * 64, (hh + 1) * 64)
                                for tt in range(4):
                                    gcol = j * 128 + tt * 32
                                    P.mm(pgs[hh][:, gcol:gcol + 32], f_[hs, tt * 128:(tt + 1) * 128], kmT[j][hs, :], r=[fn_, f"kmT{j}"], w=[f"pg{hh}"])
                            gm_ = gm8[j]; gmn = f"gm8_{j}"
                            P.memset(gm_[:], -1e30, w=[gmn])
                            for hh in range(2):
                                for tt in range(4):
                                    k8 = hh * 4 + tt
                                    own = (t0 + tt * 128) // 256
                                    gcol = j * 128 + tt * 32
                                    if own > 0:
                                        P.copy(gm_[:, k8, 0:own], pgs[hh][:, gcol:gcol + own], r=[f"pg{hh}", gmn], w=[gmn + f"_{k8}"])
                            for hh in range(0 if "nochain" in SKIP else 2):
                                for tt in range(4):
                                    k8 = hh * 4 + tt
                                    own = (t0 + tt * 128) // 256
                                    P.add("dve", lambda e, k8=k8, j=j: e.max(out=m8a[j][:, k8, :], in_=gm8[j][:, k8, :]), r=[gmn, gmn + f"_{k8}"], w=[f"m8a{j}_{k8}"])
                                    P.tt(sel8[j][:, k8, :], gm_[:, k8, :], m8a[j][:, k8, 2:3].to_broadcast([128, 32]), ALU.is_ge, r=[gmn + f"_{k8}", f"m8a{j}_{k8}"], w=[f"sel8_{j}_{k8}"])
                                    P.ts(neg8[j][:, k8, :], sel8[j][:, k8, :], 30000.0, -30000.0, ALU.mult, ALU.add, r=[f"sel8_{j}_{k8}"], w=[f"neg8_{j}_{k8}"])
                                    P.memset(neg8[j][:, k8, own:own + 1], 0.0, r=[f"neg8_{j}_{k8}"], w=[f"neg8_{j}_{k8}"])
                            pend.append(j)
                        for tt in range(0 if "v" in SKIP else 4):
                            v_ = vst[cnt["v"] % 2]; vn = f"vst{cnt['v'] % 2}"; cnt["v"] += 1
                            p_ = pa[cnt["pa"] % NPA]; pn_ = f"pa{cnt['pa'] % NPA}"; cnt["pa"] += 1
                            for kt in range(8):
                                P.mm(p_[:], hT[:, kt, tt * 128:(tt + 1) * 128], win[:, kt, 1280:1792], start=(kt == 0), stop=(kt == 7), r=["win", hres[kt]], w=[pn_])
                            P.copy(v_[:, 0:512], p_[:], r=[pn_], w=[vn + "a"], eng="act")
                            p2 = pa[cnt["pa"] % NPA]; pn2 = f"pa{cnt['pa'] % NPA}"; cnt["pa"] += 1
                            for kt in range(8):
                                P.mm(p2[:, 0:256], hT[:, kt, tt * 128:(tt + 1) * 128], win[:, kt, 2304:2560], start=(kt == 0), stop=(kt == 7), r=["win", hres[kt]], w=[pn2])
                            P.copy(v_[:, 512:768], p2[:, 0:256], r=[pn2], w=[vn + "b"], eng=("act" if "vact" in SKIP else "dve"))
                            P.dma(dV_d[t0 + tt * 128:t0 + (tt + 1) * 128, :], v_[:, 0:512], r=[vn + "a"], w=["dVd"])
                            P.dma(mV_d[t0 + tt * 128:t0 + (tt + 1) * 128, :], v_[:, 512:768], r=[vn + "b"], w=["mVd"])
                        for j in ([] if "notr" in SKIP else pend):
                            for hh in range(2):
                                pgb = pgs[hh][:].bitcast(BF16)
                                for tt in range(4):
                                    tcol = 512 + tt * 128
                                    P.tr(pgb[0:32, tcol:tcol + 128], neg8[j][:, hh * 4 + tt, :], identb[:], r=[f"neg8_{j}_{hh * 4 + tt}", "identb"], w=[f"pg{hh}"])
                            for hh in range(2):
                                pgb = pgs[hh][:].bitcast(BF16)
                                ns_ = nst[cnt["ns"] % 4]; nsn = f"nst{cnt['ns'] % 4}"; cnt["ns"] += 1
                                P.copy(ns_[:], pgb[0:32, 512:1024], r=[f"pg{hh}"], w=[nsn])
                                P.dma(mqT_d[2 * j + hh, 64:96, t0:t0 + TB], ns_[:], r=[nsn], w=["mqTd"])
                    P.emit(block)

            if stop == "A":
                return nc
            with ExitStack() as es:
                sb = lambda n, s, d=F32: es.enter_context(nc.sbuf_tensor(U(n), list(s), d))
                uTs = [sb(f"uTs{h}", [128, S], BF16) for h in range(2)]
                are = sb("are", [128, 8]); aim = sb("aim", [128, 8]); ldt = sb("ldt", [128, 8])
                dt_ = sb("dt_", [128, 8]); mag = sb("mag", [128, 8]); th = sb("th", [128, 8])
                sth = sb("sth", [128, 8]); cth = sb("cth", [128, 8])
                s1 = sb("s1", [128, 8]); s2 = sb("s2", [128, 8]); s3 = sb("s3", [128, 8]); si = sb("si", [128, 8], I32)
                abr = sb("abr", [128, 8]); abi = sb("abi", [128, 8]); rden = sb("rden", [128, 8])
                kr = sb("kr", [128, 8]); ki = sb("ki", [128, 8]); nki = sb("nki", [128, 8])
                bre = sb("bre", [128, 8, 16]); bim = sb("bim", [128, 8, 16])
                cre = sb("cre", [128, 8, 16]); cim = sb("cim", [128, 8, 16])
                bbr = sb("bbr", [128, 8, 16]); bbi = sb("bbi", [128, 8, 16]); tma = sb("tma", [128, 16])
                X = [sb(f"X{i}", [128, 128]) for i in range(2)]
                BD = sb("BD", [128, 16, 128], BF16)
                CDm = sb("CD", [128, 16, 128], BF16)
                sdv = sb("sdv", [128, 2]); sgb = sb("sgb", [128, 2]); nsgb = sb("nsgb", [128, 2]); sng = sb("sng", [128, 2])
                gw = sb("gw", [128, 2, 256], BF16)
                jidi = sb("jidi", [128, TB], I32); jid = sb("jid", [128, TB])
                tA = sb("tA", [128, TB]); q1 = sb("q1", [128, TB]); q2 = sb("q2", [128, TB]); q3 = sb("q3", [128, TB]); qi = sb("qi", [128, TB], I32)
                Cs = sb("Cs", [128, 8, TB]); Sn = sb("Sn", [128, 8, TB])
                Hre = sb("Hre", [128, 8]); Him = sb("Him", [128, 8])
                NW = 2
                mt = [[sb(f"m{k}_{w}", [128, TB]) for k in range(4)] for w in range(NW)]
                bp = [[sb(f"bp{k}_{w}", [128, TB]) for k in range(2)] for w in range(NW)]
                gg = [[sb(f"g{k}_{w}", [128, TB]) for k in range(2)] for w in range(NW)]
                pp = [[sb(f"p{k}_{w}", [128, TB]) for k in range(4)] for w in range(NW)]
                hb = [[sb(f"hb{k}_{w}", [128, TB], BF16) for k in range(2)] for w in range(NW)]
                yv = [sb(f"yv{h}", [128, TB]) for h in range(2)]
                ya = sb("ya", [128, TB]); yb = sb("yb", [128, TB]); yc = sb("yc", [128, TB])
                gl = [sb(f"gl{h}", [128, TB]) for h in range(2)]
                glb = [sb(f"glb{h}", [128, TB], BF16) for h in range(2)]
                zz = [sb(f"zz{h}", [128, TB]) for h in range(2)]
                zsq = [sb(f"zsq{h}", [128, TB], BF16) for h in range(2)]
                rs_ = sb("rs_", [128, TB]); ln_ = sb("ln_", [128, TB])
                mo = [sb(f"mo{h}", [128, TB], BF16) for h in range(2)]
                pbu = [es.enter_context(nc.psum_tensor(U(f"pbu{i}"), [128, TB], F32)) for i in range(4)]
                py = [es.enter_context(nc.psum_tensor(U(f"py{i}"), [128, TB], F32)) for i in range(2)]
                pgx = es.enter_context(nc.psum_tensor(U("pgx"), [128, TB], F32))
                pn2 = es.enter_context(nc.psum_tensor(U("pn2"), [128, TB], F32))
                with nc.Block() as block:
                    for h in range(2):
                        P.dma(uTs[h][:], uT_d[h * 128:(h + 1) * 128, :], w=[f"uTs{h}"])
                    for t_, d_, n_ in ((are, sare_d, "are"), (aim, saim_d, "aim"), (ldt, sldt_d, "ldt"), (bre, sbre_d, "bre"), (bim, sbim_d, "bim"),
                                       (cre, scre_d, "cre"), (cim, scim_d, "cim"), (sdv, sd_d, "sdv"), (sgb, sgb_d, "sgb"), (sng, sng_d, "sng")):
                        P.dma(t_[:], d_[l], w=[n_])
                    load_cast(P, gw, sgw_d[l], 2, 256, "gw")
                    P.act(dt_[:], ldt[:], AF.Exp, r=["ldt"], w=["dt_"])
                    P.tt(s1[:], are[:], dt_[:], ALU.mult, r=["are", "dt_"], w=["s1"])
                    P.act(mag[:], s1[:], AF.Exp, r=["s1"], w=["mag"])
                    P.tt(th[:], aim[:], dt_[:], ALU.mult, r=["aim", "dt_"], w=["sang"])
                    sincos(P, th[:], 8, sth[:], cth[:], s1, s2, s3, si, "s", "sth", "cth")
                    P.tt(abr[:], mag[:], cth[:], ALU.mult, r=["mag", "cth"], w=["abr"])
                    P.tt(abi[:], mag[:], sth[:], ALU.mult, r=["mag", "sth"], w=["abi"])
                    P.tt(s1[:], are[:], are[:], ALU.mult, r=["are", "st1"], w=["s1b"])
                    P.tt(s2[:], aim[:], aim[:], ALU.mult, r=["aim", "st2"], w=["s2b"])
                    P.tt(s1[:], s1[:], s2[:], ALU.add, r=["s1b", "s2b"], w=["s1b"])
                    P.add("dve", lambda e: e.reciprocal(out=rden[:], in_=s1[:]), r=["s1b"], w=["rden"])
                    P.ts(s3[:], abr[:], -1.0, None, ALU.add, r=["abr", "st3"], w=["nr"])
                    P.tt(s1[:], s3[:], are[:], ALU.mult, r=["nr", "are", "rden"], w=["s1c"])
                    P.tt(s2[:], abi[:], aim[:], ALU.mult, r=["abi", "aim", "s2b"], w=["s2c"])
                    P.tt(s1[:], s1[:], s2[:], ALU.add, r=["s1c", "s2c"], w=["s1c"])
                    P.tt(kr[:], s1[:], rden[:], ALU.mult, r=["s1c", "rden"], w=["kr"])
                    P.tt(s1[:], abi[:], are[:], ALU.mult, r=["abi", "are", "kr"], w=["s1d"])
                    P.tt(s2[:], s3[:], aim[:], ALU.mult, r=["nr", "aim", "s1c"], w=["s2d"])
                    P.tt(s1[:], s1[:], s2[:], ALU.subtract, r=["s1d", "s2d"], w=["s1d"])
                    P.tt(ki[:], s1[:], rden[:], ALU.mult, r=["s1d", "rden"], w=["ki"])
                    P.ts(nki[:], ki[:], -1.0, None, ALU.mult, r=["ki"], w=["nki"])
                    P.ts(nsgb[:], sgb[:], -1.0, None, ALU.mult, r=["sgb"], w=["nsgb"])
                    P.memset(BD[:], 0.0, w=["BD"], eng="pool")
                    P.memset(CDm[:], 0.0, w=["CD"], eng="pool")
                    P.memset(Hre[:], 0.0, w=["Hre"], eng="pool")
                    P.memset(Him[:], 0.0, w=["Him"], eng="pool")
                    for st in range(8):
                        P.ts(tma[:], bre[:, st, :], kr[:, st:st + 1], None, ALU.mult, r=["bre", "kr", "bbr"], w=["tma"])
                        P.stt(bbr[:, st, :], bim[:, st, :], nki[:, st:st + 1], tma[:], ALU.mult, ALU.add, r=["bim", "nki", "tma"], w=["bbr"])
                        P.ts(tma[:], bim[:, st, :], kr[:, st:st + 1], None, ALU.mult, r=["bim", "kr", "bbr"], w=["tma"])
                        P.stt(bbi[:, st, :], bre[:, st, :], ki[:, st:st + 1], tma[:], ALU.mult, ALU.add, r=["bre", "ki", "tma"], w=["bbi"])
                    kx = 0
                    for st in range(8):
                        gl0 = (2 * st) % 8
                        for ri, bsrc, bn in ((0, bbr, "bbr"), (1, bbi, "bbi")):
                            X_ = X[kx % 2]; Xn = f"X{kx % 2}"; kx += 1
                            P.memset(X_[:], 0.0, w=[Xn], eng="pool")
                            P.copy(X_[0:64, gl0 * 16:(gl0 + 1) * 16], bsrc[0:64, st, :], r=[bn, Xn], w=[Xn])
                            P.copy(X_[64:128, (gl0 + 1) * 16:(gl0 + 2) * 16], bsrc[64:128, st, :], r=[bn, Xn], w=[Xn])
                            pt_ = pbu[kx % 4]; ptn = f"pbu{kx % 4}"
                            P.tr(pt_[:, 0:128], X_[:], ident[:], r=[Xn], w=[ptn])
                            P.copy(BD[:, st * 2 + ri, :], pt_[:, 0:128], r=[ptn, "BD"], w=["BD"], eng="act")
                        P.copy(CDm[0:64, st * 2, gl0 * 16:(gl0 + 1) * 16], cre[0:64, st, :], r=["cre", "CD"], w=["CD"])
                        P.copy(CDm[64:128, st * 2, (gl0 + 1) * 16:(gl0 + 2) * 16], cre[64:128, st, :], r=["cre", "CD"], w=["CD"])
                        P.ts(CDm[0:64, st * 2 + 1, gl0 * 16:(gl0 + 1) * 16], cim[0:64, st, :], -1.0, None, ALU.mult, r=["cim", "CD"], w=["CD"])
                        P.ts(CDm[64:128, st * 2 + 1, (gl0 + 1) * 16:(gl0 + 2) * 16], cim[64:128, st, :], -1.0, None, ALU.mult, r=["cim", "CD"], w=["CD"])
                    P.add("pool", lambda e: e.iota(jidi[:], pattern=[[1, TB]], base=1, channel_multiplier=0), w=["jidi"])
                    P.copy(jid[:], jidi[:], r=["jidi"], w=["jid"])
                    for st in range(8):
                        P.ts(tA[:], jid[:], th[:, st:st + 1], None, ALU.mult, r=["jid", "sang", "tsin", "tcos"], w=["tang"])
                        sincos(P, tA[:], TB, Sn[:, st, :], Cs[:, st, :], q1, q2, q3, qi, "t", "Sn", "Cs")
                    it = 0
                    for i in range(NB):
                        t0 = i * TB
                        for st in range(8):
                            w_ = it % NW; it += 1
                            half = st // 4
                            m = mt[w_]; mn = [f"m{k}_{w_}" for k in range(4)]
                            b_ = bp[w_]; bn_ = [f"bp{k}_{w_}" for k in range(2)]
                            g_ = gg[w_]; gn_ = [f"g{k}_{w_}" for k in range(2)]
                            p_ = pp[w_]; pn_ = [f"p{k}_{w_}" for k in range(4)]
                            h_ = hb[w_]; hn_ = [f"hb{k}_{w_}" for k in range(2)]
                            pr, prn = pbu[(2 * it) % 4], f"pbu{(2 * it) % 4}"
                            pi2, pin = pbu[(2 * it + 1) % 4], f"pbu{(2 * it + 1) % 4}"
                            P.mm(pr[:], BD[:, st * 2, :], uTs[half][:, t0:t0 + TB], r=["BD", f"uTs{half}"], w=[prn])
                            P.mm(pi2[:], BD[:, st * 2 + 1, :], uTs[half][:, t0:t0 + TB], r=["BD", f"uTs{half}"], w=[pin])
                            cs_, sn_ = Cs[:, st, :], Sn[:, st, :]
                            P.tt(m[0][:], pr[:], cs_, ALU.mult, r=[prn, "Cs"], w=[mn[0]])
                            P.tt(m[1][:], pi2[:], sn_, ALU.mult, r=[pin, "Sn"], w=[mn[1]])
                            P.tt(m[2][:], pi2[:], cs_, ALU.mult, r=[pin, "Cs"], w=[mn[2]])
                            P.tt(m[3][:], pr[:], sn_, ALU.mult, r=[prn, "Sn"], w=[mn[3]])
                            P.tt(b_[0][:], m[0][:], m[1][:], ALU.add, r=[mn[0], mn[1]], w=[bn_[0]], eng="pool")
                            P.tt(b_[1][:], m[2][:], m[3][:], ALU.subtract, r=[mn[2], mn[3]], w=[bn_[1]], eng="pool")
                            rbc = mag[:, st:st + 1].to_broadcast([128, TB])
                            P.add("dve", lambda e, o=g_[0], d1=b_[0], ini=Hre[:, st:st + 1], rbc=rbc: e.tensor_tensor_scan(out=o[:], data0=rbc, data1=d1[:], initial=ini, op0=ALU.mult, op1=ALU.add),
                                  r=[bn_[0], "mag", "Hre"], w=[gn_[0]])
                            P.add("dve", lambda e, o=g_[1], d1=b_[1], ini=Him[:, st:st + 1], rbc=rbc: e.tensor_tensor_scan(out=o[:], data0=rbc, data1=d1[:], initial=ini, op0=ALU.mult, op1=ALU.add),
                                  r=[bn_[1], "mag", "Him"], w=[gn_[1]])
                            P.tt(p_[0][:], g_[0][:], cs_, ALU.mult, r=[gn_[0], "Cs"], w=[pn_[0]])
                            P.tt(p_[1][:], g_[1][:], sn_, ALU.mult, r=[gn_[1], "Sn"], w=[pn_[1]])
                            P.tt(p_[2][:], g_[0][:], sn_, ALU.mult, r=[gn_[0], "Sn"], w=[pn_[2]], eng="pool")
                            P.tt(p_[3][:], g_[1][:], cs_, ALU.mult, r=[gn_[1], "Cs"], w=[pn_[3]], eng="pool")
                            P.tt(h_[0][:], p_[0][:], p_[1][:], ALU.subtract, r=[pn_[0], pn_[1]], w=[hn_[0]], eng="pool")
                            P.tt(h_[1][:], p_[2][:], p_[3][:], ALU.add, r=[pn_[2], pn_[3]], w=[hn_[1]], eng="pool")
                            P.tt(Hre[:, st:st + 1], p_[0][:, TB - 1:TB], p_[1][:, TB - 1:TB], ALU.subtract, r=[pn_[0], pn_[1], "Hre"], w=["Hre"])
                            P.tt(Him[:, st:st + 1], p_[2][:, TB - 1:TB], p_[3][:, TB - 1:TB], ALU.add, r=[pn_[2], pn_[3], "Him"], w=["Him"])
                            P.mm(py[half][:], CDm[:, st * 2, :], h_[0][:], start=(st % 4 == 0), stop=False, r=["CD", hn_[0]], w=[f"py{half}"])
                            P.mm(py[half][:], CDm[:, st * 2 + 1, :], h_[1][:], start=False, stop=(st % 4 == 3), r=["CD", hn_[1]], w=[f"py{half}"])
                        for h in range(2):
                            P.stt(yv[h][:], uTs[h][:, t0:t0 + TB], sdv[:, h:h + 1], py[h][:], ALU.mult, ALU.add, r=[f"uTs{h}", "sdv", f"py{h}"], w=[f"yv{h}"])
                            P.act(ya[:], yv[h][:], AF.Square, r=[f"yv{h}"], w=["ya"])
                            P.ts(yb[:], ya[:], 0.044715, 1.0, ALU.mult, ALU.add, r=["ya"], w=["yb"])
                            P.tt(yc[:], yb[:], yv[h][:], ALU.mult, r=["yb", f"yv{h}"], w=["yc"], eng="pool")
                            P.act(ya[:], yc[:], AF.Exp, r=["yc"], w=["ya"], scale=-1.5957691216)
                            P.ts(yb[:], ya[:], 1.0, None, ALU.add, r=["ya"], w=["yb"])
                            P.add("dve", lambda e: e.reciprocal(out=yc[:], in_=yb[:]), r=["yb"], w=["yc"])
                            P.tt(gl[h][:], yv[h][:], yc[:], ALU.mult, r=[f"yv{h}", "yc"], w=[f"gl{h}"], eng="pool")
                            P.copy(glb[h][:], gl[h][:], r=[f"gl{h}"], w=[f"glb{h}"], eng="act")
                        for oh in range(2):
                            for kh in range(2):
                                P.mm(pgx[:], gw[:, kh, oh * 128:(oh + 1) * 128], glb[kh][:], start=(kh == 0), stop=(kh == 1), r=["gw", f"glb{kh}"], w=["pgx"])
                            P.act(ya[:], pgx[:], AF.Exp, r=["pgx", "nsgb"], w=["ya"], scale=-1.0, bias=nsgb[:, oh:oh + 1])
                            P.ts(yb[:], ya[:], 1.0, None, ALU.add, r=["ya"], w=["yb"])
                            P.add("dve", lambda e: e.reciprocal(out=yc[:], in_=yb[:]), r=["yb"], w=["yc"])
                            P.tt(zz[oh][:], gl[oh][:], yc[:], ALU.mult, r=[f"gl{oh}", "yc"], w=[f"zz{oh}"], eng="pool")
                            P.act(zsq[oh][:], zz[oh][:], AF.Square, r=[f"zz{oh}"], w=[f"zsq{oh}"])
                            P.mm(pn2[:], onesb[:], zsq[oh][:], start=(oh == 0), stop=(oh == 1), r=[f"zsq{oh}"], w=["pn2"])
                        P.act(ln_[:], pn2[:], AF.Ln, r=["pn2"], w=["ln_"], scale=1.0 / 256, bias=epsc[:, 0:1])
                        P.act(rs_[:], ln_[:], AF.Exp, r=["ln_"], w=["rs_"], scale=-0.5)
                        for oh in range(2):
                            P.stt(mo[oh][:], zz[oh][:], sng[:, oh:oh + 1], rs_[:], ALU.mult, ALU.mult, r=[f"zz{oh}", "sng", "rs_"], w=[f"mo{oh}"])
                            P.dma(mixT_d[oh * 128:(oh + 1) * 128, t0:t0 + TB], mo[oh][:], r=[f"mo{oh}"], w=["mixTd"])
                    P.emit(block)

            if stop == "S":
                return nc
            lam_init = 0.8 - 0.6 * math.exp(-0.3 * l)

            def attention(P, streams, nq_blocks, epilogue, psS, psAcc, Pt, cnt, acc_sets=1):
                ns = len(streams)
                LA = max(1, len(psS) // ns - 1)
                for qb_ in range(nq_blocks):
                    q0 = qb_ * TB
                    nkt = (q0 + TB) // 128
                    aoff = (qb_ % acc_sets) * (len(psAcc) // acc_sets)
                    held = {}

                    def qk(kt):
                        a_ = kt - q0 // 128
                        qs = max(a_, 0) * 128
                        for si_, st_ in enumerate(streams):
                            bi = cnt["s"] % len(psS); cnt["s"] += 1
                            ps_, psn = psS[bi], f"psS{bi}"
                            pt_, ptn = Pt[bi], f"Pt{bi}"
                            rows = st_["rows"]
                            P.mm(ps_[:, qs:TB], st_["kT"][rows, kt * 128:(kt + 1) * 128], st_["qT"][rows, q0 + qs:q0 + TB], r=[st_["kn"], st_["qn"]], w=[psn])
                            P.act(pt_[:, qs:TB], ps_[:, qs:TB], AF.Exp, r=[psn], w=[ptn], scale=0.125)
                            if a_ >= 0:
                                P.tt(pt_[:, qs:qs + 128], pt_[:, qs:qs + 128], tri[:], ALU.mult, r=[ptn, "tri"], w=[ptn], eng=("pool" if si_ % 2 else "dve"))
                            held[(kt, si_)] = (pt_, ptn, qs)

                    def av(kt):
                        for si_, st_ in enumerate(streams):
                            pt_, ptn, qs = held.pop((kt, si_))
                            for lf, ai in st_["vals"]:
                                P.mm(psAcc[aoff + ai][:, qs:TB], lf(kt), pt_[:, qs:TB], start=(kt == 0), stop=(kt == nkt - 1), r=[ptn, st_["vn"]], w=[f"acc{aoff + ai}"])

                    for kt in range(min(LA, nkt)):
                        qk(kt)
                    for kt in range(nkt):
                        if kt + LA < nkt:
                            qk(kt + LA)
                        av(kt)
                    epilogue(qb_, q0, aoff)

            with ExitStack() as es:
                sb = lambda n, s, d=F32: es.enter_context(nc.sbuf_tensor(U(n), list(s), d))
                qT = sb("qT", [128, S], BF16); kT = sb("kT", [128, S], BF16)
                Vt = sb("Vt", [128, NKT, 128], BF16)
                Pt = [sb(f"Pt{i}", [128, TB], BF16) for i in range(4)]
                lv = sb("lv", [128, 256]); lp = sb("lp", [128, 64]); ls = sb("ls", [128, 2]); nlam = sb("nlam", [128, 1])
                dsg = sb("dsg", [128, 1])
                rc0 = sb("rc0", [128, TB]); rc1 = sb("rc1", [128, TB]); o0 = sb("o0", [128, TB]); o1 = sb("o1", [128, TB])
                osq = sb("osq", [128, TB], BF16); dl = sb("dl", [128, TB]); dr = sb("dr", [128, TB])
                do_ = [sb(f"do{i}", [128, TB], BF16) for i in range(2)]
                psS = [es.enter_context(nc.psum_tensor(U(f"psS{i}"), [128, TB], F32)) for i in range(4)]
                psAcc = [es.enter_context(nc.psum_tensor(U(f"acc{i}"), [128, TB], F32)) for i in range(4)]
                with nc.Block() as block:
                    P.dma(lv[:], lam_d[l:l + 1, :].to_broadcast([128, 256]), w=["lv"])
                    P.dma(dsg[:], dsg_d[l], w=["dsg"])
                    for k2 in range(2):
                        P.tt(lp[:], lv[:, 128 * k2:128 * k2 + 64], lv[:, 128 * k2 + 64:128 * k2 + 128], ALU.mult, r=["lv", "ls"], w=["lp"])
                        P.add("dve", lambda e, k2=k2: e.tensor_reduce(out=ls[:, k2:k2 + 1], in_=lp[:], axis=AX.X, op=ALU.add), r=["lp"], w=["ls"])
                    P.act(ls[:], ls[:], AF.Exp, r=["ls"], w=["ls"])
                    P.stt(nlam[:], ls[:, 1:2], -lam_init, ls[:, 0:1], ALU.add, ALU.subtract, r=["ls"], w=["nlam"])
                    P.ts(dsg[:], dsg[:], 1.0 - lam_init, None, ALU.mult, r=["dsg"], w=["dsg"])
                    cnt = {"s": 0, "o": 0}
                    for h in range(4):
                        P.dma(qT[:], dqT_d[h], w=["qT"])
                        P.dma(kT[:], dkT_d[h], w=["kT"])
                        P.dma(Vt[:], dV_d.rearrange("(kt p) c -> p kt c", p=128)[:, :, h * 128:(h + 1) * 128], w=["Vt"])
                        streams = [dict(kT=kT, qT=qT, rows=slice(c * 64, (c + 1) * 64), kn="kT", qn="qT", vn="Vt",
                                        vals=[(lambda kt: Vt[:, kt, :], 2 * c), (lambda kt: onesb[:], 2 * c + 1)]) for c in range(2)]

                        def epi(qb_, q0, aoff, h=h):
                            P.add("dve", lambda e: e.reciprocal(out=rc0[:], in_=psAcc[1][:]), r=["acc1"], w=["rc0"])
                            P.add("dve", lambda e: e.reciprocal(out=rc1[:], in_=psAcc[3][:]), r=["acc3"], w=["rc1"])
                            P.tt(o0[:], psAcc[0][:], rc0[:], ALU.mult, r=["acc0", "rc0"], w=["o0"])
                            P.tt(o1[:], psAcc[2][:], rc1[:], ALU.mult, r=["acc2", "rc1"], w=["o1"])
                            P.stt(o0[:], o1[:], nlam[:, 0:1], o0[:], ALU.mult, ALU.add, r=["o1", "o0", "nlam"], w=["o0"])
                            P.act(osq[:], o0[:], AF.Square, r=["o0"], w=["osq"])
                            P.mm(psS[0][:], onesb[:], osq[:], r=["osq"], w=["psS0"])
                            P.act(dl[:], psS[0][:], AF.Ln, r=["psS0"], w=["dl"], scale=1.0 / 128, bias=epsc[:, 0:1])
                            P.act(dr[:], dl[:], AF.Exp, r=["dl"], w=["dr"], scale=-0.5)
                            d_ = do_[cnt["o"] % 2]; dn = f"do{cnt['o'] % 2}"; cnt["o"] += 1
                            P.stt(d_[:], o0[:], dsg[:, 0:1], dr[:], ALU.mult, ALU.mult, r=["o0", "dsg", "dr"], w=[dn])
                            P.dma(mixT_d[256 + h * 128:256 + (h + 1) * 128, q0:q0 + TB], d_[:], r=[dn], w=["mixTd"])

                        attention(P, streams, NB, epi, psS, psAcc, Pt, cnt)
                    P.emit(block)

            if stop == "D":
                return nc
            with ExitStack() as es:
                sb = lambda n, s, d=F32: es.enter_context(nc.sbuf_tensor(U(n), list(s), d))
                qT = sb("mqTs", [96, S], BF16); kT = sb("mkTs", [96, S], BF16)
                Vt = sb("mVt", [128, NKT, 128], BF16)
                onesS = sb("onesS", [96, S], BF16); tmpS = sb("tmpS", [96, S], BF16)
                Pt = [sb(f"Pt{i}", [128, TB], BF16) for i in range(4)]
                mng = sb("mngs", [64, 1])
                osb = sb("osb", [128, TB]); rc0 = sb("mrc", [64, TB]); o0 = sb("mo0", [64, TB])
                osq = sb("mosq", [64, TB], BF16); dl = sb("mdl", [64, TB]); dr = sb("mdr", [64, TB])
                do_ = [sb(f"mdo{i}", [64, TB], BF16) for i in range(2)]
                psS = [es.enter_context(nc.psum_tensor(U(f"psS{i}"), [128, TB], F32)) for i in range(4)]
                psAcc = [es.enter_context(nc.psum_tensor(U(f"acc{i}"), [128, TB], F32)) for i in range(2)]
                pmv = es.enter_context(nc.psum_tensor(U("pmv"), [128, TB], F32))
                pnm = es.enter_context(nc.psum_tensor(U("pnm"), [128, TB], F32))
                with nc.Block() as block:
                    P.dma(mng[:], mng_d[l], w=["mng"])
                    P.memset(Vt[:], 1.0, w=["Vt"], eng="pool")
                    P.memset(onesS[64:96, :], 1.0, w=["onesS"], eng="pool")
                    P.add("pool", lambda e: e.affine_select(out=tmpS[64:96, :], in_=onesS[64:96, :], pattern=[[1, S]], compare_op=ALU.is_ge, fill=0.0, base=0, channel_multiplier=-256), r=["onesS"], w=["tmpS"])
                    P.add("pool", lambda e: e.affine_select(out=kT[64:96, :], in_=tmpS[64:96, :], pattern=[[-1, S]], compare_op=ALU.is_ge, fill=0.0, base=255, channel_multiplier=256), r=["tmpS"], w=["kT1h"])
                    cnt = {"s": 0, "o": 0, "a": 0}
                    for h in range(4):
                        P.dma(qT[:], mqT_d[h], w=["qT"])
                        P.dma(kT[0:64, :], mkT_d[h], r=["kT1h"], w=["kT"])
                        P.dma(Vt[:, :, 0:64], mV_d.rearrange("(kt p) c -> p kt c", p=128)[:, :, h * 64:(h + 1) * 64], w=["Vt"])
                        streams = [dict(kT=kT, qT=qT, rows=slice(0, 96), kn="kT", qn="qT", vn="Vt", vals=[(lambda kt: Vt[:, kt, :], 0)])]

                        def epi(qb_, q0, aoff, h=h):
                            P.copy(osb[:], psAcc[aoff][:], r=[f"acc{aoff}"], w=["osb"], eng="act")
                            P.mm(pmv[0:64, :], ident[:, 64:128], osb[:], r=["osb"], w=["pmv"])
                            P.add("dve", lambda e: e.reciprocal(out=rc0[:], in_=pmv[0:64, :]), r=["pmv"], w=["rc0"])
                            P.tt(o0[:], osb[0:64, :], rc0[:], ALU.mult, r=["osb", "rc0"], w=["o0"])
                            P.act(osq[:], o0[:], AF.Square, r=["o0"], w=["osq"])
                            P.mm(pnm[0:64, :], onesb[0:64, 0:64], osq[:], r=["osq"], w=["pnm"])
                            P.act(dl[:], pnm[0:64, :], AF.Ln, r=["pnm"], w=["dl"], scale=1.0 / 64, bias=epsc[0:64, 0:1])
                            P.act(dr[:], dl[:], AF.Exp, r=["dl"], w=["dr"], scale=-0.5)
                            d_ = do_[cnt["o"] % 2]; dn = f"mdo{cnt['o'] % 2}"; cnt["o"] += 1
                            P.stt(d_[:], o0[:], mng[:, 0:1], dr[:], ALU.mult, ALU.mult, r=["o0", "mng", "dr"], w=[dn])
                            P.dma(mixT_d[768 + h * 64:768 + (h + 1) * 64, q0:q0 + TB], d_[:], r=[dn], w=["mixTd"])

                        attention(P, streams, NB, epi, psS, psAcc, Pt, cnt)
                    P.emit(block)

            if stop == "M":
                return nc
            with ExitStack() as es:
                sb = lambda n, s, d=F32: es.enter_context(nc.sbuf_tensor(U(n), list(s), d))
                wo = sb("wo", [128, 8, D], BF16)
                xbs = [sb(f"xb{i}", [128, 8, TB]) for i in range(2)]
                mxs = [sb(f"mx{i}", [128, 8, TB], BF16) for i in range(2)]
                pc = [es.enter_context(nc.psum_tensor(U(f"pc{i}"), [128, TB], F32)) for i in range(4)]
                with nc.Block() as block:
                    load_cast(P, wo, wout_d[l], 8, D, "wo")
                    mixv = mixT_d.rearrange("(ft p) t -> p ft t", p=128)

                    def ld(i):
                        P.dma(xbs[i % 2][:], xTv[:, :, i * TB:(i + 1) * TB], r=["xTd"], w=[f"xb{i % 2}_{ft}" for ft in range(8)])
                        P.dma(mxs[i % 2][:], mixv[:, :, i * TB:(i + 1) * TB], w=[f"mx{i % 2}"])
                    ld(0)
                    k = 0
                    for i in range(NB):
                        if i + 1 < NB:
                            ld(i + 1)
                        xb = xbs[i % 2]; mx = mxs[i % 2]
                        for fo in range(8):
                            p_ = pc[k % 4]; pn_ = f"pc{k % 4}"; k += 1
                            for kt in range(8):
                                P.mm(p_[:], wo[:, kt, fo * 128:(fo + 1) * 128], mx[:, kt, :], start=(kt == 0), stop=(kt == 7), r=["wo", f"mx{i % 2}"], w=[pn_])
                            P.stt(xb[:, fo, :], p_[:], modv[:, l, 16 + fo:17 + fo], xb[:, fo, :], ALU.mult, ALU.add, r=[pn_, f"xb{i % 2}_{fo}"], w=[f"xb{i % 2}_{fo}"])
                        P.dma(xTv[:, :, i * TB:(i + 1) * TB], xb[:], r=[f"xb{i % 2}_{ft}" for ft in range(8)], w=["xTd"])
                    P.emit(block)

            if stop == "C1":
                return nc
            with ExitStack() as es:
                sb = lambda n, s, d=F32: es.enter_context(nc.sbuf_tensor(U(n), list(s), d))
                w1 = sb("w1", [128, 8, DFF], BF16)
                w2 = sb("w2", [128, 32, D], BF16)
                xb = sb("xb", [128, 8, TB])
                hT = sb("hT", [128, 8, TB], BF16)
                hid = sb("hid", [128, 32, TB], BF16)
                sqs = [sb(f"sq{i}", [128, TB], BF16) for i in range(2)]
                tmps = [sb(f"nt{i}", [128, TB]) for i in range(2)]
                rstd = sb("rstd", [128, TB]); lnv = sb("lnv", [128, TB])
                rl = [sb(f"rl{i}", [128, TB]) for i in range(2)]
                pc = [es.enter_context(nc.psum_tensor(U(f"pc{i}"), [128, TB], F32)) for i in range(5)]
                pd = [es.enter_context(nc.psum_tensor(U(f"pd{i}"), [128, TB], F32)) for i in range(2)]
                pn = es.enter_context(nc.psum_tensor(U("pn"), [128, TB], F32))
                with nc.Block() as block:
                    load_cast(P, w1, w1_d[l], 8, DFF, "w1")
                    load_cast(P, w2, w2_d[l], 32, D, "w2")
                    k = 0; k2 = 0
                    for i in range(NB):
                        P.dma(xb[:], xTv[:, :, i * TB:(i + 1) * TB], r=["xTd"], w=[f"xb_{ft}" for ft in range(8)] + ["xb"])
                        norm_mod(P, xb, "xb", hT, "hT", A2, lambda kt: modv[:, l, 24 + kt:25 + kt], l, pn, tmps, sqs, rstd, lnv)
                        hres = [f"hT_{kt}" for kt in range(8)]
                        for ft in range(32):
                            p_ = pc[k % 5]; pn_ = f"pc{k % 5}"; r_ = rl[k % 2]; rn = f"rl{k % 2}"; k += 1
                            for kt in range(8):
                                P.mm(p_[:], w1[:, kt, ft * 128:(ft + 1) * 128], hT[:, kt, :], start=(kt == 0), stop=(kt == 7), r=["w1", hres[kt]], w=[pn_])
                            P.act(r_[:], p_[:], AF.Relu, r=[pn_], w=[rn])
                            P.tt(hid[:, ft, :], r_[:], r_[:], ALU.mult, r=[rn], w=[f"hid{ft}"], eng=("pool" if ft % 2 else "dve"))
                        for fo in range(8):
                            p_ = pd[k2 % 2]; pn_ = f"pd{k2 % 2}"; k2 += 1
                            for ft in range(32):
                                P.mm(p_[:], w2[:, ft, fo * 128:(fo + 1) * 128], hid[:, ft, :], start=(ft == 0), stop=(ft == 31), r=["w2", f"hid{ft}"], w=[pn_])
                            P.stt(xb[:, fo, :], p_[:], modv[:, l, 40 + fo:41 + fo], xb[:, fo, :], ALU.mult, ALU.add, r=[pn_, "xb", f"xb_{fo}"], w=[f"xb_{fo}"])
                        P.dma(xTv[:, :, i * TB:(i + 1) * TB], xb[:], r=[f"xb_{ft}" for ft in range(8)], w=["xTd", "xb"])
                    P.emit(block)

        if stop == "C2":
            return nc
        with ExitStack() as es:
            sb = lambda n, s, d=F32: es.enter_context(nc.sbuf_tensor(U(n), list(s), d))
            xbs = [sb(f"xb{i}", [128, 8, TB]) for i in range(2)]
            yn = sb("yn", [128, 8, TB])
            sqs = [sb(f"sq{i}", [128, TB], BF16) for i in range(2)]
            rstd = sb("rstd", [128, TB]); lnv = sb("lnv", [128, TB])
            ost = [sb(f"ost{i}", [128, D]) for i in range(2)]
            pt = [es.enter_context(nc.psum_tensor(U(f"pt{i}"), [128, TB], F32)) for i in range(6)]
            pn = es.enter_context(nc.psum_tensor(U("pn"), [128, TB], F32))
            with nc.Block() as block:
                P.dma(xbs[0][:], xTv[:, :, 0:TB], w=["xb0"])
                k = 0; ko = 0
                for i in range(NB):
                    if i + 1 < NB:
                        P.dma(xbs[(i + 1) % 2][:], xTv[:, :, (i + 1) * TB:(i + 2) * TB], w=[f"xb{(i + 1) % 2}"])
                    xb = xbs[i % 2]; xn = f"xb{i % 2}"
                    for kt in range(8):
                        sq = sqs[kt % 2]; sqn = f"sq{kt % 2}"
                        P.act(sq[:], xb[:, kt, :], AF.Square, r=[xn], w=[sqn])
                        P.mm(pn[:], onesb[:], sq[:], start=(kt == 0), stop=(kt == 7), r=[sqn], w=["pn"])
                    P.act(lnv[:], pn[:], AF.Ln, r=["pn"], w=["lnv"], scale=1.0 / D, bias=epsc[:, 0:1])
                    P.act(rstd[:], lnv[:], AF.Exp, r=["lnv"], w=["rstd"], scale=-0.5)
                    for kt in range(8):
                        P.stt(yn[:, kt, :], xb[:, kt, :], fgT[:, kt:kt + 1], rstd[:], ALU.mult, ALU.mult, r=[xn, "rstd"], w=[f"yn{kt}"])
                    for tt in range(4):
                        o_ = ost[ko % 2]; on = f"ost{ko % 2}"; ko += 1
                        for hf in range(2):
                            p_ = pt[k % 6]; pn_ = f"pt{k % 6}"; k += 1
                            for kk in range(4):
                                kt = hf * 4 + kk
                                P.tr(p_[:, kk * 128:(kk + 1) * 128], yn[:, kt, tt * 128:(tt + 1) * 128], ident[:], r=[f"yn{kt}"], w=[pn_])
                            P.copy(o_[:, hf * 512:(hf + 1) * 512], p_[:], r=[pn_], w=[f"{on}_{hf}"], eng=("act" if hf else "dve"))
                        P.dma(out_d[i * TB + tt * 128:i * TB + (tt + 1) * 128, :], o_[:], r=[f"{on}_0", f"{on}_1"], w=["outd"])
                P.emit(block)
    return nc


def _layout_inputs(inp, S):
    f = lambda a: np.ascontiguousarray(a, dtype=np.float32)
    col8 = lambda v: f(np.asarray(v).reshape(8, 128).T)
    cst = np.zeros((128, 2), np.float32)
    inv = (ROPE_THETA ** (-np.arange(0, 16, 2, dtype=np.float32) / 16)).astype(np.float32)
    for s0 in (0, 64):
        for d in range(16):
            cst[s0 + d, 0] = inv[d % 8]
            cst[s0 + d, 1] = -1.0 if d < 8 else 1.0
    L = DEPTH
    shared = {
        "cst": cst,
        "w_ada": f(inp["w_ada"]),
        "b_adaT": f(np.asarray(inp["b_ada"]).reshape(L, 48, 128).transpose(0, 2, 1)),
        "n1gT": f(np.asarray(inp["norm1_g"]).reshape(L, 8, 128).transpose(0, 2, 1)),
        "n2gT": f(np.asarray(inp["norm2_g"]).reshape(L, 8, 128).transpose(0, 2, 1)),
        "fgT": col8(inp["final_g"]),
        "w_in": f(inp["w_in"]), "w_out": f(inp["w_out"]), "mlp_w1": f(inp["mlp_w1"]), "mlp_w2": f(inp["mlp_w2"]),
        "s_are": f(np.asarray(inp["ssm_a_re"]).reshape(L, 8, 128).transpose(0, 2, 1)),
        "s_aim": f(np.asarray(inp["ssm_a_im"]).reshape(L, 8, 128).transpose(0, 2, 1)),
        "s_ldt": f(np.repeat(np.asarray(inp["ssm_log_dt"]).reshape(L, 8, 2), 64, axis=2).transpose(0, 2, 1)),
        "s_bre": f(np.asarray(inp["ssm_b_re"]).reshape(L, 8, 2, 64, 16).transpose(0, 2, 3, 1, 4).reshape(L, 128, 8, 16)),
        "s_bim": f(np.asarray(inp["ssm_b_im"]).reshape(L, 8, 2, 64, 16).transpose(0, 2, 3, 1, 4).reshape(L, 128, 8, 16)),
        "s_cre": f(np.asarray(inp["ssm_c_re"]).reshape(L, 8, 2, 16, 64).transpose(0, 2, 4, 1, 3).reshape(L, 128, 8, 16)),
        "s_cim": f(np.asarray(inp["ssm_c_im"]).reshape(L, 8, 2, 16, 64).transpose(0, 2, 4, 1, 3).reshape(L, 128, 8, 16)),
        "s_d": f(np.asarray(inp["ssm_d"]).reshape(L, 2, 128).transpose(0, 2, 1)),
        "s_gw": f(inp["ssm_glu_w"]),
        "s_gb": f(np.asarray(inp["ssm_glu_b"]).reshape(L, 2, 128).transpose(0, 2, 1)),
        "s_ng": f(np.asarray(inp["ssm_norm_g"]).reshape(L, 2, 128).transpose(0, 2, 1)),
        "lamv": f(np.concatenate([np.asarray(inp[k]) for k in ("diff_lq1", "diff_lk1", "diff_lq2", "diff_lk2")], axis=1)),
        "dsg": f(np.asarray(inp["diff_subln_g"]).reshape(L, 128, 1)),
        "mng": f(np.asarray(inp["moba_norm_g"]).reshape(L, 64, 1)),
    }
    x = np.asarray(inp["x"]); c = np.asarray(inp["c"]); pos = np.asarray(inp["positions"])
    maps = []
    for core in range(8):
        b = core % x.shape[0]
        m = dict(shared)
        m["x"] = f(x[b, :S])
        m["pos"] = np.ascontiguousarray(pos[b:b + 1, :S], dtype=np.int32)
        m["cT"] = col8(c[b])
        maps.append(m)
    return maps


def kernel(**inputs):
    S = SEQ
    nc = build(S)
    maps = _layout_inputs(inputs, S)
    res = run_bass_kernel_spmd(nc, maps, core_ids=list(range(8)))
    B = np.asarray(inputs["x"]).shape[0]
    return np.stack([np.asarray(res.results[b]["out"], dtype=np.float32) for b in range(B)], axis=0)
```

```python
import math
from contextlib import ExitStack

import numpy as np
import concourse.bass as bass
import concourse.mybir as mybir
from concourse.bass_utils import run_bass_kernel_spmd

F32 = mybir.dt.float32
BF16 = mybir.dt.bfloat16
I32 = mybir.dt.int32
ALU = mybir.AluOpType
AF = mybir.ActivationFunctionType
AX = mybir.AxisListType

D = 1024
SEQ = 8192
DEPTH = 2
DFF = 4096
INW = 2560
TB = 512
EPS = 1e-6
ROPE_THETA = 500000.0

ENGS = ("pe", "act", "dve", "pool", "sp")
PSUM_PREFIX = ("pa", "pb", "pn", "pg", "pst", "modps", "pbu", "py", "psS", "acc", "pmv", "pnm", "pc", "pd", "pt")


class Prog:
    NDMA = 6

    def __init__(self, nc, sems):
        self.nc = nc
        self.sems = sems
        self.cnt = {e: 0 for e in ENGS}
        self.dcnt = {}
        self.drot = {e: 0 for e in ENGS}
        self.reset()

    def reset(self):
        self.ops = {e: [] for e in ENGS}
        self.last_w = {}
        self.readers = {}

    def add(self, eng, fn, r=(), w=(), dma=False):
        deps = []
        for x in r:
            if x in self.last_w:
                deps.append(self.last_w[x])
            if x.startswith(PSUM_PREFIX):
                deps.extend(o for o in self.readers.get(x, ()) if o["eng"] != eng)
        for x in w:
            if x in self.last_w:
                deps.append(self.last_w[x])
            deps.extend(self.readers.get(x, ()))
        op = {"fn": fn, "deps": [], "dma": dma, "sig": False, "eng": eng, "val": None, "sem": None}
        for d in deps:
            if d is op:
                continue
            if d["eng"] == eng and not d["dma"] and not dma and eng == "pe":
                continue
            if not any(d is x for x in op["deps"]):
                op["deps"].append(d)
                d["sig"] = True
        self.ops[eng].append(op)
        for x in r:
            self.readers.setdefault(x, []).append(op)
        for x in w:
            self.last_w[x] = op
            self.readers[x] = []
        return op

    def emit(self, block):
        nc = self.nc
        for e in ENGS:
            for op in self.ops[e]:
                if op["dma"]:
                    k = ("dma", e, self.drot[e] % self.NDMA)
                    self.drot[e] += 1
                    op["sem"] = k
                    op["prev"] = self.dcnt.get(k, 0)
                    self.dcnt[k] = op["prev"] + 16
                    op["val"] = self.dcnt[k]
                elif op["sig"]:
                    self.cnt[e] += 1
                    op["sem"] = e
                    op["val"] = self.cnt[e]
        final_dma = dict(self.dcnt)
        sems = self.sems
        ops = self.ops

        def run(e, eng):
            waited = {}
            for op in ops[e]:
                need = {}
                for d in op["deps"]:
                    k = d["sem"]
                    if d["val"] > need.get(k, 0):
                        need[k] = d["val"]
                if op["dma"] and op["prev"] > 0:
                    k = op["sem"]
                    need[k] = max(need.get(k, 0), op["prev"])
                for k, v in need.items():
                    if waited.get(k, 0) < v:
                        eng.wait_ge(sems[k], v)
                        waited[k] = v
                inst = op["fn"](eng)
                if op["dma"]:
                    inst.then_inc(sems[op["sem"]], 16)
                elif op["sig"]:
                    inst.then_inc(sems[op["sem"]], 1)
            if e == "sp":
                for k, v in final_dma.items():
                    if v > 0 and waited.get(k, 0) < v:
                        eng.wait_ge(sems[k], v)

        for e, starter in (("pe", block.tensor), ("act", block.scalar), ("dve", block.vector), ("pool", block.gpsimd), ("sp", block.sync)):
            if ops[e] or e == "sp":
                starter(lambda eng, e=e: run(e, eng))

        self.reset()

    def dma(self, out, in_, r=(), w=(), q="sp", **kw):
        return self.add(q, lambda eng: eng.dma_start(out=out, in_=in_, **kw), r=r, w=w, dma=True)

    def mm(self, out, lhsT, rhs, start=True, stop=True, r=(), w=(), **kw):
        return self.add("pe", lambda eng: eng.matmul(out, lhsT, rhs, start=start, stop=stop, **kw), r=r, w=w)

    def tr(self, out, in_, ident, r=(), w=()):
        return self.add("pe", lambda eng: eng.transpose(out, in_, ident), r=r, w=w)

    def act(self, out, in_, func, r=(), w=(), **kw):
        return self.add("act", lambda eng: eng.activation(out=out, in_=in_, func=func, **kw), r=r, w=w)

    def tt(self, out, in0, in1, op, r=(), w=(), eng="dve"):
        return self.add(eng, lambda e: e.tensor_tensor(out=out, in0=in0, in1=in1, op=op), r=r, w=w)

    def ts(self, out, in0, s1, s2, op0, op1=None, r=(), w=(), eng="dve", **kw):
        if op1 is None:
            return self.add(eng, lambda e: e.tensor_scalar(out=out, in0=in0, scalar1=s1, scalar2=None, op0=op0, **kw), r=r, w=w)
        return self.add(eng, lambda e: e.tensor_scalar(out=out, in0=in0, scalar1=s1, scalar2=s2, op0=op0, op1=op1, **kw), r=r, w=w)

    def stt(self, out, in0, scalar, in1, op0, op1, r=(), w=()):
        return self.add("dve", lambda e: e.scalar_tensor_tensor(out=out, in0=in0, scalar=scalar, in1=in1, op0=op0, op1=op1), r=r, w=w)

    def copy(self, out, in_, r=(), w=(), eng="dve"):
        if eng == "act":
            return self.add("act", lambda e: e.copy(out=out, in_=in_), r=r, w=w)
        return self.add(eng, lambda e: e.tensor_copy(out=out, in_=in_), r=r, w=w)

    def memset(self, ap, val, r=(), w=(), eng="dve"):
        return self.add(eng, lambda e: e.memset(ap, val), r=r, w=w)


PI = math.pi
SKIP = set()
TWO_PI = 2.0 * math.pi
CW1 = 6.28125
CW2 = TWO_PI - 6.28125
PI_LO = 3.141592


def sincos(P, ang, n, out_sin, out_cos, t1, t2, t3, ti, tag, rsin, rcos, np_=128):
    a = lambda nm: tag + nm
    sl = lambda t: t[0:np_, 0:n]
    P.ts(sl(t1), ang, 1.0 / TWO_PI, None, ALU.mult, r=[a("ang")], w=[a("t1")])
    P.copy(sl(ti), sl(t1), r=[a("t1")], w=[a("ti")])
    P.copy(sl(t1), sl(ti), r=[a("ti")], w=[a("t1")])
    P.stt(sl(t2), sl(t1), -CW1, ang, ALU.mult, ALU.add, r=[a("t1"), a("ang")], w=[a("t2")])
    P.stt(sl(t3), sl(t1), -CW2, sl(t2), ALU.mult, ALU.add, r=[a("t1"), a("t2")], w=[a("t3")])
    P.ts(sl(t1), sl(t3), PI, -TWO_PI, ALU.is_gt, ALU.mult, r=[a("t3")], w=[a("t1")])
    P.tt(sl(t2), sl(t3), sl(t1), ALU.add, r=[a("t3"), a("t1")], w=[a("t2")])
    P.ts(sl(t1), sl(t2), -PI, TWO_PI, ALU.is_lt, ALU.mult, r=[a("t2")], w=[a("t1")])
    P.tt(sl(t3), sl(t2), sl(t1), ALU.add, r=[a("t2"), a("t1")], w=[a("t3")])
    P.ts(sl(t1), sl(t3), PI_LO, -PI_LO, ALU.min, ALU.max, r=[a("t3")], w=[a("t1")])
    P.act(out_sin, sl(t1), AF.Sin, r=[a("t1")], w=[rsin])
    P.ts(sl(t2), sl(t3), PI / 2, None, ALU.add, r=[a("t3")], w=[a("t2")])
    P.ts(sl(t1), sl(t2), PI, -TWO_PI, ALU.is_gt, ALU.mult, r=[a("t2"), a("t1")], w=[a("t1")])
    P.tt(sl(t3), sl(t2), sl(t1), ALU.add, r=[a("t2"), a("t1")], w=[a("t3")])
    P.ts(sl(t2), sl(t3), PI_LO, -PI_LO, ALU.min, ALU.max, r=[a("t3")], w=[a("t2")])
    P.act(out_cos, sl(t2), AF.Sin, r=[a("t2")], w=[rcos])


def build(S, dbg=False, nl=DEPTH, stop=None):
    NB = S // TB
    NKT = S // 128
    nc = bass.Bass("TRN2", target_bir_lowering=False)
    _uid = [0]

    def U(n):
        _uid[0] += 1
        return f"{n}_u{_uid[0]}"

    din = lambda n, s, d=F32: nc.dram_tensor(n, list(s), d, kind="ExternalInput").ap()
    dscr = lambda n, s, d=F32: nc.dram_tensor(n, list(s), d, kind=("ExternalOutput" if dbg else "Internal")).ap()
    x_d = din("x", [S, D])
    out_d = nc.dram_tensor("out", [S, D], F32, kind="ExternalOutput").ap()
    pos_d = din("pos", [1, S], I32)
    cT_d = din("cT", [128, 8])
    cst_d = din("cst", [128, 2])
    wada_d = din("w_ada", [DEPTH, D, 6 * D])
    bada_d = din("b_adaT", [DEPTH, 128, 48])
    n1g_d = din("n1gT", [DEPTH, 128, 8])
    n2g_d = din("n2gT", [DEPTH, 128, 8])
    fg_d = din("fgT", [128, 8])
    win_d = din("w_in", [DEPTH, D, INW])
    wout_d = din("w_out", [DEPTH, D, D])
    w1_d = din("mlp_w1", [DEPTH, D, DFF])
    w2_d = din("mlp_w2", [DEPTH, DFF, D])
    sare_d = din("s_are", [DEPTH, 128, 8])
    saim_d = din("s_aim", [DEPTH, 128, 8])
    sldt_d = din("s_ldt", [DEPTH, 128, 8])
    sbre_d = din("s_bre", [DEPTH, 128, 8, 16])
    sbim_d = din("s_bim", [DEPTH, 128, 8, 16])
    scre_d = din("s_cre", [DEPTH, 128, 8, 16])
    scim_d = din("s_cim", [DEPTH, 128, 8, 16])
    sd_d = din("s_d", [DEPTH, 128, 2])
    sgw_d = din("s_gw", [DEPTH, 256, 256])
    sgb_d = din("s_gb", [DEPTH, 128, 2])
    sng_d = din("s_ng", [DEPTH, 128, 2])
    lam_d = din("lamv", [DEPTH, 256])
    dsg_d = din("dsg", [DEPTH, 128, 1])
    mng_d = din("mng", [DEPTH, 64, 1])
    xT_d = dscr("xT", [D, S])
    cosT_d = dscr("cosT", [128, S])
    sinT_d = dscr("sinT", [128, S])
    uT_d = dscr("uT", [256, S], BF16)
    dqT_d = dscr("dqT", [4, 128, S], BF16)
    dkT_d = dscr("dkT", [4, 128, S], BF16)
    dV_d = dscr("dV", [S, 512], BF16)
    mqT_d = dscr("mqT", [4, 96, S], BF16)
    mkT_d = dscr("mkT", [4, 64, S], BF16)
    mV_d = dscr("mV", [S, 256], BF16)
    mixT_d = dscr("mixT", [D, S], BF16)
    dbgk_d = dscr("dbgk", [128, 32]); dbgg_d = dscr("dbgg", [128, 32]); dbgm_d = dscr("dbgm", [128, 8]); dbgs_d = dscr("dbgs", [128, 32])

    with ExitStack() as top:
        sems = {}
        for e in ENGS:
            sems[e] = top.enter_context(nc.semaphore("s_" + e))
            for i in range(Prog.NDMA):
                sems[("dma", e, i)] = top.enter_context(nc.semaphore(f"d_{e}_{i}"))
        P = Prog(nc, sems)
        gsb = lambda n, s, d=F32: top.enter_context(nc.sbuf_tensor(U(n), list(s), d))
        ident = gsb("ident", [128, 128])
        identb = gsb("identb", [128, 128], BF16)
        onesb = gsb("onesb", [128, 128], BF16)
        tri = gsb("tri", [128, 128], BF16)
        pswap = gsb("pswap", [128, 128], BF16)
        epsc = gsb("epsc", [128, 1])
        cst = gsb("cstc", [128, 2])
        modv = gsb("modv", [128, DEPTH, 48])
        A1 = gsb("A1", [128, DEPTH, 8])
        A2 = gsb("A2", [128, DEPTH, 8])
        fgT = gsb("fgTs", [128, 8])

        with ExitStack() as es:
            sb = lambda n, s, d=F32: es.enter_context(nc.sbuf_tensor(U(n), list(s), d))
            onesf = sb("onesf", [128, 128])
            b1 = sb("b1", [128, 128])
            b2 = sb("b2", [128, 128])
            cT = sb("cTs", [128, 8])
            scT = sb("scT", [128, 8])
            tmp8 = sb("tmp8", [128, 8])
            wa = [sb(f"wa{i}", [128, 8, 512]) for i in range(2)]
            bada = sb("bada", [128, DEPTH, 48])
            ng1 = sb("ng1", [128, DEPTH, 8])
            ng2 = sb("ng2", [128, DEPTH, 8])
            posi = [sb(f"posi{i}", [128, TB], I32) for i in range(2)]
            ang = sb("ang", [128, TB])
            t1 = sb("t1", [128, TB]); t2 = sb("t2", [128, TB]); t3 = sb("t3", [128, TB])
            ti = sb("ti", [128, TB], I32)
            sn = [sb(f"sn{i}", [128, TB]) for i in range(2)]
            cs = [sb(f"cs{i}", [128, TB]) for i in range(2)]
            modps = es.enter_context(nc.psum_tensor(U("modps"), [128, DEPTH * 48], F32))
            with nc.Block() as block:
                P.memset(onesf[:], 1.0, w=["onesf"], eng="pool")
                P.memset(onesb[:], 1.0, w=["onesb"], eng="pool")
                P.memset(epsc[:], EPS, w=["epsc"], eng="pool")
                P.memset(pswap[:], 0.0, w=["pswap"], eng="pool")
                P.add("pool", lambda e: e.affine_select(out=ident[:], in_=onesf[:], pattern=[[-1, 128]], compare_op=ALU.is_equal, fill=0.0, base=0, channel_multiplier=1), r=["onesf"], w=["ident"])
                P.copy(identb[:], ident[:], r=["ident"], w=["identb"], eng="pool")
                P.add("pool", lambda e: e.affine_select(out=tri[:], in_=onesb[:], pattern=[[1, 128]], compare_op=ALU.is_ge, fill=0.0, base=0, channel_multiplier=-1), r=["onesb"], w=["tri"])
                P.add("pool", lambda e: e.affine_select(out=b1[:], in_=onesf[:], pattern=[[-1, 128]], compare_op=ALU.is_equal, fill=0.0, base=-8, channel_multiplier=1), r=["onesf"], w=["b1"])
                P.add("pool", lambda e: e.affine_select(out=b2[:], in_=onesf[:], pattern=[[-1, 128]], compare_op=ALU.is_equal, fill=0.0, base=8, channel_multiplier=1), r=["onesf"], w=["b2"])
                for s0 in (0, 64):
                    P.copy(pswap[:, s0:s0 + 8], b1[:, s0:s0 + 8], r=["b1", "pswap"], w=["pswap"], eng="pool")
                    P.copy(pswap[:, s0 + 8:s0 + 16], b2[:, s0 + 8:s0 + 16], r=["b2", "pswap"], w=["pswap"], eng="pool")
                P.dma(cT[:], cT_d, w=["cT"])
                P.dma(cst[:], cst_d, w=["cst"])
                P.dma(bada[:], bada_d.rearrange("l p j -> p l j"), w=["bada"])
                P.dma(ng1[:], n1g_d.rearrange("l p j -> p l j"), w=["ng1"])
                P.dma(ng2[:], n2g_d.rearrange("l p j -> p l j"), w=["ng2"])
                P.dma(fgT[:], fg_d, w=["fgT"])
                P.act(tmp8[:], cT[:], AF.Exp, r=["cT"], w=["tmp8"], scale=-1.0)
                P.ts(tmp8[:], tmp8[:], 1.0, None, ALU.add, r=["tmp8"], w=["tmp8"])
                P.add("dve", lambda e: e.reciprocal(out=scT[:], in_=tmp8[:]), r=["tmp8"], w=["scT"])
                P.tt(scT[:], scT[:], cT[:], ALU.mult, r=["scT", "cT"], w=["scT"])
                k = 0
                for l in range(DEPTH):
                    wv = wada_d[l].rearrange("(kt p) n -> p kt n", p=128)
                    for cb in range(12):
                        wb_ = wa[k % 2]; wn = f"wa{k % 2}"; k += 1
                        P.dma(wb_[:], wv[:, :, cb * 512:(cb + 1) * 512], w=[wn])
                        for j in range(4):
                            col = l * 48 + cb * 4 + j
                            for kt in range(8):
                                P.mm(modps[:, col:col + 1], wb_[:, kt, j * 128:(j + 1) * 128], scT[:, kt:kt + 1],
                                     start=(kt == 0), stop=(kt == 7), r=[wn, "scT"], w=["modps"])
                P.tt(modv[:].rearrange("p l j -> p (l j)"), modps[:], bada[:].rearrange("p l j -> p (l j)"), ALU.add, r=["modps", "bada"], w=["modv"])
                for l in range(DEPTH):
                    P.stt(A1[:, l, :], modv[:, l, 8:16], 1.0, ng1[:, l, :], ALU.add, ALU.mult, r=["modv", "ng1"], w=["A1"])
                    P.stt(A2[:, l, :], modv[:, l, 32:40], 1.0, ng2[:, l, :], ALU.add, ALU.mult, r=["modv", "ng2"], w=["A2"])
                for i in range(NB):
                    pi_ = posi[i % 2]; pn_ = f"posi{i % 2}"
                    P.dma(pi_[:], pos_d[0:1, i * TB:(i + 1) * TB].to_broadcast([128, TB]), w=[pn_])
                    P.copy(t1[:], pi_[:], r=[pn_], w=["rt1"])
                    P.ts(ang[:], t1[:], cst[:, 0:1], None, ALU.mult, r=["rt1", "cst"], w=["rang"])
                    sincos(P, ang[:], TB, sn[i % 2][:], cs[i % 2][:], t1, t2, t3, ti, "r", f"sn{i % 2}", f"cs{i % 2}")
                    P.ts(sn[i % 2][:], sn[i % 2][:], cst[:, 1:2], None, ALU.mult, r=[f"sn{i % 2}", "cst"], w=[f"sn{i % 2}"])
                    P.dma(sinT_d[:, i * TB:(i + 1) * TB], sn[i % 2][:], r=[f"sn{i % 2}"], w=["sinT"])
                    P.dma(cosT_d[:, i * TB:(i + 1) * TB], cs[i % 2][:], r=[f"cs{i % 2}"], w=["cosT"])
                P.emit(block)

        if stop == "0":
            return nc
        with ExitStack() as es:
            sb = lambda n, s, d=F32: es.enter_context(nc.sbuf_tensor(U(n), list(s), d))
            xin = [sb(f"xin{i}", [128, D]) for i in range(3)]
            xo = [sb(f"xo{i}", [128, 8, TB]) for i in range(2)]
            pst = [es.enter_context(nc.psum_tensor(U(f"pst{i}"), [128, TB], F32)) for i in range(8)]
            with nc.Block() as block:
                k = 0
                for i in range(NB):
                    o_ = xo[i % 2]; on = f"xo{i % 2}"
                    for tt in range(4):
                        xi = xin[k % 3]; xn = f"xin{k % 3}"; k += 1
                        P.dma(xi[:], x_d[i * TB + tt * 128:i * TB + (tt + 1) * 128, :], w=[xn])
                        for ft in range(8):
                            P.tr(pst[ft][:, tt * 128:(tt + 1) * 128], xi[:, ft * 128:(ft + 1) * 128], ident[:], r=[xn], w=[f"pst{ft}"])
                    for ft in range(8):
                        P.copy(o_[:, ft, :], pst[ft][:], r=[f"pst{ft}"], w=[f"{on}_{ft}"], eng=("act" if ft % 2 else "dve"))
                    P.dma(xT_d.rearrange("(ft p) t -> p ft t", p=128)[:, :, i * TB:(i + 1) * TB], o_[:], r=[f"{on}_{ft}" for ft in range(8)], w=["xTd"])
                P.emit(block)

        if stop == "T0":
            return nc
        xTv = xT_d.rearrange("(ft p) t -> p ft t", p=128)

        def load_cast(P, dst3, src2, nk, ncol, name, q="pool", step=2048, col_major=False):
            sv = src2.rearrange("(kt p) n -> p kt n", p=128)
            chunks = [(kt, c0) for kt in range(nk) for c0 in range(0, ncol, step)]
            if col_major:
                chunks.sort(key=lambda t: (t[1], t[0]))
            for kt, c0 in chunks:
                c1 = min(ncol, c0 + step)
                P.dma(dst3[:, kt, c0:c1], sv[:, kt, c0:c1], w=[f"{name}_{kt}_{c0 // step}"], q=q)

        def norm_mod(P, xb, xn, hT, hn, A, B, l, pn, tmps, sqs, rstd, lnv, pool_share=True):
            for kt in range(8):
                sq = sqs[kt % 2]; sqn = f"sq{kt % 2}"
                P.act(sq[:], xb[:, kt, :], AF.Square, r=[xn], w=[sqn])
                P.mm(pn[:], onesb[:], sq[:], start=(kt == 0), stop=(kt == 7), r=[sqn, "onesb"], w=["pn"])
            P.act(lnv[:], pn[:], AF.Ln, r=["pn", "epsc"], w=["lnv"], scale=1.0 / D, bias=epsc[:, 0:1])
            P.act(rstd[:], lnv[:], AF.Exp, r=["lnv"], w=["rstd"], scale=-0.5)
            for kt in range(8):
                tm = tmps[kt % 2]; tn = f"nt{kt % 2}"
                P.tt(tm[:], xb[:, kt, :], rstd[:], ALU.mult, r=[xn, "rstd"], w=[tn], eng=("pool" if (pool_share and kt % 2) else "dve"))
                P.act(hT[:, kt, :], tm[:], AF.Identity, r=[tn, "A", "modv"], w=[f"{hn}_{kt}"], scale=A[:, l, kt:kt + 1], bias=B(kt))

        for l in range(nl):
            with ExitStack() as es:
                sb = lambda n, s, d=F32: es.enter_context(nc.sbuf_tensor(U(n), list(s), d))
                win = sb("win", [128, 8, INW], BF16)
                xbs = [sb(f"xb{i}", [128, 8, TB]) for i in range(2)]
                hTs = [sb(f"hT{i}", [128, 8, TB], BF16) for i in range(2)]
                sqs = [sb(f"sq{i}", [128, TB], BF16) for i in range(2)]
                tmps = [sb(f"nt{i}", [128, TB]) for i in range(2)]
                rstd = sb("rstd", [128, TB]); lnv = sb("lnv", [128, TB])
                cosb = [sb(f"cosb{i}", [128, TB]) for i in range(2)]
                sinb = [sb(f"sinb{i}", [128, TB]) for i in range(2)]
                qb = [sb(f"qb{i}", [128, TB], BF16) for i in range(2)]
                r1 = [sb(f"r1_{i}", [128, TB]) for i in range(2)]
                r2 = [sb(f"r2_{i}", [128, TB]) for i in range(2)]
                rf = [sb(f"rf{i}", [128, TB]) for i in range(2)]
                stg = [sb(f"stg{i}", [128, TB], BF16) for i in range(4)]
                vst = [sb(f"vst{i}", [128, 768], BF16) for i in range(2)]
                kmT = [sb(f"kmT{j}", [128, 32]) for j in range(2)]
                gm8 = [sb(f"gm8_{j}", [128, 8, 32]) for j in range(2)]
                m8a = [sb(f"m8a{j}", [128, 8, 8]) for j in range(2)]
                sel8 = [sb(f"sel8_{j}", [128, 8, 32]) for j in range(2)]
                neg8 = [sb(f"neg8_{j}", [128, 8, 32], BF16) for j in range(2)]
                nst = [sb(f"nst{i}", [32, TB], BF16) for i in range(4)]
                NPA = 3
                pa = [es.enter_context(nc.psum_tensor(U(f"pa{i}"), [128, TB], F32)) for i in range(NPA)]
                pb = [es.enter_context(nc.psum_tensor(U(f"pb{i}"), [128, TB], F32)) for i in range(2)]
                pn = es.enter_context(nc.psum_tensor(U("pn"), [128, TB], F32))
                pgs = [es.enter_context(nc.psum_tensor(U(f"pg{i}"), [128, TB], F32)) for i in range(2)]
                with nc.Block() as block:
                    load_cast(P, win, win_d[l], 8, INW, "win", step=1280, col_major=True)
                    for j in range(2):
                        P.memset(kmT[j][:], 0.0, w=[f"kmT{j}"], eng="pool")
                    cnt = {"pa": 0, "pb": 0, "stg": 0, "qb": 0, "r": 0, "v": 0, "ns": 0}

                    def load_blk(i):
                        P.dma(xbs[i % 2][:], xTv[:, :, i * TB:(i + 1) * TB], w=[f"xb{i % 2}"])
                        P.dma(cosb[i % 2][:], cosT_d[:, i * TB:(i + 1) * TB], w=[f"cosb{i % 2}"])
                        P.dma(sinb[i % 2][:], sinT_d[:, i * TB:(i + 1) * TB], w=[f"sinb{i % 2}"])

                    load_blk(0)
                    for i in range(NB):
                        t0 = i * TB
                        if i + 1 < NB:
                            load_blk(i + 1)
                        xb = xbs[i % 2]; xn = f"xb{i % 2}"; hT = hTs[i % 2]; hn = f"hT{i % 2}"
                        cb_, cbn = cosb[i % 2], f"cosb{i % 2}"
                        sb_, sbn = sinb[i % 2], f"sinb{i % 2}"
                        norm_mod(P, xb, xn, hT, hn, A1, lambda kt: modv[:, l, kt:kt + 1], l, pn, tmps, sqs, rstd, lnv)
                        hres = [f"{hn}_{kt}" for kt in range(8)]

                        def proj(c0):
                            p_ = pa[cnt["pa"] % NPA]; pn_ = f"pa{cnt['pa'] % NPA}"; cnt["pa"] += 1
                            for kt in range(8):
                                P.mm(p_[:], win[:, kt, c0:c0 + 128], hT[:, kt, :], start=(kt == 0), stop=(kt == 7), r=[f"win_{kt}_{c0 // 1280}", hres[kt]], w=[pn_])
                            return p_, pn_

                        RENG = "dve" if "nopool" in SKIP else "pool"

                        def rope(p_, pn_, want_f32):
                            q_ = qb[cnt["qb"] % 2]; qn = f"qb{cnt['qb'] % 2}"; cnt["qb"] += 1
                            P.copy(q_[:], p_[:], r=[pn_], w=[qn], eng="act")
                            s_ = pb[cnt["pb"] % 2]; sn_ = f"pb{cnt['pb'] % 2}"; cnt["pb"] += 1
                            if "nosw" not in SKIP:
                                P.mm(s_[:], pswap[:], q_[:], r=[qn, "pswap"], w=[sn_])
                            else:
                                s_, sn_ = p_, pn_
                            k_ = cnt["r"] % 2; cnt["r"] += 1
                            P.tt(r1[k_][:], p_[:], cb_[:], ALU.mult, r=[pn_, cbn, qn], w=[f"r1_{k_}"])
                            P.tt(r2[k_][:], s_[:], sb_[:], ALU.mult, r=[sn_, sbn], w=[f"r2_{k_}"])
                            g_ = stg[cnt["stg"] % 4]; gn = f"stg{cnt['stg'] % 4}"; cnt["stg"] += 1
                            if want_f32:
                                P.tt(rf[k_][:], r1[k_][:], r2[k_][:], ALU.add, r=[f"r1_{k_}", f"r2_{k_}"], w=[f"rf{k_}"], eng=RENG)
                                P.copy(g_[:], rf[k_][:], r=[f"rf{k_}"], w=[gn], eng="act")
                                return rf[k_], f"rf{k_}", g_, gn
                            P.tt(g_[:], r1[k_][:], r2[k_][:], ALU.add, r=[f"r1_{k_}", f"r2_{k_}"], w=[gn], eng=RENG)
                            return None, None, g_, gn

                        for j in range(2):
                            p_, pn_ = proj(j * 128)
                            g_ = stg[cnt["stg"] % 4]; gn = f"stg{cnt['stg'] % 4}"; cnt["stg"] += 1
                            P.copy(g_[:], p_[:], r=[pn_], w=[gn], eng="act")
                            P.dma(uT_d[j * 128:(j + 1) * 128, t0:t0 + TB], g_[:], r=[gn], w=["uTd"])
                        for h in range(0 if "rope" in SKIP else 4):
                            p_, pn_ = proj(256 + h * 128)
                            _, _, g_, gn = rope(p_, pn_, False)
                            if "nodma" not in SKIP:
                                P.dma(dqT_d[h, :, t0:t0 + TB], g_[:], r=[gn], w=["dqTd"])
                        for h in range(0 if "rope" in SKIP else 4):
                            p_, pn_ = proj(768 + h * 128)
                            _, _, g_, gn = rope(p_, pn_, False)
                            if "nodma" not in SKIP:
                                P.dma(dkT_d[h, :, t0:t0 + TB], g_[:], r=[gn], w=["dkTd"])
                        for j in range(0 if "mk" in SKIP else 2):
                            p_, pn_ = proj(2048 + j * 128)
                            f_, fn_, g_, gn = rope(p_, pn_, True)
                            P.add("dve", lambda e, f_=f_, j=j, i=i: e.tensor_reduce(out=kmT[j][:, 2 * i:2 * i + 2], in_=f_[:].rearrange("p (b k) -> p b k", k=256), axis=AX.X, op=ALU.add),
                                  r=[fn_], w=[f"kmT{j}"])
                            P.ts(kmT[j][:, 2 * i:2 * i + 2], kmT[j][:, 2 * i:2 * i + 2], 1.0 / 256, None, ALU.mult, r=[f"kmT{j}"], w=[f"kmT{j}"])
                            for hh in range(2):
                                P.dma(mkT_d[2 * j + hh, :, t0:t0 + TB], g_[hh * 64:(hh + 1) * 64, :], r=[gn], w=["mkTd"])
                        pend = []
                        for j in range(0 if "mq" in SKIP else 2):
                            p_, pn_ = proj(1792 + j * 128)
                            f_, fn_, g_, gn = rope(p_, pn_, True)
                            for hh in range(2):
                                P.dma(mqT_d[2 * j + hh, 0:64, t0:t0 + TB], g_[hh * 64:(hh + 1) * 64, :], r=[gn], w=["mqTd"])
                            for hh in range(0 if "nogate" in SKIP else 2):
                                hs = slice(hh * 64, (hh + 1) * 64)
                                for tt in range(4):
                                    gcol = j * 128 + tt * 32
                                    P.mm(pgs[hh][:, gcol:gcol + 32], f_[hs, tt * 128:(tt + 1) * 128], kmT[j][hs, :], r=[fn_, f"kmT{j}"], w=[f"pg{hh}"])
                            gm_ = gm8[j]; gmn = f"gm8_{j}"
                            P.memset(gm_[:], -1e30, w=[gmn])
                            for hh in range(2):
                                for tt in range(4):
                                    k8 = hh * 4 + tt
                                    own = (t0 + tt * 128) // 256
                                    gcol = j * 128 + tt * 32
                                    if own > 0:
                                        P.copy(gm_[:, k8, 0:own], pgs[hh][:, gcol:gcol + own], r=[f"pg{hh}", gmn], w=[gmn + f"_{k8}"])
                            for hh in range(0 if "nochain" in SKIP else 2):
                                for tt in range(4):
                                    k8 = hh * 4 + tt
                                    own = (t0 + tt * 128) // 256
                                    P.add("dve", lambda e, k8=k8, j=j: e.max(out=m8a[j][:, k8, :], in_=gm8[j][:, k8, :]), r=[gmn, gmn + f"_{k8}"], w=[f"m8a{j}_{k8}"])
                                    P.tt(sel8[j][:, k8, :], gm_[:, k8, :], m8a[j][:, k8, 2:3].to_broadcast([128, 32]), ALU.is_ge, r=[gmn + f"_{k8}", f"m8a{j}_{k8}"], w=[f"sel8_{j}_{k8}"])
                                    P.ts(neg8[j][:, k8, :], sel8[j][:, k8, :], 30000.0, -30000.0, ALU.mult, ALU.add, r=[f"sel8_{j}_{k8}"], w=[f"neg8_{j}_{k8}"])
                                    P.memset(neg8[j][:, k8, own:own + 1], 0.0, r=[f"neg8_{j}_{k8}"], w=[f"neg8_{j}_{k8}"])
                            pend.append(j)
                        for tt in range(0 if "v" in SKIP else 4):
                            v_ = vst[cnt["v"] % 2]; vn = f"vst{cnt['v'] % 2}"; cnt["v"] += 1
                            p_ = pa[cnt["pa"] % NPA]; pn_ = f"pa{cnt['pa'] % NPA}"; cnt["pa"] += 1
                            for kt in range(8):
                                P.mm(p_[:], hT[:, kt, tt * 128:(tt + 1) * 128], win[:, kt, 1280:1792], start=(kt == 0), stop=(kt == 7), r=[f"win_{kt}_1", hres[kt]], w=[pn_])
                            P.copy(v_[:, 0:512], p_[:], r=[pn_], w=[vn + "a"], eng="act")
                            p2 = pa[cnt["pa"] % NPA]; pn2 = f"pa{cnt['pa'] % NPA}"; cnt["pa"] += 1
                            for kt in range(8):
                                P.mm(p2[:, 0:256], hT[:, kt, tt * 128:(tt + 1) * 128], win[:, kt, 2304:2560], start=(kt == 0), stop=(kt == 7), r=[f"win_{kt}_1", hres[kt]], w=[pn2])
                            P.copy(v_[:, 512:768], p2[:, 0:256], r=[pn2], w=[vn + "b"], eng=("act" if "vact" in SKIP else "dve"))
                            P.dma(dV_d[t0 + tt * 128:t0 + (tt + 1) * 128, :], v_[:, 0:512], r=[vn + "a"], w=["dVd"])
                            P.dma(mV_d[t0 + tt * 128:t0 + (tt + 1) * 128, :], v_[:, 512:768], r=[vn + "b"], w=["mVd"])
                        for j in ([] if "notr" in SKIP else pend):
                            for hh in range(2):
                                pgb = pgs[hh][:].bitcast(BF16)
                                for tt in range(4):
                                    tcol = 512 + tt * 128
                                    P.tr(pgb[0:32, tcol:tcol + 128], neg8[j][:, hh * 4 + tt, :], identb[:], r=[f"neg8_{j}_{hh * 4 + tt}", "identb"], w=[f"pg{hh}"])
                            for hh in range(2):
                                pgb = pgs[hh][:].bitcast(BF16)
                                ns_ = nst[cnt["ns"] % 4]; nsn = f"nst{cnt['ns'] % 4}"; cnt["ns"] += 1
                                P.copy(ns_[:], pgb[0:32, 512:1024], r=[f"pg{hh}"], w=[nsn])
                                P.dma(mqT_d[2 * j + hh, 64:96, t0:t0 + TB], ns_[:], r=[nsn], w=["mqTd"])
                    P.emit(block)

            if stop == "A":
                return nc
            with ExitStack() as es:
                sb = lambda n, s, d=F32: es.enter_context(nc.sbuf_tensor(U(n), list(s), d))
                uTs = [sb(f"uTs{h}", [128, S], BF16) for h in range(2)]
                are = sb("are", [128, 8]); aim = sb("aim", [128, 8]); ldt = sb("ldt", [128, 8])
                dt_ = sb("dt_", [128, 8]); mag = sb("mag", [128, 8]); th = sb("th", [128, 8])
                sth = sb("sth", [128, 8]); cth = sb("cth", [128, 8])
                s1 = sb("s1", [128, 8]); s2 = sb("s2", [128, 8]); s3 = sb("s3", [128, 8]); si = sb("si", [128, 8], I32)
                abr = sb("abr", [128, 8]); abi = sb("abi", [128, 8]); rden = sb("rden", [128, 8])
                kr = sb("kr", [128, 8]); ki = sb("ki", [128, 8]); nki = sb("nki", [128, 8])
                bre = sb("bre", [128, 8, 16]); bim = sb("bim", [128, 8, 16])
                cre = sb("cre", [128, 8, 16]); cim = sb("cim", [128, 8, 16])
                bbr = sb("bbr", [128, 8, 16]); bbi = sb("bbi", [128, 8, 16]); tma = sb("tma", [128, 16])
                X = [sb(f"X{i}", [128, 128]) for i in range(2)]
                BD = sb("BD", [128, 16, 128], BF16)
                CDm = sb("CD", [128, 16, 128], BF16)
                sdv = sb("sdv", [128, 2]); sgb = sb("sgb", [128, 2]); nsgb = sb("nsgb", [128, 2]); sng = sb("sng", [128, 2])
                gw = sb("gw", [128, 2, 256], BF16)
                jidi = sb("jidi", [128, TB], I32); jid = sb("jid", [128, TB])
                tA = sb("tA", [128, TB]); q1 = sb("q1", [128, TB]); q2 = sb("q2", [128, TB]); q3 = sb("q3", [128, TB]); qi = sb("qi", [128, TB], I32)
                Cs = sb("Cs", [128, 8, TB]); Sn = sb("Sn", [128, 8, TB])
                Hre = sb("Hre", [128, 8]); Him = sb("Him", [128, 8])
                NW = 3
                mt = [[sb(f"m{k}_{w}", [128, TB]) for k in range(4)] for w in range(NW)]
                bp = [[sb(f"bp{k}_{w}", [128, TB]) for k in range(2)] for w in range(NW)]
                gg = [[sb(f"g{k}_{w}", [128, TB]) for k in range(2)] for w in range(NW)]
                pp = [[sb(f"p{k}_{w}", [128, TB]) for k in range(4)] for w in range(NW)]
                hb = [[sb(f"hb{k}_{w}", [128, TB], BF16) for k in range(2)] for w in range(NW)]
                yv = [sb(f"yv{h}", [128, TB]) for h in range(2)]
                ya = sb("ya", [128, TB]); yb = sb("yb", [128, TB]); yc = sb("yc", [128, TB])
                gl = [sb(f"gl{h}", [128, TB]) for h in range(2)]
                glb = [sb(f"glb{h}", [128, TB], BF16) for h in range(2)]
                zz = [sb(f"zz{h}", [128, TB]) for h in range(2)]
                zsq = [sb(f"zsq{h}", [128, TB], BF16) for h in range(2)]
                rs_ = sb("rs_", [128, TB]); ln_ = sb("ln_", [128, TB])
                mo = [sb(f"mo{h}", [128, TB], BF16) for h in range(2)]
                pbu = [es.enter_context(nc.psum_tensor(U(f"pbu{i}"), [128, TB], F32)) for i in range(4)]
                py = [es.enter_context(nc.psum_tensor(U(f"py{i}"), [128, TB], F32)) for i in range(2)]
                pgx = es.enter_context(nc.psum_tensor(U("pgx"), [128, TB], F32))
                pn2 = es.enter_context(nc.psum_tensor(U("pn2"), [128, TB], F32))
                with nc.Block() as block:
                    for h in range(2):
                        P.dma(uTs[h][:], uT_d[h * 128:(h + 1) * 128, :], w=[f"uTs{h}"])
                    for t_, d_, n_ in ((are, sare_d, "are"), (aim, saim_d, "aim"), (ldt, sldt_d, "ldt"), (bre, sbre_d, "bre"), (bim, sbim_d, "bim"),
                                       (cre, scre_d, "cre"), (cim, scim_d, "cim"), (sdv, sd_d, "sdv"), (sgb, sgb_d, "sgb"), (sng, sng_d, "sng")):
                        P.dma(t_[:], d_[l], w=[n_])
                    load_cast(P, gw, sgw_d[l], 2, 256, "gw")
                    P.act(dt_[:], ldt[:], AF.Exp, r=["ldt"], w=["dt_"])
                    P.tt(s1[:], are[:], dt_[:], ALU.mult, r=["are", "dt_"], w=["s1"])
                    P.act(mag[:], s1[:], AF.Exp, r=["s1"], w=["mag"])
                    P.tt(th[:], aim[:], dt_[:], ALU.mult, r=["aim", "dt_"], w=["sang"])
                    sincos(P, th[:], 8, sth[:], cth[:], s1, s2, s3, si, "s", "sth", "cth")
                    P.tt(abr[:], mag[:], cth[:], ALU.mult, r=["mag", "cth"], w=["abr"])
                    P.tt(abi[:], mag[:], sth[:], ALU.mult, r=["mag", "sth"], w=["abi"])
                    P.tt(s1[:], are[:], are[:], ALU.mult, r=["are", "st1"], w=["s1b"])
                    P.tt(s2[:], aim[:], aim[:], ALU.mult, r=["aim", "st2"], w=["s2b"])
                    P.tt(s1[:], s1[:], s2[:], ALU.add, r=["s1b", "s2b"], w=["s1b"])
                    P.add("dve", lambda e: e.reciprocal(out=rden[:], in_=s1[:]), r=["s1b"], w=["rden"])
                    P.ts(s3[:], abr[:], -1.0, None, ALU.add, r=["abr", "st3"], w=["nr"])
                    P.tt(s1[:], s3[:], are[:], ALU.mult, r=["nr", "are", "rden"], w=["s1c"])
                    P.tt(s2[:], abi[:], aim[:], ALU.mult, r=["abi", "aim", "s2b"], w=["s2c"])
                    P.tt(s1[:], s1[:], s2[:], ALU.add, r=["s1c", "s2c"], w=["s1c"])
                    P.tt(kr[:], s1[:], rden[:], ALU.mult, r=["s1c", "rden"], w=["kr"])
                    P.tt(s1[:], abi[:], are[:], ALU.mult, r=["abi", "are", "kr"], w=["s1d"])
                    P.tt(s2[:], s3[:], aim[:], ALU.mult, r=["nr", "aim", "s1c"], w=["s2d"])
                    P.tt(s1[:], s1[:], s2[:], ALU.subtract, r=["s1d", "s2d"], w=["s1d"])
                    P.tt(ki[:], s1[:], rden[:], ALU.mult, r=["s1d", "rden"], w=["ki"])
                    P.ts(nki[:], ki[:], -1.0, None, ALU.mult, r=["ki"], w=["nki"])
                    P.ts(nsgb[:], sgb[:], -1.0, None, ALU.mult, r=["sgb"], w=["nsgb"])
                    P.memset(BD[:], 0.0, w=["BD"], eng="pool")
                    P.memset(CDm[:], 0.0, w=["CD"], eng="pool")
                    P.memset(Hre[:], 0.0, w=["Hre"], eng="pool")
                    P.memset(Him[:], 0.0, w=["Him"], eng="pool")
                    for st in range(8):
                        P.ts(tma[:], bre[:, st, :], kr[:, st:st + 1], None, ALU.mult, r=["bre", "kr", "bbr"], w=["tma"])
                        P.stt(bbr[:, st, :], bim[:, st, :], nki[:, st:st + 1], tma[:], ALU.mult, ALU.add, r=["bim", "nki", "tma"], w=["bbr"])
                        P.ts(tma[:], bim[:, st, :], kr[:, st:st + 1], None, ALU.mult, r=["bim", "kr", "bbr"], w=["tma"])
                        P.stt(bbi[:, st, :], bre[:, st, :], ki[:, st:st + 1], tma[:], ALU.mult, ALU.add, r=["bre", "ki", "tma"], w=["bbi"])
                    kx = 0
                    for st in range(8):
                        gl0 = (2 * st) % 8
                        for ri, bsrc, bn in ((0, bbr, "bbr"), (1, bbi, "bbi")):
                            X_ = X[kx % 2]; Xn = f"X{kx % 2}"; kx += 1
                            P.memset(X_[:], 0.0, w=[Xn], eng="pool")
                            P.copy(X_[0:64, gl0 * 16:(gl0 + 1) * 16], bsrc[0:64, st, :], r=[bn, Xn], w=[Xn])
                            P.copy(X_[64:128, (gl0 + 1) * 16:(gl0 + 2) * 16], bsrc[64:128, st, :], r=[bn, Xn], w=[Xn])
                            pt_ = pbu[kx % 4]; ptn = f"pbu{kx % 4}"
                            P.tr(pt_[:, 0:128], X_[:], ident[:], r=[Xn], w=[ptn])
                            P.copy(BD[:, st * 2 + ri, :], pt_[:, 0:128], r=[ptn, "BD"], w=["BD"], eng="act")
                        P.copy(CDm[0:64, st * 2, gl0 * 16:(gl0 + 1) * 16], cre[0:64, st, :], r=["cre", "CD"], w=["CD"])
                        P.copy(CDm[64:128, st * 2, (gl0 + 1) * 16:(gl0 + 2) * 16], cre[64:128, st, :], r=["cre", "CD"], w=["CD"])
                        P.ts(CDm[0:64, st * 2 + 1, gl0 * 16:(gl0 + 1) * 16], cim[0:64, st, :], -1.0, None, ALU.mult, r=["cim", "CD"], w=["CD"])
                        P.ts(CDm[64:128, st * 2 + 1, (gl0 + 1) * 16:(gl0 + 2) * 16], cim[64:128, st, :], -1.0, None, ALU.mult, r=["cim", "CD"], w=["CD"])
                    P.add("pool", lambda e: e.iota(jidi[:], pattern=[[1, TB]], base=1, channel_multiplier=0), w=["jidi"])
                    P.copy(jid[:], jidi[:], r=["jidi"], w=["jid"])
                    for st in range(8):
                        P.ts(tA[:], jid[:], th[:, st:st + 1], None, ALU.mult, r=["jid", "sang", "tsin", "tcos"], w=["tang"])
                        sincos(P, tA[:], TB, Sn[:, st, :], Cs[:, st, :], q1, q2, q3, qi, "t", "Sn", "Cs")
                    it = 0
                    for i in range(NB):
                        t0 = i * TB
                        for st in range(8):
                            w_ = it % NW; it += 1
                            half = st // 4
                            m = mt[w_]; mn = [f"m{k}_{w_}" for k in range(4)]
                            b_ = bp[w_]; bn_ = [f"bp{k}_{w_}" for k in range(2)]
                            g_ = gg[w_]; gn_ = [f"g{k}_{w_}" for k in range(2)]
                            p_ = pp[w_]; pn_ = [f"p{k}_{w_}" for k in range(4)]
                            h_ = hb[w_]; hn_ = [f"hb{k}_{w_}" for k in range(2)]
                            pr, prn = pbu[(2 * it) % 4], f"pbu{(2 * it) % 4}"
                            pi2, pin = pbu[(2 * it + 1) % 4], f"pbu{(2 * it + 1) % 4}"
                            P.mm(pr[:], BD[:, st * 2, :], uTs[half][:, t0:t0 + TB], r=["BD", f"uTs{half}"], w=[prn])
                            P.mm(pi2[:], BD[:, st * 2 + 1, :], uTs[half][:, t0:t0 + TB], r=["BD", f"uTs{half}"], w=[pin])
                            cs_, sn_ = Cs[:, st, :], Sn[:, st, :]
                            P.tt(m[0][:], pr[:], cs_, ALU.mult, r=[prn, "Cs"], w=[mn[0]])
                            P.tt(m[1][:], pi2[:], sn_, ALU.mult, r=[pin, "Sn"], w=[mn[1]])
                            P.tt(m[2][:], pi2[:], cs_, ALU.mult, r=[pin, "Cs"], w=[mn[2]])
                            P.tt(m[3][:], pr[:], sn_, ALU.mult, r=[prn, "Sn"], w=[mn[3]])
                            P.tt(b_[0][:], m[0][:], m[1][:], ALU.add, r=[mn[0], mn[1]], w=[bn_[0]], eng="pool")
                            P.tt(b_[1][:], m[2][:], m[3][:], ALU.subtract, r=[mn[2], mn[3]], w=[bn_[1]], eng="pool")
                            rbc = mag[:, st:st + 1].to_broadcast([128, TB])
                            P.add("dve", lambda e, o=g_[0], d1=b_[0], ini=Hre[:, st:st + 1], rbc=rbc: e.tensor_tensor_scan(out=o[:], data0=rbc, data1=d1[:], initial=ini, op0=ALU.mult, op1=ALU.add),
                                  r=[bn_[0], "mag", "Hre"], w=[gn_[0]])
                            P.add("dve", lambda e, o=g_[1], d1=b_[1], ini=Him[:, st:st + 1], rbc=rbc: e.tensor_tensor_scan(out=o[:], data0=rbc, data1=d1[:], initial=ini, op0=ALU.mult, op1=ALU.add),
                                  r=[bn_[1], "mag", "Him"], w=[gn_[1]])
                            P.tt(p_[0][:], g_[0][:], cs_, ALU.mult, r=[gn_[0], "Cs"], w=[pn_[0]])
                            P.tt(p_[1][:], g_[1][:], sn_, ALU.mult, r=[gn_[1], "Sn"], w=[pn_[1]])
                            P.tt(p_[2][:], g_[0][:], sn_, ALU.mult, r=[gn_[0], "Sn"], w=[pn_[2]], eng="pool")
                            P.tt(p_[3][:], g_[1][:], cs_, ALU.mult, r=[gn_[1], "Cs"], w=[pn_[3]], eng="pool")
                            P.tt(h_[0][:], p_[0][:], p_[1][:], ALU.subtract, r=[pn_[0], pn_[1]], w=[hn_[0]], eng="pool")
                            P.tt(h_[1][:], p_[2][:], p_[3][:], ALU.add, r=[pn_[2], pn_[3]], w=[hn_[1]], eng="pool")
                            P.tt(Hre[:, st:st + 1], p_[0][:, TB - 1:TB], p_[1][:, TB - 1:TB], ALU.subtract, r=[pn_[0], pn_[1], "Hre"], w=["Hre"])
                            P.tt(Him[:, st:st + 1], p_[2][:, TB - 1:TB], p_[3][:, TB - 1:TB], ALU.add, r=[pn_[2], pn_[3], "Him"], w=["Him"])
                            P.mm(py[half][:], CDm[:, st * 2, :], h_[0][:], start=(st % 4 == 0), stop=False, r=["CD", hn_[0]], w=[f"py{half}"])
                            P.mm(py[half][:], CDm[:, st * 2 + 1, :], h_[1][:], start=False, stop=(st % 4 == 3), r=["CD", hn_[1]], w=[f"py{half}"])
                        for h in range(2):
                            P.stt(yv[h][:], uTs[h][:, t0:t0 + TB], sdv[:, h:h + 1], py[h][:], ALU.mult, ALU.add, r=[f"uTs{h}", "sdv", f"py{h}"], w=[f"yv{h}"])
                            P.act(ya[:], yv[h][:], AF.Square, r=[f"yv{h}"], w=["ya"])
                            P.ts(yb[:], ya[:], 0.044715, 1.0, ALU.mult, ALU.add, r=["ya"], w=["yb"])
                            P.tt(yc[:], yb[:], yv[h][:], ALU.mult, r=["yb", f"yv{h}"], w=["yc"], eng="pool")
                            P.act(ya[:], yc[:], AF.Exp, r=["yc"], w=["ya"], scale=-1.5957691216)
                            P.ts(yb[:], ya[:], 1.0, None, ALU.add, r=["ya"], w=["yb"])
                            P.add("dve", lambda e: e.reciprocal(out=yc[:], in_=yb[:]), r=["yb"], w=["yc"])
                            P.tt(gl[h][:], yv[h][:], yc[:], ALU.mult, r=[f"yv{h}", "yc"], w=[f"gl{h}"], eng="pool")
                            P.copy(glb[h][:], gl[h][:], r=[f"gl{h}"], w=[f"glb{h}"], eng="act")
                        for oh in range(2):
                            for kh in range(2):
                                P.mm(pgx[:], gw[:, kh, oh * 128:(oh + 1) * 128], glb[kh][:], start=(kh == 0), stop=(kh == 1), r=[f"gw_{kh}_0", f"glb{kh}"], w=["pgx"])
                            P.act(ya[:], pgx[:], AF.Exp, r=["pgx", "nsgb"], w=["ya"], scale=-1.0, bias=nsgb[:, oh:oh + 1])
                            P.ts(yb[:], ya[:], 1.0, None, ALU.add, r=["ya"], w=["yb"])
                            P.add("dve", lambda e: e.reciprocal(out=yc[:], in_=yb[:]), r=["yb"], w=["yc"])
                            P.tt(zz[oh][:], gl[oh][:], yc[:], ALU.mult, r=[f"gl{oh}", "yc"], w=[f"zz{oh}"], eng="pool")
                            P.act(zsq[oh][:], zz[oh][:], AF.Square, r=[f"zz{oh}"], w=[f"zsq{oh}"])
                            P.mm(pn2[:], onesb[:], zsq[oh][:], start=(oh == 0), stop=(oh == 1), r=[f"zsq{oh}"], w=["pn2"])
                        P.act(ln_[:], pn2[:], AF.Ln, r=["pn2"], w=["ln_"], scale=1.0 / 256, bias=epsc[:, 0:1])
                        P.act(rs_[:], ln_[:], AF.Exp, r=["ln_"], w=["rs_"], scale=-0.5)
                        for oh in range(2):
                            P.stt(mo[oh][:], zz[oh][:], sng[:, oh:oh + 1], rs_[:], ALU.mult, ALU.mult, r=[f"zz{oh}", "sng", "rs_"], w=[f"mo{oh}"])
                            P.dma(mixT_d[oh * 128:(oh + 1) * 128, t0:t0 + TB], mo[oh][:], r=[f"mo{oh}"], w=["mixTd"])
                    P.emit(block)

            if stop == "S":
                return nc
            lam_init = 0.8 - 0.6 * math.exp(-0.3 * l)

            def attention(P, streams, nq_blocks, epilogue, psS, psAcc, Pt, cnt, acc_sets=1):
                ns = len(streams)
                LA = max(1, len(psS) // ns - 1)
                for qb_ in range(nq_blocks):
                    q0 = qb_ * TB
                    nkt = (q0 + TB) // 128
                    aoff = (qb_ % acc_sets) * (len(psAcc) // acc_sets)
                    held = {}

                    def qk(kt):
                        a_ = kt - q0 // 128
                        qs = max(a_, 0) * 128
                        for si_, st_ in enumerate(streams):
                            bi = cnt["s"] % len(psS); cnt["s"] += 1
                            ps_, psn = psS[bi], f"psS{bi}"
                            pt_, ptn = Pt[bi], f"Pt{bi}"
                            rows = st_["rows"]
                            P.mm(ps_[:, qs:TB], st_["kT"][rows, kt * 128:(kt + 1) * 128], st_["qT"][rows, q0 + qs:q0 + TB], r=[st_["kn"], st_["qn"]], w=[psn])
                            P.act(pt_[:, qs:TB], ps_[:, qs:TB], AF.Exp, r=[psn], w=[ptn], scale=0.125)
                            if a_ >= 0:
                                P.tt(pt_[:, qs:qs + 128], pt_[:, qs:qs + 128], tri[:], ALU.mult, r=[ptn, "tri"], w=[ptn], eng=("pool" if si_ % 2 else "dve"))
                            held[(kt, si_)] = (pt_, ptn, qs)

                    def av(kt):
                        for si_, st_ in enumerate(streams):
                            pt_, ptn, qs = held.pop((kt, si_))
                            for lf, ai in st_["vals"]:
                                P.mm(psAcc[aoff + ai][:, qs:TB], lf(kt), pt_[:, qs:TB], start=(kt == 0), stop=(kt == nkt - 1), r=[ptn, st_["vn"]], w=[f"acc{aoff + ai}"])

                    for kt in range(min(LA, nkt)):
                        qk(kt)
                    for kt in range(nkt):
                        if kt + LA < nkt:
                            qk(kt + LA)
                        av(kt)
                    epilogue(qb_, q0, aoff)

            with ExitStack() as es:
                sb = lambda n, s, d=F32: es.enter_context(nc.sbuf_tensor(U(n), list(s), d))
                qT = sb("qT", [128, S], BF16); kT = sb("kT", [128, S], BF16)
                Vt = sb("Vt", [128, NKT, 128], BF16)
                Pt = [sb(f"Pt{i}", [128, TB], BF16) for i in range(4)]
                lv = sb("lv", [128, 256]); lp = sb("lp", [128, 64]); ls = sb("ls", [128, 2]); nlam = sb("nlam", [128, 1])
                dsg = sb("dsg", [128, 1])
                rc0 = sb("rc0", [128, TB]); rc1 = sb("rc1", [128, TB]); o0 = sb("o0", [128, TB]); o1 = sb("o1", [128, TB])
                osq = sb("osq", [128, TB], BF16); dl = sb("dl", [128, TB]); dr = sb("dr", [128, TB])
                do_ = [sb(f"do{i}", [128, TB], BF16) for i in range(2)]
                psS = [es.enter_context(nc.psum_tensor(U(f"psS{i}"), [128, TB], F32)) for i in range(4)]
                psAcc = [es.enter_context(nc.psum_tensor(U(f"acc{i}"), [128, TB], F32)) for i in range(4)]
                with nc.Block() as block:
                    P.dma(lv[:], lam_d[l:l + 1, :].to_broadcast([128, 256]), w=["lv"])
                    P.dma(dsg[:], dsg_d[l], w=["dsg"])
                    for k2 in range(2):
                        P.tt(lp[:], lv[:, 128 * k2:128 * k2 + 64], lv[:, 128 * k2 + 64:128 * k2 + 128], ALU.mult, r=["lv", "ls"], w=["lp"])
                        P.add("dve", lambda e, k2=k2: e.tensor_reduce(out=ls[:, k2:k2 + 1], in_=lp[:], axis=AX.X, op=ALU.add), r=["lp"], w=["ls"])
                    P.act(ls[:], ls[:], AF.Exp, r=["ls"], w=["ls"])
                    P.stt(nlam[:], ls[:, 1:2], -lam_init, ls[:, 0:1], ALU.add, ALU.subtract, r=["ls"], w=["nlam"])
                    P.ts(dsg[:], dsg[:], 1.0 - lam_init, None, ALU.mult, r=["dsg"], w=["dsg"])
                    cnt = {"s": 0, "o": 0}
                    for h in range(4):
                        P.dma(qT[:], dqT_d[h], w=["qT"])
                        P.dma(kT[:], dkT_d[h], w=["kT"])
                        P.dma(Vt[:], dV_d.rearrange("(kt p) c -> p kt c", p=128)[:, :, h * 128:(h + 1) * 128], w=["Vt"])
                        streams = [dict(kT=kT, qT=qT, rows=slice(c * 64, (c + 1) * 64), kn="kT", qn="qT", vn="Vt",
                                        vals=[(lambda kt: Vt[:, kt, :], 2 * c), (lambda kt: onesb[:], 2 * c + 1)]) for c in range(2)]

                        def epi(qb_, q0, aoff, h=h):
                            P.add("dve", lambda e: e.reciprocal(out=rc0[:], in_=psAcc[1][:]), r=["acc1"], w=["rc0"])
                            P.add("dve", lambda e: e.reciprocal(out=rc1[:], in_=psAcc[3][:]), r=["acc3"], w=["rc1"])
                            P.tt(o0[:], psAcc[0][:], rc0[:], ALU.mult, r=["acc0", "rc0"], w=["o0"])
                            P.tt(o1[:], psAcc[2][:], rc1[:], ALU.mult, r=["acc2", "rc1"], w=["o1"])
                            P.stt(o0[:], o1[:], nlam[:, 0:1], o0[:], ALU.mult, ALU.add, r=["o1", "o0", "nlam"], w=["o0"])
                            P.act(osq[:], o0[:], AF.Square, r=["o0"], w=["osq"])
                            P.mm(psS[0][:], onesb[:], osq[:], r=["osq"], w=["psS0"])
                            P.act(dl[:], psS[0][:], AF.Ln, r=["psS0"], w=["dl"], scale=1.0 / 128, bias=epsc[:, 0:1])
                            P.act(dr[:], dl[:], AF.Exp, r=["dl"], w=["dr"], scale=-0.5)
                            d_ = do_[cnt["o"] % 2]; dn = f"do{cnt['o'] % 2}"; cnt["o"] += 1
                            P.stt(d_[:], o0[:], dsg[:, 0:1], dr[:], ALU.mult, ALU.mult, r=["o0", "dsg", "dr"], w=[dn])
                            P.dma(mixT_d[256 + h * 128:256 + (h + 1) * 128, q0:q0 + TB], d_[:], r=[dn], w=["mixTd"])

                        attention(P, streams, NB, epi, psS, psAcc, Pt, cnt)
                    P.emit(block)

            if stop == "D":
                return nc
            with ExitStack() as es:
                sb = lambda n, s, d=F32: es.enter_context(nc.sbuf_tensor(U(n), list(s), d))
                qT = sb("mqTs", [96, S], BF16); kT = sb("mkTs", [96, S], BF16)
                Vt = sb("mVt", [128, NKT, 128], BF16)
                onesS = sb("onesS", [96, S], BF16); tmpS = sb("tmpS", [96, S], BF16)
                Pt = [sb(f"Pt{i}", [128, TB], BF16) for i in range(4)]
                mng = sb("mngs", [64, 1])
                osb = sb("osb", [128, TB]); rc0 = sb("mrc", [64, TB]); o0 = sb("mo0", [64, TB])
                osq = sb("mosq", [64, TB], BF16); dl = sb("mdl", [64, TB]); dr = sb("mdr", [64, TB])
                do_ = [sb(f"mdo{i}", [64, TB], BF16) for i in range(2)]
                psS = [es.enter_context(nc.psum_tensor(U(f"psS{i}"), [128, TB], F32)) for i in range(4)]
                psAcc = [es.enter_context(nc.psum_tensor(U(f"acc{i}"), [128, TB], F32)) for i in range(2)]
                pmv = es.enter_context(nc.psum_tensor(U("pmv"), [128, TB], F32))
                pnm = es.enter_context(nc.psum_tensor(U("pnm"), [128, TB], F32))
                with nc.Block() as block:
                    P.dma(mng[:], mng_d[l], w=["mng"])
                    P.memset(Vt[:], 1.0, w=["Vt"], eng="pool")
                    P.memset(onesS[64:96, :], 1.0, w=["onesS"], eng="pool")
                    P.add("pool", lambda e: e.affine_select(out=tmpS[64:96, :], in_=onesS[64:96, :], pattern=[[1, S]], compare_op=ALU.is_ge, fill=0.0, base=0, channel_multiplier=-256), r=["onesS"], w=["tmpS"])
                    P.add("pool", lambda e: e.affine_select(out=kT[64:96, :], in_=tmpS[64:96, :], pattern=[[-1, S]], compare_op=ALU.is_ge, fill=0.0, base=255, channel_multiplier=256), r=["tmpS"], w=["kT1h"])
                    cnt = {"s": 0, "o": 0, "a": 0}
                    for h in range(4):
                        P.dma(qT[:], mqT_d[h], w=["qT"])
                        P.dma(kT[0:64, :], mkT_d[h], r=["kT1h"], w=["kT"])
                        P.dma(Vt[:, :, 0:64], mV_d.rearrange("(kt p) c -> p kt c", p=128)[:, :, h * 64:(h + 1) * 64], w=["Vt"])
                        streams = [dict(kT=kT, qT=qT, rows=slice(0, 96), kn="kT", qn="qT", vn="Vt", vals=[(lambda kt: Vt[:, kt, :], 0)])]

                        def epi(qb_, q0, aoff, h=h):
                            P.copy(osb[:], psAcc[aoff][:], r=[f"acc{aoff}"], w=["osb"], eng="act")
                            P.mm(pmv[0:64, :], ident[:, 64:128], osb[:], r=["osb"], w=["pmv"])
                            P.add("dve", lambda e: e.reciprocal(out=rc0[:], in_=pmv[0:64, :]), r=["pmv"], w=["rc0"])
                            P.tt(o0[:], osb[0:64, :], rc0[:], ALU.mult, r=["osb", "rc0"], w=["o0"])
                            P.act(osq[:], o0[:], AF.Square, r=["o0"], w=["osq"])
                            P.mm(pnm[0:64, :], onesb[0:64, 0:64], osq[:], r=["osq"], w=["pnm"])
                            P.act(dl[:], pnm[0:64, :], AF.Ln, r=["pnm"], w=["dl"], scale=1.0 / 64, bias=epsc[0:64, 0:1])
                            P.act(dr[:], dl[:], AF.Exp, r=["dl"], w=["dr"], scale=-0.5)
                            d_ = do_[cnt["o"] % 2]; dn = f"mdo{cnt['o'] % 2}"; cnt["o"] += 1
                            P.stt(d_[:], o0[:], mng[:, 0:1], dr[:], ALU.mult, ALU.mult, r=["o0", "mng", "dr"], w=[dn])
                            P.dma(mixT_d[768 + h * 64:768 + (h + 1) * 64, q0:q0 + TB], d_[:], r=[dn], w=["mixTd"])

                        attention(P, streams, NB, epi, psS, psAcc, Pt, cnt, acc_sets=2)
                    P.emit(block)

            if stop == "M":
                return nc
            with ExitStack() as es:
                sb = lambda n, s, d=F32: es.enter_context(nc.sbuf_tensor(U(n), list(s), d))
                wo = sb("wo", [128, 8, D], BF16)
                xbs = [sb(f"xb{i}", [128, 8, TB]) for i in range(2)]
                mxs = [sb(f"mx{i}", [128, 8, TB], BF16) for i in range(2)]
                pc = [es.enter_context(nc.psum_tensor(U(f"pc{i}"), [128, TB], F32)) for i in range(4)]
                with nc.Block() as block:
                    load_cast(P, wo, wout_d[l], 8, D, "wo", step=1024)
                    mixv = mixT_d.rearrange("(ft p) t -> p ft t", p=128)

                    def ld(i):
                        P.dma(xbs[i % 2][:], xTv[:, :, i * TB:(i + 1) * TB], r=["xTd"], w=[f"xb{i % 2}_{ft}" for ft in range(8)])
                        P.dma(mxs[i % 2][:], mixv[:, :, i * TB:(i + 1) * TB], w=[f"mx{i % 2}"])
                    ld(0)
                    k = 0
                    for i in range(NB):
                        if i + 1 < NB:
                            ld(i + 1)
                        xb = xbs[i % 2]; mx = mxs[i % 2]
                        for fo in range(8):
                            p_ = pc[k % 4]; pn_ = f"pc{k % 4}"; k += 1
                            for kt in range(8):
                                P.mm(p_[:], wo[:, kt, fo * 128:(fo + 1) * 128], mx[:, kt, :], start=(kt == 0), stop=(kt == 7), r=[f"wo_{kt}_0", f"mx{i % 2}"], w=[pn_])
                            P.stt(xb[:, fo, :], p_[:], modv[:, l, 16 + fo:17 + fo], xb[:, fo, :], ALU.mult, ALU.add, r=[pn_, f"xb{i % 2}_{fo}"], w=[f"xb{i % 2}_{fo}"])
                        P.dma(xTv[:, :, i * TB:(i + 1) * TB], xb[:], r=[f"xb{i % 2}_{ft}" for ft in range(8)], w=["xTd"])
                    P.emit(block)

            if stop == "C1":
                return nc
            with ExitStack() as es:
                sb = lambda n, s, d=F32: es.enter_context(nc.sbuf_tensor(U(n), list(s), d))
                w1 = sb("w1", [128, 8, DFF], BF16)
                w2 = sb("w2", [128, 32, D], BF16)
                xb = sb("xb", [128, 8, TB])
                hT = sb("hT", [128, 8, TB], BF16)
                hid = sb("hid", [128, 32, TB], BF16)
                sqs = [sb(f"sq{i}", [128, TB], BF16) for i in range(2)]
                tmps = [sb(f"nt{i}", [128, TB]) for i in range(2)]
                rstd = sb("rstd", [128, TB]); lnv = sb("lnv", [128, TB])
                rl = [sb(f"rl{i}", [128, TB]) for i in range(2)]
                pc = [es.enter_context(nc.psum_tensor(U(f"pc{i}"), [128, TB], F32)) for i in range(5)]
                pd = [es.enter_context(nc.psum_tensor(U(f"pd{i}"), [128, TB], F32)) for i in range(2)]
                pn = es.enter_context(nc.psum_tensor(U("pn"), [128, TB], F32))
                with nc.Block() as block:
                    load_cast(P, w1, w1_d[l], 8, DFF, "w1", step=1024, col_major=True)
                    load_cast(P, w2, w2_d[l], 32, D, "w2", step=1024)
                    k = 0; k2 = 0
                    for i in range(NB):
                        P.dma(xb[:], xTv[:, :, i * TB:(i + 1) * TB], r=["xTd"], w=[f"xb_{ft}" for ft in range(8)] + ["xb"])
                        norm_mod(P, xb, "xb", hT, "hT", A2, lambda kt: modv[:, l, 24 + kt:25 + kt], l, pn, tmps, sqs, rstd, lnv)
                        hres = [f"hT_{kt}" for kt in range(8)]
                        for ft in range(32):
                            p_ = pc[k % 5]; pn_ = f"pc{k % 5}"; r_ = rl[k % 2]; rn = f"rl{k % 2}"; k += 1
                            for kt in range(8):
                                P.mm(p_[:], w1[:, kt, ft * 128:(ft + 1) * 128], hT[:, kt, :], start=(kt == 0), stop=(kt == 7), r=[f"w1_{kt}_{ft // 8}", hres[kt]], w=[pn_])
                            P.act(r_[:], p_[:], AF.Relu, r=[pn_], w=[rn])
                            P.tt(hid[:, ft, :], r_[:], r_[:], ALU.mult, r=[rn], w=[f"hid{ft}"], eng=("pool" if ft % 2 else "dve"))
                        for fo in range(8):
                            p_ = pd[k2 % 2]; pn_ = f"pd{k2 % 2}"; k2 += 1
                            for ft in range(32):
                                P.mm(p_[:], w2[:, ft, fo * 128:(fo + 1) * 128], hid[:, ft, :], start=(ft == 0), stop=(ft == 31), r=[f"w2_{ft}_0", f"hid{ft}"], w=[pn_])
                            P.stt(xb[:, fo, :], p_[:], modv[:, l, 40 + fo:41 + fo], xb[:, fo, :], ALU.mult, ALU.add, r=[pn_, "xb", f"xb_{fo}"], w=[f"xb_{fo}"])
                        P.dma(xTv[:, :, i * TB:(i + 1) * TB], xb[:], r=[f"xb_{ft}" for ft in range(8)], w=["xTd", "xb"])
                    P.emit(block)

        if stop == "C2":
            return nc
        with ExitStack() as es:
            sb = lambda n, s, d=F32: es.enter_context(nc.sbuf_tensor(U(n), list(s), d))
            xbs = [sb(f"xb{i}", [128, 8, TB]) for i in range(2)]
            yn = sb("yn", [128, 8, TB])
            sqs = [sb(f"sq{i}", [128, TB], BF16) for i in range(2)]
            rstd = sb("rstd", [128, TB]); lnv = sb("lnv", [128, TB])
            ost = [sb(f"ost{i}", [128, D]) for i in range(2)]
            pt = [es.enter_context(nc.psum_tensor(U(f"pt{i}"), [128, TB], F32)) for i in range(6)]
            pn = es.enter_context(nc.psum_tensor(U("pn"), [128, TB], F32))
            with nc.Block() as block:
                P.dma(xbs[0][:], xTv[:, :, 0:TB], w=["xb0"])
                k = 0; ko = 0
                for i in range(NB):
                    if i + 1 < NB:
                        P.dma(xbs[(i + 1) % 2][:], xTv[:, :, (i + 1) * TB:(i + 2) * TB], w=[f"xb{(i + 1) % 2}"])
                    xb = xbs[i % 2]; xn = f"xb{i % 2}"
                    for kt in range(8):
                        sq = sqs[kt % 2]; sqn = f"sq{kt % 2}"
                        P.act(sq[:], xb[:, kt, :], AF.Square, r=[xn], w=[sqn])
                        P.mm(pn[:], onesb[:], sq[:], start=(kt == 0), stop=(kt == 7), r=[sqn], w=["pn"])
                    P.act(lnv[:], pn[:], AF.Ln, r=["pn"], w=["lnv"], scale=1.0 / D, bias=epsc[:, 0:1])
                    P.act(rstd[:], lnv[:], AF.Exp, r=["lnv"], w=["rstd"], scale=-0.5)
                    for kt in range(8):
                        P.stt(yn[:, kt, :], xb[:, kt, :], fgT[:, kt:kt + 1], rstd[:], ALU.mult, ALU.mult, r=[xn, "rstd"], w=[f"yn{kt}"])
                    for tt in range(4):
                        o_ = ost[ko % 2]; on = f"ost{ko % 2}"; ko += 1
                        for hf in range(2):
                            p_ = pt[k % 6]; pn_ = f"pt{k % 6}"; k += 1
                            for kk in range(4):
                                kt = hf * 4 + kk
                                P.tr(p_[:, kk * 128:(kk + 1) * 128], yn[:, kt, tt * 128:(tt + 1) * 128], ident[:], r=[f"yn{kt}"], w=[pn_])
                            P.copy(o_[:, hf * 512:(hf + 1) * 512], p_[:], r=[pn_], w=[f"{on}_{hf}"], eng=("act" if hf else "dve"))
                        P.dma(out_d[i * TB + tt * 128:i * TB + (tt + 1) * 128, :], o_[:], r=[f"{on}_0", f"{on}_1"], w=["outd"])
                P.emit(block)
    return nc


def _layout_inputs(inp, S):
    f = lambda a: np.ascontiguousarray(a, dtype=np.float32)
    col8 = lambda v: f(np.asarray(v).reshape(8, 128).T)
    cst = np.zeros((128, 2), np.float32)
    inv = (ROPE_THETA ** (-np.arange(0, 16, 2, dtype=np.float32) / 16)).astype(np.float32)
    for s0 in (0, 64):
        for d in range(16):
            cst[s0 + d, 0] = inv[d % 8]
            cst[s0 + d, 1] = -1.0 if d < 8 else 1.0
    L = DEPTH
    shared = {
        "cst": cst,
        "w_ada": f(inp["w_ada"]),
        "b_adaT": f(np.asarray(inp["b_ada"]).reshape(L, 48, 128).transpose(0, 2, 1)),
        "n1gT": f(np.asarray(inp["norm1_g"]).reshape(L, 8, 128).transpose(0, 2, 1)),
        "n2gT": f(np.asarray(inp["norm2_g"]).reshape(L, 8, 128).transpose(0, 2, 1)),
        "fgT": col8(inp["final_g"]),
        "w_in": f(inp["w_in"]), "w_out": f(inp["w_out"]), "mlp_w1": f(inp["mlp_w1"]), "mlp_w2": f(inp["mlp_w2"]),
        "s_are": f(np.asarray(inp["ssm_a_re"]).reshape(L, 8, 128).transpose(0, 2, 1)),
        "s_aim": f(np.asarray(inp["ssm_a_im"]).reshape(L, 8, 128).transpose(0, 2, 1)),
        "s_ldt": f(np.repeat(np.asarray(inp["ssm_log_dt"]).reshape(L, 8, 2), 64, axis=2).transpose(0, 2, 1)),
        "s_bre": f(np.asarray(inp["ssm_b_re"]).reshape(L, 8, 2, 64, 16).transpose(0, 2, 3, 1, 4).reshape(L, 128, 8, 16)),
        "s_bim": f(np.asarray(inp["ssm_b_im"]).reshape(L, 8, 2, 64, 16).transpose(0, 2, 3, 1, 4).reshape(L, 128, 8, 16)),
        "s_cre": f(np.asarray(inp["ssm_c_re"]).reshape(L, 8, 2, 16, 64).transpose(0, 2, 4, 1, 3).reshape(L, 128, 8, 16)),
        "s_cim": f(np.asarray(inp["ssm_c_im"]).reshape(L, 8, 2, 16, 64).transpose(0, 2, 4, 1, 3).reshape(L, 128, 8, 16)),
        "s_d": f(np.asarray(inp["ssm_d"]).reshape(L, 2, 128).transpose(0, 2, 1)),
        "s_gw": f(inp["ssm_glu_w"]),
        "s_gb": f(np.asarray(inp["ssm_glu_b"]).reshape(L, 2, 128).transpose(0, 2, 1)),
        "s_ng": f(np.asarray(inp["ssm_norm_g"]).reshape(L, 2, 128).transpose(0, 2, 1)),
        "lamv": f(np.concatenate([np.asarray(inp[k]) for k in ("diff_lq1", "diff_lk1", "diff_lq2", "diff_lk2")], axis=1)),
        "dsg": f(np.asarray(inp["diff_subln_g"]).reshape(L, 128, 1)),
        "mng": f(np.asarray(inp["moba_norm_g"]).reshape(L, 64, 1)),
    }
    x = np.asarray(inp["x"]); c = np.asarray(inp["c"]); pos = np.asarray(inp["positions"])
    maps = []
    for core in range(8):
        b = core % x.shape[0]
        m = dict(shared)
        m["x"] = f(x[b, :S])
        m["pos"] = np.ascontiguousarray(pos[b:b + 1, :S], dtype=np.int32)
        m["cT"] = col8(c[b])
        maps.append(m)
    return maps


def kernel(**inputs):
    S = SEQ
    nc = build(S)
    maps = _layout_inputs(inputs, S)
    res = run_bass_kernel_spmd(nc, maps, core_ids=list(range(8)))
    B = np.asarray(inputs["x"]).shape[0]
    return np.stack([np.asarray(res.results[b]["out"], dtype=np.float32) for b in range(B)], axis=0)
```

```python
import math
from contextlib import ExitStack

import numpy as np
import concourse.bass as bass
import concourse.mybir as mybir
from concourse.bass_utils import run_bass_kernel_spmd

F32 = mybir.dt.float32
BF16 = mybir.dt.bfloat16
I32 = mybir.dt.int32
ALU = mybir.AluOpType
AF = mybir.ActivationFunctionType
AX = mybir.AxisListType

D = 1024
SEQ = 8192
DEPTH = 2
DFF = 4096
INW = 2560
TB = 512
EPS = 1e-6
ROPE_THETA = 500000.0

ENGS = ("pe", "act", "dve", "pool", "sp")
PSUM_PREFIX = ("pa", "pb", "pn", "pg", "pst", "modps", "pbu", "py", "psS", "acc", "pmv", "pnm", "pc", "pd", "pt")


class Prog:
    NDMA = 6

    def __init__(self, nc, sems):
        self.nc = nc
        self.sems = sems
        self.cnt = {e: 0 for e in ENGS}
        self.dcnt = {}
        self.drot = {e: 0 for e in ENGS}
        self.reset()

    def reset(self):
        self.ops = {e: [] for e in ENGS}
        self.last_w = {}
        self.readers = {}

    def add(self, eng, fn, r=(), w=(), dma=False):
        deps = []
        for x in r:
            if x in self.last_w:
                deps.append(self.last_w[x])
            if x.startswith(PSUM_PREFIX):
                deps.extend(o for o in self.readers.get(x, ()) if o["eng"] != eng)
        for x in w:
            if x in self.last_w:
                deps.append(self.last_w[x])
            deps.extend(self.readers.get(x, ()))
        op = {"fn": fn, "deps": [], "dma": dma, "sig": False, "eng": eng, "val": None, "sem": None}
        for d in deps:
            if d is op:
                continue
            if d["eng"] == eng and not d["dma"] and not dma and eng == "pe":
                continue
            if not any(d is x for x in op["deps"]):
                op["deps"].append(d)
                d["sig"] = True
        self.ops[eng].append(op)
        for x in r:
            self.readers.setdefault(x, []).append(op)
        for x in w:
            self.last_w[x] = op
            self.readers[x] = []
        return op

    def emit(self, block):
        nc = self.nc
        for e in ENGS:
            for op in self.ops[e]:
                if op["dma"]:
                    k = ("dma", e, self.drot[e] % self.NDMA)
                    self.drot[e] += 1
                    op["sem"] = k
                    op["prev"] = self.dcnt.get(k, 0)
                    self.dcnt[k] = op["prev"] + 16
                    op["val"] = self.dcnt[k]
                elif op["sig"]:
                    self.cnt[e] += 1
                    op["sem"] = e
                    op["val"] = self.cnt[e]
        final_dma = dict(self.dcnt)
        sems = self.sems
        ops = self.ops

        def run(e, eng):
            waited = {}
            for op in ops[e]:
                need = {}
                for d in op["deps"]:
                    k = d["sem"]
                    if d["val"] > need.get(k, 0):
                        need[k] = d["val"]
                if op["dma"] and op["prev"] > 0:
                    k = op["sem"]
                    need[k] = max(need.get(k, 0), op["prev"])
                for k, v in need.items():
                    if waited.get(k, 0) < v:
                        eng.wait_ge(sems[k], v)
                        waited[k] = v
                inst = op["fn"](eng)
                if op["dma"]:
                    inst.then_inc(sems[op["sem"]], 16)
                elif op["sig"]:
                    inst.then_inc(sems[op["sem"]], 1)
            if e == "sp":
                for k, v in final_dma.items():
                    if v > 0 and waited.get(k, 0) < v:
                        eng.wait_ge(sems[k], v)

        for e, starter in (("pe", block.tensor), ("act", block.scalar), ("dve", block.vector), ("pool", block.gpsimd), ("sp", block.sync)):
            if ops[e] or e == "sp":
                starter(lambda eng, e=e: run(e, eng))

        self.reset()

    def dma(self, out, in_, r=(), w=(), q="sp", **kw):
        return self.add(q, lambda eng: eng.dma_start(out=out, in_=in_, **kw), r=r, w=w, dma=True)

    def mm(self, out, lhsT, rhs, start=True, stop=True, r=(), w=(), **kw):
        return self.add("pe", lambda eng: eng.matmul(out, lhsT, rhs, start=start, stop=stop, **kw), r=r, w=w)

    def tr(self, out, in_, ident, r=(), w=()):
        return self.add("pe", lambda eng: eng.transpose(out, in_, ident), r=r, w=w)

    def act(self, out, in_, func, r=(), w=(), **kw):
        return self.add("act", lambda eng: eng.activation(out=out, in_=in_, func=func, **kw), r=r, w=w)

    def tt(self, out, in0, in1, op, r=(), w=(), eng="dve"):
        return self.add(eng, lambda e: e.tensor_tensor(out=out, in0=in0, in1=in1, op=op), r=r, w=w)

    def ts(self, out, in0, s1, s2, op0, op1=None, r=(), w=(), eng="dve", **kw):
        if op1 is None:
            return self.add(eng, lambda e: e.tensor_scalar(out=out, in0=in0, scalar1=s1, scalar2=None, op0=op0, **kw), r=r, w=w)
        return self.add(eng, lambda e: e.tensor_scalar(out=out, in0=in0, scalar1=s1, scalar2=s2, op0=op0, op1=op1, **kw), r=r, w=w)

    def stt(self, out, in0, scalar, in1, op0, op1, r=(), w=()):
        return self.add("dve", lambda e: e.scalar_tensor_tensor(out=out, in0=in0, scalar=scalar, in1=in1, op0=op0, op1=op1), r=r, w=w)

    def copy(self, out, in_, r=(), w=(), eng="dve"):
        if eng == "act":
            return self.add("act", lambda e: e.copy(out=out, in_=in_), r=r, w=w)
        return self.add(eng, lambda e: e.tensor_copy(out=out, in_=in_), r=r, w=w)

    def memset(self, ap, val, r=(), w=(), eng="dve"):
        return self.add(eng, lambda e: e.memset(ap, val), r=r, w=w)


PI = math.pi
SKIP = set()
TWO_PI = 2.0 * math.pi
CW1 = 6.28125
CW2 = TWO_PI - 6.28125
PI_LO = 3.141592


def sincos(P, ang, n, out_sin, out_cos, t1, t2, t3, ti, tag, rsin, rcos, np_=128):
    a = lambda nm: tag + nm
    sl = lambda t: t[0:np_, 0:n]
    P.ts(sl(t1), ang, 1.0 / TWO_PI, None, ALU.mult, r=[a("ang")], w=[a("t1")])
    P.copy(sl(ti), sl(t1), r=[a("t1")], w=[a("ti")])
    P.copy(sl(t1), sl(ti), r=[a("ti")], w=[a("t1")])
    P.stt(sl(t2), sl(t1), -CW1, ang, ALU.mult, ALU.add, r=[a("t1"), a("ang")], w=[a("t2")])
    P.stt(sl(t3), sl(t1), -CW2, sl(t2), ALU.mult, ALU.add, r=[a("t1"), a("t2")], w=[a("t3")])
    P.ts(sl(t1), sl(t3), PI, -TWO_PI, ALU.is_gt, ALU.mult, r=[a("t3")], w=[a("t1")])
    P.tt(sl(t2), sl(t3), sl(t1), ALU.add, r=[a("t3"), a("t1")], w=[a("t2")])
    P.ts(sl(t1), sl(t2), -PI, TWO_PI, ALU.is_lt, ALU.mult, r=[a("t2")], w=[a("t1")])
    P.tt(sl(t3), sl(t2), sl(t1), ALU.add, r=[a("t2"), a("t1")], w=[a("t3")])
    P.ts(sl(t1), sl(t3), PI_LO, -PI_LO, ALU.min, ALU.max, r=[a("t3")], w=[a("t1")])
    P.act(out_sin, sl(t1), AF.Sin, r=[a("t1")], w=[rsin])
    P.ts(sl(t2), sl(t3), PI / 2, None, ALU.add, r=[a("t3")], w=[a("t2")])
    P.ts(sl(t1), sl(t2), PI, -TWO_PI, ALU.is_gt, ALU.mult, r=[a("t2"), a("t1")], w=[a("t1")])
    P.tt(sl(t3), sl(t2), sl(t1), ALU.add, r=[a("t2"), a("t1")], w=[a("t3")])
    P.ts(sl(t2), sl(t3), PI_LO, -PI_LO, ALU.min, ALU.max, r=[a("t3")], w=[a("t2")])
    P.act(out_cos, sl(t2), AF.Sin, r=[a("t2")], w=[rcos])


def build(S, dbg=False, nl=DEPTH, stop=None):
    NB = S // TB
    NKT = S // 128
    nc = bass.Bass("TRN2", target_bir_lowering=False)
    _uid = [0]

    def U(n):
        _uid[0] += 1
        return f"{n}_u{_uid[0]}"

    din = lambda n, s, d=F32: nc.dram_tensor(n, list(s), d, kind="ExternalInput").ap()
    dscr = lambda n, s, d=F32: nc.dram_tensor(n, list(s), d, kind=("ExternalOutput" if dbg else "Internal")).ap()
    x_d = din("x", [S, D])
    out_d = nc.dram_tensor("out", [S, D], F32, kind="ExternalOutput").ap()
    pos_d = din("pos", [1, S], I32)
    cT_d = din("cT", [128, 8])
    cst_d = din("cst", [128, 2])
    wada_d = din("w_ada", [DEPTH, D, 6 * D])
    bada_d = din("b_adaT", [DEPTH, 128, 48])
    n1g_d = din("n1gT", [DEPTH, 128, 8])
    n2g_d = din("n2gT", [DEPTH, 128, 8])
    fg_d = din("fgT", [128, 8])
    win_d = din("w_in", [DEPTH, D, INW])
    wout_d = din("w_out", [DEPTH, D, D])
    w1_d = din("mlp_w1", [DEPTH, D, DFF])
    w2_d = din("mlp_w2", [DEPTH, DFF, D])
    sare_d = din("s_are", [DEPTH, 128, 8])
    saim_d = din("s_aim", [DEPTH, 128, 8])
    sldt_d = din("s_ldt", [DEPTH, 128, 8])
    sbre_d = din("s_bre", [DEPTH, 128, 8, 16])
    sbim_d = din("s_bim", [DEPTH, 128, 8, 16])
    scre_d = din("s_cre", [DEPTH, 128, 8, 16])
    scim_d = din("s_cim", [DEPTH, 128, 8, 16])
    sd_d = din("s_d", [DEPTH, 128, 2])
    sgw_d = din("s_gw", [DEPTH, 256, 256])
    sgb_d = din("s_gb", [DEPTH, 128, 2])
    sng_d = din("s_ng", [DEPTH, 128, 2])
    lam_d = din("lamv", [DEPTH, 256])
    dsg_d = din("dsg", [DEPTH, 128, 1])
    mng_d = din("mng", [DEPTH, 64, 1])
    xT_d = dscr("xT", [D, S])
    cosT_d = dscr("cosT", [128, S])
    sinT_d = dscr("sinT", [128, S])
    uT_d = dscr("uT", [256, S], BF16)
    dqT_d = dscr("dqT", [4, 128, S], BF16)
    dkT_d = dscr("dkT", [4, 128, S], BF16)
    dV_d = dscr("dV", [S, 512], BF16)
    mqT_d = dscr("mqT", [4, 96, S], BF16)
    mkT_d = dscr("mkT", [4, 64, S], BF16)
    mV_d = dscr("mV", [S, 256], BF16)
    mixT_d = dscr("mixT", [D, S], BF16)
    dbgk_d = dscr("dbgk", [128, 32]); dbgg_d = dscr("dbgg", [128, 32]); dbgm_d = dscr("dbgm", [128, 8]); dbgs_d = dscr("dbgs", [128, 32])

    with ExitStack() as top:
        sems = {}
        for e in ENGS:
            sems[e] = top.enter_context(nc.semaphore("s_" + e))
            for i in range(Prog.NDMA):
                sems[("dma", e, i)] = top.enter_context(nc.semaphore(f"d_{e}_{i}"))
        P = Prog(nc, sems)
        gsb = lambda n, s, d=F32: top.enter_context(nc.sbuf_tensor(U(n), list(s), d))
        ident = gsb("ident", [128, 128])
        identb = gsb("identb", [128, 128], BF16)
        onesb = gsb("onesb", [128, 128], BF16)
        tri = gsb("tri", [128, 128], BF16)
        pswap = gsb("pswap", [128, 128], BF16)
        epsc = gsb("epsc", [128, 1])
        cst = gsb("cstc", [128, 2])
        modv = gsb("modv", [128, DEPTH, 48])
        A1 = gsb("A1", [128, DEPTH, 8])
        A2 = gsb("A2", [128, DEPTH, 8])
        fgT = gsb("fgTs", [128, 8])

        with ExitStack() as es:
            sb = lambda n, s, d=F32: es.enter_context(nc.sbuf_tensor(U(n), list(s), d))
            onesf = sb("onesf", [128, 128])
            b1 = sb("b1", [128, 128])
            b2 = sb("b2", [128, 128])
            cT = sb("cTs", [128, 8])
            scT = sb("scT", [128, 8])
            tmp8 = sb("tmp8", [128, 8])
            wa = [sb(f"wa{i}", [128, 8, 512]) for i in range(2)]
            bada = sb("bada", [128, DEPTH, 48])
            ng1 = sb("ng1", [128, DEPTH, 8])
            ng2 = sb("ng2", [128, DEPTH, 8])
            posi = [sb(f"posi{i}", [128, TB], I32) for i in range(2)]
            ang = sb("ang", [128, TB])
            t1 = sb("t1", [128, TB]); t2 = sb("t2", [128, TB]); t3 = sb("t3", [128, TB])
            ti = sb("ti", [128, TB], I32)
            sn = [sb(f"sn{i}", [128, TB]) for i in range(2)]
            cs = [sb(f"cs{i}", [128, TB]) for i in range(2)]
            modps = es.enter_context(nc.psum_tensor(U("modps"), [128, DEPTH * 48], F32))
            with nc.Block() as block:
                P.memset(onesf[:], 1.0, w=["onesf"], eng="pool")
                P.memset(onesb[:], 1.0, w=["onesb"], eng="pool")
                P.memset(epsc[:], EPS, w=["epsc"], eng="pool")
                P.memset(pswap[:], 0.0, w=["pswap"], eng="pool")
                P.add("pool", lambda e: e.affine_select(out=ident[:], in_=onesf[:], pattern=[[-1, 128]], compare_op=ALU.is_equal, fill=0.0, base=0, channel_multiplier=1), r=["onesf"], w=["ident"])
                P.copy(identb[:], ident[:], r=["ident"], w=["identb"], eng="pool")
                P.add("pool", lambda e: e.affine_select(out=tri[:], in_=onesb[:], pattern=[[1, 128]], compare_op=ALU.is_ge, fill=0.0, base=0, channel_multiplier=-1), r=["onesb"], w=["tri"])
                P.add("pool", lambda e: e.affine_select(out=b1[:], in_=onesf[:], pattern=[[-1, 128]], compare_op=ALU.is_equal, fill=0.0, base=-8, channel_multiplier=1), r=["onesf"], w=["b1"])
                P.add("pool", lambda e: e.affine_select(out=b2[:], in_=onesf[:], pattern=[[-1, 128]], compare_op=ALU.is_equal, fill=0.0, base=8, channel_multiplier=1), r=["onesf"], w=["b2"])
                for s0 in (0, 64):
                    P.copy(pswap[:, s0:s0 + 8], b1[:, s0:s0 + 8], r=["b1", "pswap"], w=["pswap"], eng="pool")
                    P.copy(pswap[:, s0 + 8:s0 + 16], b2[:, s0 + 8:s0 + 16], r=["b2", "pswap"], w=["pswap"], eng="pool")
                P.dma(cT[:], cT_d, w=["cT"])
                P.dma(cst[:], cst_d, w=["cst"])
                P.dma(bada[:], bada_d.rearrange("l p j -> p l j"), w=["bada"])
                P.dma(ng1[:], n1g_d.rearrange("l p j -> p l j"), w=["ng1"])
                P.dma(ng2[:], n2g_d.rearrange("l p j -> p l j"), w=["ng2"])
                P.dma(fgT[:], fg_d, w=["fgT"])
                P.act(tmp8[:], cT[:], AF.Exp, r=["cT"], w=["tmp8"], scale=-1.0)
                P.ts(tmp8[:], tmp8[:], 1.0, None, ALU.add, r=["tmp8"], w=["tmp8"])
                P.add("dve", lambda e: e.reciprocal(out=scT[:], in_=tmp8[:]), r=["tmp8"], w=["scT"])
                P.tt(scT[:], scT[:], cT[:], ALU.mult, r=["scT", "cT"], w=["scT"])
                k = 0
                for l in range(DEPTH):
                    wv = wada_d[l].rearrange("(kt p) n -> p kt n", p=128)
                    for cb in range(12):
                        wb_ = wa[k % 2]; wn = f"wa{k % 2}"; k += 1
                        P.dma(wb_[:], wv[:, :, cb * 512:(cb + 1) * 512], w=[wn])
                        for j in range(4):
                            col = l * 48 + cb * 4 + j
                            for kt in range(8):
                                P.mm(modps[:, col:col + 1], wb_[:, kt, j * 128:(j + 1) * 128], scT[:, kt:kt + 1],
                                     start=(kt == 0), stop=(kt == 7), r=[wn, "scT"], w=["modps"])
                P.tt(modv[:].rearrange("p l j -> p (l j)"), modps[:], bada[:].rearrange("p l j -> p (l j)"), ALU.add, r=["modps", "bada"], w=["modv"])
                for l in range(DEPTH):
                    P.stt(A1[:, l, :], modv[:, l, 8:16], 1.0, ng1[:, l, :], ALU.add, ALU.mult, r=["modv", "ng1"], w=["A1"])
                    P.stt(A2[:, l, :], modv[:, l, 32:40], 1.0, ng2[:, l, :], ALU.add, ALU.mult, r=["modv", "ng2"], w=["A2"])
                for i in range(NB):
                    pi_ = posi[i % 2]; pn_ = f"posi{i % 2}"
                    P.dma(pi_[:], pos_d[0:1, i * TB:(i + 1) * TB].to_broadcast([128, TB]), w=[pn_])
                    P.copy(t1[:], pi_[:], r=[pn_], w=["rt1"])
                    P.ts(ang[:], t1[:], cst[:, 0:1], None, ALU.mult, r=["rt1", "cst"], w=["rang"])
                    sincos(P, ang[:], TB, sn[i % 2][:], cs[i % 2][:], t1, t2, t3, ti, "r", f"sn{i % 2}", f"cs{i % 2}")
                    P.ts(sn[i % 2][:], sn[i % 2][:], cst[:, 1:2], None, ALU.mult, r=[f"sn{i % 2}", "cst"], w=[f"sn{i % 2}"])
                    P.dma(sinT_d[:, i * TB:(i + 1) * TB], sn[i % 2][:], r=[f"sn{i % 2}"], w=["sinT"])
                    P.dma(cosT_d[:, i * TB:(i + 1) * TB], cs[i % 2][:], r=[f"cs{i % 2}"], w=["cosT"])
                P.emit(block)

        if stop == "0":
            return nc
        with ExitStack() as es:
            sb = lambda n, s, d=F32: es.enter_context(nc.sbuf_tensor(U(n), list(s), d))
            xin = [sb(f"xin{i}", [128, D]) for i in range(3)]
            xo = [sb(f"xo{i}", [128, 8, TB]) for i in range(2)]
            pst = [es.enter_context(nc.psum_tensor(U(f"pst{i}"), [128, TB], F32)) for i in range(8)]
            with nc.Block() as block:
                k = 0
                for i in range(NB):
                    o_ = xo[i % 2]; on = f"xo{i % 2}"
                    for tt in range(4):
                        xi = xin[k % 3]; xn = f"xin{k % 3}"; k += 1
                        P.dma(xi[:], x_d[i * TB + tt * 128:i * TB + (tt + 1) * 128, :], w=[xn])
                        for ft in range(8):
                            P.tr(pst[ft][:, tt * 128:(tt + 1) * 128], xi[:, ft * 128:(ft + 1) * 128], ident[:], r=[xn], w=[f"pst{ft}"])
                    for ft in range(8):
                        P.copy(o_[:, ft, :], pst[ft][:], r=[f"pst{ft}"], w=[f"{on}_{ft}"], eng=("act" if ft % 2 else "dve"))
                    P.dma(xT_d.rearrange("(ft p) t -> p ft t", p=128)[:, :, i * TB:(i + 1) * TB], o_[:], r=[f"{on}_{ft}" for ft in range(8)], w=["xTd"])
                P.emit(block)

        if stop == "T0":
            return nc
        xTv = xT_d.rearrange("(ft p) t -> p ft t", p=128)

        def load_cast(P, dst3, src2, nk, ncol, name, q="pool", step=2048, col_major=False):
            sv = src2.rearrange("(kt p) n -> p kt n", p=128)
            chunks = [(kt, c0) for kt in range(nk) for c0 in range(0, ncol, step)]
            if col_major:
                chunks.sort(key=lambda t: (t[1], t[0]))
            for kt, c0 in chunks:
                c1 = min(ncol, c0 + step)
                P.dma(dst3[:, kt, c0:c1], sv[:, kt, c0:c1], w=[f"{name}_{kt}_{c0 // step}"], q=q)

        def norm_mod(P, xb, xn, hT, hn, A, B, l, pn, tmps, sqs, rstd, lnv, pool_share=True):
            for kt in range(8):
                sq = sqs[kt % 2]; sqn = f"sq{kt % 2}"
                P.act(sq[:], xb[:, kt, :], AF.Square, r=[xn], w=[sqn])
                P.mm(pn[:], onesb[:], sq[:], start=(kt == 0), stop=(kt == 7), r=[sqn, "onesb"], w=["pn"])
            P.act(lnv[:], pn[:], AF.Ln, r=["pn", "epsc"], w=["lnv"], scale=1.0 / D, bias=epsc[:, 0:1])
            P.act(rstd[:], lnv[:], AF.Exp, r=["lnv"], w=["rstd"], scale=-0.5)
            for kt in range(8):
                tm = tmps[kt % 2]; tn = f"nt{kt % 2}"
                P.tt(tm[:], xb[:, kt, :], rstd[:], ALU.mult, r=[xn, "rstd"], w=[tn], eng=("pool" if (pool_share and kt % 2) else "dve"))
                P.act(hT[:, kt, :], tm[:], AF.Identity, r=[tn, "A", "modv"], w=[f"{hn}_{kt}"], scale=A[:, l, kt:kt + 1], bias=B(kt))

        for l in range(nl):
            with ExitStack() as es:
                sb = lambda n, s, d=F32: es.enter_context(nc.sbuf_tensor(U(n), list(s), d))
                win = sb("win", [128, 8, INW], BF16)
                xbs = [sb(f"xb{i}", [128, 8, TB]) for i in range(2)]
                hTs = [sb(f"hT{i}", [128, 8, TB], BF16) for i in range(2)]
                sqs = [sb(f"sq{i}", [128, TB], BF16) for i in range(2)]
                tmps = [sb(f"nt{i}", [128, TB]) for i in range(2)]
                rstd = sb("rstd", [128, TB]); lnv = sb("lnv", [128, TB])
                cosb = [sb(f"cosb{i}", [128, TB]) for i in range(2)]
                sinb = [sb(f"sinb{i}", [128, TB]) for i in range(2)]
                qb = [sb(f"qb{i}", [128, TB], BF16) for i in range(2)]
                r1 = [sb(f"r1_{i}", [128, TB]) for i in range(2)]
                r2 = [sb(f"r2_{i}", [128, TB]) for i in range(2)]
                rf = [sb(f"rf{i}", [128, TB]) for i in range(2)]
                stg = [sb(f"stg{i}", [128, TB], BF16) for i in range(4)]
                vst = [sb(f"vst{i}", [128, 768], BF16) for i in range(2)]
                kmT = [sb(f"kmT{j}", [128, 32]) for j in range(2)]
                gm8 = [sb(f"gm8_{j}", [128, 8, 32]) for j in range(2)]
                m8a = [sb(f"m8a{j}", [128, 8, 8]) for j in range(2)]
                sel8 = [sb(f"sel8_{j}", [128, 8, 32]) for j in range(2)]
                neg8 = [sb(f"neg8_{j}", [128, 8, 32], BF16) for j in range(2)]
                nst = [sb(f"nst{i}", [32, TB], BF16) for i in range(4)]
                NPA = 3
                pa = [es.enter_context(nc.psum_tensor(U(f"pa{i}"), [128, TB], F32)) for i in range(NPA)]
                pb = [es.enter_context(nc.psum_tensor(U(f"pb{i}"), [128, TB], F32)) for i in range(2)]
                pn = es.enter_context(nc.psum_tensor(U("pn"), [128, TB], F32))
                pgs = [es.enter_context(nc.psum_tensor(U(f"pg{i}"), [128, TB], F32)) for i in range(2)]
                with nc.Block() as block:
                    load_cast(P, win, win_d[l], 8, INW, "win", step=1280, col_major=True)
                    for j in range(2):
                        P.memset(kmT[j][:], 0.0, w=[f"kmT{j}"], eng="pool")
                    cnt = {"pa": 0, "pb": 0, "stg": 0, "qb": 0, "r": 0, "v": 0, "ns": 0}

                    def load_blk(i):
                        P.dma(xbs[i % 2][:], xTv[:, :, i * TB:(i + 1) * TB], w=[f"xb{i % 2}"])
                        P.dma(cosb[i % 2][:], cosT_d[:, i * TB:(i + 1) * TB], w=[f"cosb{i % 2}"])
                        P.dma(sinb[i % 2][:], sinT_d[:, i * TB:(i + 1) * TB], w=[f"sinb{i % 2}"])

                    load_blk(0)
                    for i in range(NB):
                        t0 = i * TB
                        if i + 1 < NB:
                            load_blk(i + 1)
                        xb = xbs[i % 2]; xn = f"xb{i % 2}"; hT = hTs[i % 2]; hn = f"hT{i % 2}"
                        cb_, cbn = cosb[i % 2], f"cosb{i % 2}"
                        sb_, sbn = sinb[i % 2], f"sinb{i % 2}"
                        norm_mod(P, xb, xn, hT, hn, A1, lambda kt: modv[:, l, kt:kt + 1], l, pn, tmps, sqs, rstd, lnv)
                        hres = [f"{hn}_{kt}" for kt in range(8)]

                        def proj(c0):
                            p_ = pa[cnt["pa"] % NPA]; pn_ = f"pa{cnt['pa'] % NPA}"; cnt["pa"] += 1
                            for kt in range(8):
                                P.mm(p_[:], win[:, kt, c0:c0 + 128], hT[:, kt, :], start=(kt == 0), stop=(kt == 7), r=[f"win_{kt}_{c0 // 1280}", hres[kt]], w=[pn_])
                            return p_, pn_

                        RENG = "dve" if "nopool" in SKIP else "pool"

                        def rope(p_, pn_, want_f32):
                            q_ = qb[cnt["qb"] % 2]; qn = f"qb{cnt['qb'] % 2}"; cnt["qb"] += 1
                            P.copy(q_[:], p_[:], r=[pn_], w=[qn], eng="act")
                            s_ = pb[cnt["pb"] % 2]; sn_ = f"pb{cnt['pb'] % 2}"; cnt["pb"] += 1
                            if "nosw" not in SKIP:
                                P.mm(s_[:], pswap[:], q_[:], r=[qn, "pswap"], w=[sn_])
                            else:
                                s_, sn_ = p_, pn_
                            k_ = cnt["r"] % 2; cnt["r"] += 1
                            P.tt(r1[k_][:], p_[:], cb_[:], ALU.mult, r=[pn_, cbn, qn], w=[f"r1_{k_}"])
                            P.tt(r2[k_][:], s_[:], sb_[:], ALU.mult, r=[sn_, sbn], w=[f"r2_{k_}"])
                            g_ = stg[cnt["stg"] % 4]; gn = f"stg{cnt['stg'] % 4}"; cnt["stg"] += 1
                            if want_f32:
                                P.tt(rf[k_][:], r1[k_][:], r2[k_][:], ALU.add, r=[f"r1_{k_}", f"r2_{k_}"], w=[f"rf{k_}"], eng=RENG)
                                P.copy(g_[:], rf[k_][:], r=[f"rf{k_}"], w=[gn], eng="act")
                                return rf[k_], f"rf{k_}", g_, gn
                            P.tt(g_[:], r1[k_][:], r2[k_][:], ALU.add, r=[f"r1_{k_}", f"r2_{k_}"], w=[gn], eng=RENG)
                            return None, None, g_, gn

                        for j in range(2):
                            p_, pn_ = proj(j * 128)
                            g_ = stg[cnt["stg"] % 4]; gn = f"stg{cnt['stg'] % 4}"; cnt["stg"] += 1
                            P.copy(g_[:], p_[:], r=[pn_], w=[gn], eng="act")
                            P.dma(uT_d[j * 128:(j + 1) * 128, t0:t0 + TB], g_[:], r=[gn], w=["uTd"])
                        for h in range(0 if "rope" in SKIP else 4):
                            p_, pn_ = proj(256 + h * 128)
                            _, _, g_, gn = rope(p_, pn_, False)
                            if "nodma" not in SKIP:
                                P.dma(dqT_d[h, :, t0:t0 + TB], g_[:], r=[gn], w=["dqTd"])
                        for h in range(0 if "rope" in SKIP else 4):
                            p_, pn_ = proj(768 + h * 128)
                            _, _, g_, gn = rope(p_, pn_, False)
                            if "nodma" not in SKIP:
                                P.dma(dkT_d[h, :, t0:t0 + TB], g_[:], r=[gn], w=["dkTd"])
                        for j in range(0 if "mk" in SKIP else 2):
                            p_, pn_ = proj(2048 + j * 128)
                            f_, fn_, g_, gn = rope(p_, pn_, True)
                            P.add("dve", lambda e, f_=f_, j=j, i=i: e.tensor_reduce(out=kmT[j][:, 2 * i:2 * i + 2], in_=f_[:].rearrange("p (b k) -> p b k", k=256), axis=AX.X, op=ALU.add),
                                  r=[fn_], w=[f"kmT{j}"])
                            P.ts(kmT[j][:, 2 * i:2 * i + 2], kmT[j][:, 2 * i:2 * i + 2], 1.0 / 256, None, ALU.mult, r=[f"kmT{j}"], w=[f"kmT{j}"])
                            for hh in range(2):
                                P.dma(mkT_d[2 * j + hh, :, t0:t0 + TB], g_[hh * 64:(hh + 1) * 64, :], r=[gn], w=["mkTd"])
                        pend = []
                        for j in range(0 if "mq" in SKIP else 2):
                            p_, pn_ = proj(1792 + j * 128)
                            f_, fn_, g_, gn = rope(p_, pn_, True)
                            for hh in range(2):
                                P.dma(mqT_d[2 * j + hh, 0:64, t0:t0 + TB], g_[hh * 64:(hh + 1) * 64, :], r=[gn], w=["mqTd"])
                            for hh in range(0 if "nogate" in SKIP else 2):
                                hs = slice(hh * 64, (hh + 1) * 64)
                                for tt in range(4):
                                    gcol = j * 128 + tt * 32
                                    P.mm(pgs[hh][:, gcol:gcol + 32], f_[hs, tt * 128:(tt + 1) * 128], kmT[j][hs, :], r=[fn_, f"kmT{j}"], w=[f"pg{hh}"])
                            gm_ = gm8[j]; gmn = f"gm8_{j}"
                            P.memset(gm_[:], -1e30, w=[gmn])
                            for hh in range(2):
                                for tt in range(4):
                                    k8 = hh * 4 + tt
                                    own = (t0 + tt * 128) // 256
                                    gcol = j * 128 + tt * 32
                                    if own > 0:
                                        P.copy(gm_[:, k8, 0:own], pgs[hh][:, gcol:gcol + own], r=[f"pg{hh}", gmn], w=[gmn + f"_{k8}"])
                            for hh in range(0 if "nochain" in SKIP else 2):
                                for tt in range(4):
                                    k8 = hh * 4 + tt
                                    own = (t0 + tt * 128) // 256
                                    P.add("dve", lambda e, k8=k8, j=j: e.max(out=m8a[j][:, k8, :], in_=gm8[j][:, k8, :]), r=[gmn, gmn + f"_{k8}"], w=[f"m8a{j}_{k8}"])
                                    P.tt(sel8[j][:, k8, :], gm_[:, k8, :], m8a[j][:, k8, 2:3].to_broadcast([128, 32]), ALU.is_ge, r=[gmn + f"_{k8}", f"m8a{j}_{k8}"], w=[f"sel8_{j}_{k8}"])
                                    P.ts(neg8[j][:, k8, :], sel8[j][:, k8, :], 30000.0, -30000.0, ALU.mult, ALU.add, r=[f"sel8_{j}_{k8}"], w=[f"neg8_{j}_{k8}"])
                                    P.memset(neg8[j][:, k8, own:own + 1], 0.0, r=[f"neg8_{j}_{k8}"], w=[f"neg8_{j}_{k8}"])
                            pend.append(j)
                        for tt in range(0 if "v" in SKIP else 4):
                            v_ = vst[cnt["v"] % 2]; vn = f"vst{cnt['v'] % 2}"; cnt["v"] += 1
                            p_ = pa[cnt["pa"] % NPA]; pn_ = f"pa{cnt['pa'] % NPA}"; cnt["pa"] += 1
                            for kt in range(8):
                                P.mm(p_[:], hT[:, kt, tt * 128:(tt + 1) * 128], win[:, kt, 1280:1792], start=(kt == 0), stop=(kt == 7), r=[f"win_{kt}_1", hres[kt]], w=[pn_])
                            P.copy(v_[:, 0:512], p_[:], r=[pn_], w=[vn + "a"], eng="act")
                            p2 = pa[cnt["pa"] % NPA]; pn2 = f"pa{cnt['pa'] % NPA}"; cnt["pa"] += 1
                            for kt in range(8):
                                P.mm(p2[:, 0:256], hT[:, kt, tt * 128:(tt + 1) * 128], win[:, kt, 2304:2560], start=(kt == 0), stop=(kt == 7), r=[f"win_{kt}_1", hres[kt]], w=[pn2])
                            P.copy(v_[:, 512:768], p2[:, 0:256], r=[pn2], w=[vn + "b"], eng=("act" if "vact" in SKIP else "dve"))
                            P.dma(dV_d[t0 + tt * 128:t0 + (tt + 1) * 128, :], v_[:, 0:512], r=[vn + "a"], w=["dVd"])
                            P.dma(mV_d[t0 + tt * 128:t0 + (tt + 1) * 128, :], v_[:, 512:768], r=[vn + "b"], w=["mVd"])
                        for j in ([] if "notr" in SKIP else pend):
                            for hh in range(2):
                                pgb = pgs[hh][:].bitcast(BF16)
                                for tt in range(4):
                                    tcol = 512 + tt * 128
                                    P.tr(pgb[0:32, tcol:tcol + 128], neg8[j][:, hh * 4 + tt, :], identb[:], r=[f"neg8_{j}_{hh * 4 + tt}", "identb"], w=[f"pg{hh}"])
                            for hh in range(2):
                                pgb = pgs[hh][:].bitcast(BF16)
                                ns_ = nst[cnt["ns"] % 4]; nsn = f"nst{cnt['ns'] % 4}"; cnt["ns"] += 1
                                P.copy(ns_[:], pgb[0:32, 512:1024], r=[f"pg{hh}"], w=[nsn])
                                P.dma(mqT_d[2 * j + hh, 64:96, t0:t0 + TB], ns_[:], r=[nsn], w=["mqTd"])
                    P.emit(block)

            if stop == "A":
                return nc
            with ExitStack() as es:
                sb = lambda n, s, d=F32: es.enter_context(nc.sbuf_tensor(U(n), list(s), d))
                uTs = [sb(f"uTs{h}", [128, S], BF16) for h in range(2)]
                are = sb("are", [128, 8]); aim = sb("aim", [128, 8]); ldt = sb("ldt", [128, 8])
                dt_ = sb("dt_", [128, 8]); mag = sb("mag", [128, 8]); th = sb("th", [128, 8])
                sth = sb("sth", [128, 8]); cth = sb("cth", [128, 8])
                s1 = sb("s1", [128, 8]); s2 = sb("s2", [128, 8]); s3 = sb("s3", [128, 8]); si = sb("si", [128, 8], I32)
                abr = sb("abr", [128, 8]); abi = sb("abi", [128, 8]); rden = sb("rden", [128, 8])
                kr = sb("kr", [128, 8]); ki = sb("ki", [128, 8]); nki = sb("nki", [128, 8])
                bre = sb("bre", [128, 8, 16]); bim = sb("bim", [128, 8, 16])
                cre = sb("cre", [128, 8, 16]); cim = sb("cim", [128, 8, 16])
                bbr = sb("bbr", [128, 8, 16]); bbi = sb("bbi", [128, 8, 16]); tma = sb("tma", [128, 16])
                X = [sb(f"X{i}", [128, 128]) for i in range(2)]
                BD = sb("BD", [128, 16, 128], BF16)
                CDm = sb("CD", [128, 16, 128], BF16)
                sdv = sb("sdv", [128, 2]); sgb = sb("sgb", [128, 2]); nsgb = sb("nsgb", [128, 2]); sng = sb("sng", [128, 2])
                gw = sb("gw", [128, 2, 256], BF16)
                jidi = sb("jidi", [128, TB], I32); jid = sb("jid", [128, TB])
                tA = sb("tA", [128, TB]); q1 = sb("q1", [128, TB]); q2 = sb("q2", [128, TB]); q3 = sb("q3", [128, TB]); qi = sb("qi", [128, TB], I32)
                Cs = sb("Cs", [128, 8, TB]); Sn = sb("Sn", [128, 8, TB])
                Hre = sb("Hre", [128, 8]); Him = sb("Him", [128, 8])
                NW = 3
                mt = [[sb(f"m{k}_{w}", [128, TB]) for k in range(4)] for w in range(NW)]
                bp = [[sb(f"bp{k}_{w}", [128, TB]) for k in range(2)] for w in range(NW)]
                gg = [[sb(f"g{k}_{w}", [128, TB]) for k in range(2)] for w in range(NW)]
                pp = [[sb(f"p{k}_{w}", [128, TB]) for k in range(4)] for w in range(NW)]
                hb = [[sb(f"hb{k}_{w}", [128, TB], BF16) for k in range(2)] for w in range(NW)]
                yv = [sb(f"yv{h}", [128, TB]) for h in range(2)]
                ya = sb("ya", [128, TB]); yb = sb("yb", [128, TB]); yc = sb("yc", [128, TB])
                gl = [sb(f"gl{h}", [128, TB]) for h in range(2)]
                glb = [sb(f"glb{h}", [128, TB], BF16) for h in range(2)]
                zz = [sb(f"zz{h}", [128, TB]) for h in range(2)]
                zsq = [sb(f"zsq{h}", [128, TB], BF16) for h in range(2)]
                rs_ = sb("rs_", [128, TB]); ln_ = sb("ln_", [128, TB])
                mo = [sb(f"mo{h}", [128, TB], BF16) for h in range(2)]
                pbu = [es.enter_context(nc.psum_tensor(U(f"pbu{i}"), [128, TB], F32)) for i in range(4)]
                py = [es.enter_context(nc.psum_tensor(U(f"py{i}"), [128, TB], F32)) for i in range(2)]
                pgx = es.enter_context(nc.psum_tensor(U("pgx"), [128, TB], F32))
                pn2 = es.enter_context(nc.psum_tensor(U("pn2"), [128, TB], F32))
                with nc.Block() as block:
                    for h in range(2):
                        P.dma(uTs[h][:], uT_d[h * 128:(h + 1) * 128, :], w=[f"uTs{h}"])
                    for t_, d_, n_ in ((are, sare_d, "are"), (aim, saim_d, "aim"), (ldt, sldt_d, "ldt"), (bre, sbre_d, "bre"), (bim, sbim_d, "bim"),
                                       (cre, scre_d, "cre"), (cim, scim_d, "cim"), (sdv, sd_d, "sdv"), (sgb, sgb_d, "sgb"), (sng, sng_d, "sng")):
                        P.dma(t_[:], d_[l], w=[n_])
                    load_cast(P, gw, sgw_d[l], 2, 256, "gw")
                    P.act(dt_[:], ldt[:], AF.Exp, r=["ldt"], w=["dt_"])
                    P.tt(s1[:], are[:], dt_[:], ALU.mult, r=["are", "dt_"], w=["s1"])
                    P.act(mag[:], s1[:], AF.Exp, r=["s1"], w=["mag"])
                    P.tt(th[:], aim[:], dt_[:], ALU.mult, r=["aim", "dt_"], w=["sang"])
                    sincos(P, th[:], 8, sth[:], cth[:], s1, s2, s3, si, "s", "sth", "cth")
                    P.tt(abr[:], mag[:], cth[:], ALU.mult, r=["mag", "cth"], w=["abr"])
                    P.tt(abi[:], mag[:], sth[:], ALU.mult, r=["mag", "sth"], w=["abi"])
                    P.tt(s1[:], are[:], are[:], ALU.mult, r=["are", "st1"], w=["s1b"])
                    P.tt(s2[:], aim[:], aim[:], ALU.mult, r=["aim", "st2"], w=["s2b"])
                    P.tt(s1[:], s1[:], s2[:], ALU.add, r=["s1b", "s2b"], w=["s1b"])
                    P.add("dve", lambda e: e.reciprocal(out=rden[:], in_=s1[:]), r=["s1b"], w=["rden"])
                    P.ts(s3[:], abr[:], -1.0, None, ALU.add, r=["abr", "st3"], w=["nr"])
                    P.tt(s1[:], s3[:], are[:], ALU.mult, r=["nr", "are", "rden"], w=["s1c"])
                    P.tt(s2[:], abi[:], aim[:], ALU.mult, r=["abi", "aim", "s2b"], w=["s2c"])
                    P.tt(s1[:], s1[:], s2[:], ALU.add, r=["s1c", "s2c"], w=["s1c"])
                    P.tt(kr[:], s1[:], rden[:], ALU.mult, r=["s1c", "rden"], w=["kr"])
                    P.tt(s1[:], abi[:], are[:], ALU.mult, r=["abi", "are", "kr"], w=["s1d"])
                    P.tt(s2[:], s3[:], aim[:], ALU.mult, r=["nr", "aim", "s1c"], w=["s2d"])
                    P.tt(s1[:], s1[:], s2[:], ALU.subtract, r=["s1d", "s2d"], w=["s1d"])
                    P.tt(ki[:], s1[:], rden[:], ALU.mult, r=["s1d", "rden"], w=["ki"])
                    P.ts(nki[:], ki[:], -1.0, None, ALU.mult, r=["ki"], w=["nki"])
                    P.ts(nsgb[:], sgb[:], -1.0, None, ALU.mult, r=["sgb"], w=["nsgb"])
                    P.memset(BD[:], 0.0, w=["BD"], eng="pool")
                    P.memset(CDm[:], 0.0, w=["CD"], eng="pool")
                    P.memset(Hre[:], 0.0, w=["Hre"], eng="pool")
                    P.memset(Him[:], 0.0, w=["Him"], eng="pool")
                    for st in range(8):
                        P.ts(tma[:], bre[:, st, :], kr[:, st:st + 1], None, ALU.mult, r=["bre", "kr", "bbr"], w=["tma"])
                        P.stt(bbr[:, st, :], bim[:, st, :], nki[:, st:st + 1], tma[:], ALU.mult, ALU.add, r=["bim", "nki", "tma"], w=["bbr"])
                        P.ts(tma[:], bim[:, st, :], kr[:, st:st + 1], None, ALU.mult, r=["bim", "kr", "bbr"], w=["tma"])
                        P.stt(bbi[:, st, :], bre[:, st, :], ki[:, st:st + 1], tma[:], ALU.mult, ALU.add, r=["bre", "ki", "tma"], w=["bbi"])
                    kx = 0
                    for st in range(8):
                        gl0 = (2 * st) % 8
                        for ri, bsrc, bn in ((0, bbr, "bbr"), (1, bbi, "bbi")):
                            X_ = X[kx % 2]; Xn = f"X{kx % 2}"; kx += 1
                            P.memset(X_[:], 0.0, w=[Xn], eng="pool")
                            P.copy(X_[0:64, gl0 * 16:(gl0 + 1) * 16], bsrc[0:64, st, :], r=[bn, Xn], w=[Xn])
                            P.copy(X_[64:128, (gl0 + 1) * 16:(gl0 + 2) * 16], bsrc[64:128, st, :], r=[bn, Xn], w=[Xn])
                            pt_ = pbu[kx % 4]; ptn = f"pbu{kx % 4}"
                            P.tr(pt_[:, 0:128], X_[:], ident[:], r=[Xn], w=[ptn])
                            P.copy(BD[:, st * 2 + ri, :], pt_[:, 0:128], r=[ptn, "BD"], w=["BD"], eng="act")
                        P.copy(CDm[0:64, st * 2, gl0 * 16:(gl0 + 1) * 16], cre[0:64, st, :], r=["cre", "CD"], w=["CD"])
                        P.copy(CDm[64:128, st * 2, (gl0 + 1) * 16:(gl0 + 2) * 16], cre[64:128, st, :], r=["cre", "CD"], w=["CD"])
                        P.ts(CDm[0:64, st * 2 + 1, gl0 * 16:(gl0 + 1) * 16], cim[0:64, st, :], -1.0, None, ALU.mult, r=["cim", "CD"], w=["CD"])
                        P.ts(CDm[64:128, st * 2 + 1, (gl0 + 1) * 16:(gl0 + 2) * 16], cim[64:128, st, :], -1.0, None, ALU.mult, r=["cim", "CD"], w=["CD"])
                    P.add("pool", lambda e: e.iota(jidi[:], pattern=[[1, TB]], base=1, channel_multiplier=0), w=["jidi"])
                    P.copy(jid[:], jidi[:], r=["jidi"], w=["jid"])
                    for st in range(8):
                        P.ts(tA[:], jid[:], th[:, st:st + 1], None, ALU.mult, r=["jid", "sang", "tsin", "tcos"], w=["tang"])
                        sincos(P, tA[:], TB, Sn[:, st, :], Cs[:, st, :], q1, q2, q3, qi, "t", "Sn", "Cs")
                    it = 0
                    for i in range(NB):
                        t0 = i * TB
                        for st in range(8):
                            w_ = it % NW; it += 1
                            half = st // 4
                            m = mt[w_]; mn = [f"m{k}_{w_}" for k in range(4)]
                            b_ = bp[w_]; bn_ = [f"bp{k}_{w_}" for k in range(2)]
                            g_ = gg[w_]; gn_ = [f"g{k}_{w_}" for k in range(2)]
                            p_ = pp[w_]; pn_ = [f"p{k}_{w_}" for k in range(4)]
                            h_ = hb[w_]; hn_ = [f"hb{k}_{w_}" for k in range(2)]
                            pr, prn = pbu[(2 * it) % 4], f"pbu{(2 * it) % 4}"
                            pi2, pin = pbu[(2 * it + 1) % 4], f"pbu{(2 * it + 1) % 4}"
                            P.mm(pr[:], BD[:, st * 2, :], uTs[half][:, t0:t0 + TB], r=["BD", f"uTs{half}"], w=[prn])
                            P.mm(pi2[:], BD[:, st * 2 + 1, :], uTs[half][:, t0:t0 + TB], r=["BD", f"uTs{half}"], w=[pin])
                            cs_, sn_ = Cs[:, st, :], Sn[:, st, :]
                            P.tt(m[0][:], pr[:], cs_, ALU.mult, r=[prn, "Cs"], w=[mn[0]])
                            P.tt(m[1][:], pi2[:], sn_, ALU.mult, r=[pin, "Sn"], w=[mn[1]])
                            P.tt(m[2][:], pi2[:], cs_, ALU.mult, r=[pin, "Cs"], w=[mn[2]])
                            P.tt(m[3][:], pr[:], sn_, ALU.mult, r=[prn, "Sn"], w=[mn[3]])
                            P.tt(b_[0][:], m[0][:], m[1][:], ALU.add, r=[mn[0], mn[1]], w=[bn_[0]], eng="pool")
                            P.tt(b_[1][:], m[2][:], m[3][:], ALU.subtract, r=[mn[2], mn[3]], w=[bn_[1]], eng="pool")
                            rbc = mag[:, st:st + 1].to_broadcast([128, TB])
                            P.add("dve", lambda e, o=g_[0], d1=b_[0], ini=Hre[:, st:st + 1], rbc=rbc: e.tensor_tensor_scan(out=o[:], data0=rbc, data1=d1[:], initial=ini, op0=ALU.mult, op1=ALU.add),
                                  r=[bn_[0], "mag", "Hre"], w=[gn_[0]])
                            P.add("dve", lambda e, o=g_[1], d1=b_[1], ini=Him[:, st:st + 1], rbc=rbc: e.tensor_tensor_scan(out=o[:], data0=rbc, data1=d1[:], initial=ini, op0=ALU.mult, op1=ALU.add),
                                  r=[bn_[1], "mag", "Him"], w=[gn_[1]])
                            P.tt(p_[0][:], g_[0][:], cs_, ALU.mult, r=[gn_[0], "Cs"], w=[pn_[0]])
                            P.tt(p_[1][:], g_[1][:], sn_, ALU.mult, r=[gn_[1], "Sn"], w=[pn_[1]])
                            P.tt(p_[2][:], g_[0][:], sn_, ALU.mult, r=[gn_[0], "Sn"], w=[pn_[2]], eng="pool")
                            P.tt(p_[3][:], g_[1][:], cs_, ALU.mult, r=[gn_[1], "Cs"], w=[pn_[3]], eng="pool")
                            P.tt(h_[0][:], p_[0][:], p_[1][:], ALU.subtract, r=[pn_[0], pn_[1]], w=[hn_[0]], eng="pool")
                            P.tt(h_[1][:], p_[2][:], p_[3][:], ALU.add, r=[pn_[2], pn_[3]], w=[hn_[1]], eng="pool")
                            P.tt(Hre[:, st:st + 1], p_[0][:, TB - 1:TB], p_[1][:, TB - 1:TB], ALU.subtract, r=[pn_[0], pn_[1], "Hre"], w=["Hre"])
                            P.tt(Him[:, st:st + 1], p_[2][:, TB - 1:TB], p_[3][:, TB - 1:TB], ALU.add, r=[pn_[2], pn_[3], "Him"], w=["Him"])
                            P.mm(py[half][:], CDm[:, st * 2, :], h_[0][:], start=(st % 4 == 0), stop=False, r=["CD", hn_[0]], w=[f"py{half}"])
                            P.mm(py[half][:], CDm[:, st * 2 + 1, :], h_[1][:], start=False, stop=(st % 4 == 3), r=["CD", hn_[1]], w=[f"py{half}"])
                        for h in range(2):
                            P.stt(yv[h][:], uTs[h][:, t0:t0 + TB], sdv[:, h:h + 1], py[h][:], ALU.mult, ALU.add, r=[f"uTs{h}", "sdv", f"py{h}"], w=[f"yv{h}"])
                            P.act(ya[:], yv[h][:], AF.Square, r=[f"yv{h}"], w=["ya"])
                            P.ts(yb[:], ya[:], 0.044715, 1.0, ALU.mult, ALU.add, r=["ya"], w=["yb"])
                            P.tt(yc[:], yb[:], yv[h][:], ALU.mult, r=["yb", f"yv{h}"], w=["yc"], eng="pool")
                            P.act(ya[:], yc[:], AF.Sigmoid, r=["yc"], w=["ya"], scale=1.5957691216)
                            P.tt(gl[h][:], yv[h][:], ya[:], ALU.mult, r=[f"yv{h}", "ya"], w=[f"gl{h}"], eng="pool")
                            P.copy(glb[h][:], gl[h][:], r=[f"gl{h}"], w=[f"glb{h}"], eng="act")
                        for oh in range(2):
                            for kh in range(2):
                                P.mm(pgx[:], gw[:, kh, oh * 128:(oh + 1) * 128], glb[kh][:], start=(kh == 0), stop=(kh == 1), r=[f"gw_{kh}_0", f"glb{kh}"], w=["pgx"])
                            P.act(yc[:], pgx[:], AF.Sigmoid, r=["pgx", "sgb"], w=["yc"], bias=sgb[:, oh:oh + 1])
                            P.tt(zz[oh][:], gl[oh][:], yc[:], ALU.mult, r=[f"gl{oh}", "yc"], w=[f"zz{oh}"], eng="pool")
                            P.act(zsq[oh][:], zz[oh][:], AF.Square, r=[f"zz{oh}"], w=[f"zsq{oh}"])
                            P.mm(pn2[:], onesb[:], zsq[oh][:], start=(oh == 0), stop=(oh == 1), r=[f"zsq{oh}"], w=["pn2"])
                        P.act(ln_[:], pn2[:], AF.Ln, r=["pn2"], w=["ln_"], scale=1.0 / 256, bias=epsc[:, 0:1])
                        P.act(rs_[:], ln_[:], AF.Exp, r=["ln_"], w=["rs_"], scale=-0.5)
                        for oh in range(2):
                            P.stt(mo[oh][:], zz[oh][:], sng[:, oh:oh + 1], rs_[:], ALU.mult, ALU.mult, r=[f"zz{oh}", "sng", "rs_"], w=[f"mo{oh}"])
                            P.dma(mixT_d[oh * 128:(oh + 1) * 128, t0:t0 + TB], mo[oh][:], r=[f"mo{oh}"], w=["mixTd"])
                    P.emit(block)

            if stop == "S":
                return nc
            lam_init = 0.8 - 0.6 * math.exp(-0.3 * l)

            def attention(P, streams, nq_blocks, epilogue, psS, psAcc, Pt, cnt, acc_sets=1):
                ns = len(streams)
                LA = max(1, len(psS) // ns - 1)
                for qb_ in range(nq_blocks):
                    q0 = qb_ * TB
                    nkt = (q0 + TB) // 128
                    aoff = (qb_ % acc_sets) * (len(psAcc) // acc_sets)
                    held = {}

                    def qk(kt):
                        a_ = kt - q0 // 128
                        qs = max(a_, 0) * 128
                        for si_, st_ in enumerate(streams):
                            bi = cnt["s"] % len(psS); cnt["s"] += 1
                            ps_, psn = psS[bi], f"psS{bi}"
                            pt_, ptn = Pt[bi], f"Pt{bi}"
                            rows = st_["rows"]
                            P.mm(ps_[:, qs:TB], st_["kT"][rows, kt * 128:(kt + 1) * 128], st_["qT"][rows, q0 + qs:q0 + TB], r=[st_["kn"], st_["qn"]], w=[psn])
                            P.act(pt_[:, qs:TB], ps_[:, qs:TB], AF.Exp, r=[psn], w=[ptn], scale=0.125)
                            if a_ >= 0:
                                P.tt(pt_[:, qs:qs + 128], pt_[:, qs:qs + 128], tri[:], ALU.mult, r=[ptn, "tri"], w=[ptn], eng=("pool" if si_ % 2 else "dve"))
                            held[(kt, si_)] = (pt_, ptn, qs)

                    def av(kt):
                        for si_, st_ in enumerate(streams):
                            pt_, ptn, qs = held.pop((kt, si_))
                            for lf, ai in st_["vals"]:
                                P.mm(psAcc[aoff + ai][:, qs:TB], lf(kt), pt_[:, qs:TB], start=(kt == 0), stop=(kt == nkt - 1), r=[ptn, st_["vn"]], w=[f"acc{aoff + ai}"])

                    for kt in range(min(LA, nkt)):
                        qk(kt)
                    for kt in range(nkt):
                        if kt + LA < nkt:
                            qk(kt + LA)
                        av(kt)
                    epilogue(qb_, q0, aoff)

            with ExitStack() as es:
                sb = lambda n, s, d=F32: es.enter_context(nc.sbuf_tensor(U(n), list(s), d))
                qT = sb("qT", [128, S], BF16); kT = sb("kT", [128, S], BF16)
                Vt = sb("Vt", [128, NKT, 128], BF16)
                Pt = [sb(f"Pt{i}", [128, TB], BF16) for i in range(4)]
                lv = sb("lv", [128, 256]); lp = sb("lp", [128, 64]); ls = sb("ls", [128, 2]); nlam = sb("nlam", [128, 1])
                dsg = sb("dsg", [128, 1])
                rc0 = sb("rc0", [128, TB]); rc1 = sb("rc1", [128, TB]); o0 = sb("o0", [128, TB]); o1 = sb("o1", [128, TB])
                osq = sb("osq", [128, TB], BF16); dl = sb("dl", [128, TB]); dr = sb("dr", [128, TB])
                do_ = [sb(f"do{i}", [128, TB], BF16) for i in range(2)]
                psS = [es.enter_context(nc.psum_tensor(U(f"psS{i}"), [128, TB], F32)) for i in range(4)]
                psAcc = [es.enter_context(nc.psum_tensor(U(f"acc{i}"), [128, TB], F32)) for i in range(4)]
                with nc.Block() as block:
                    P.dma(lv[:], lam_d[l:l + 1, :].to_broadcast([128, 256]), w=["lv"])
                    P.dma(dsg[:], dsg_d[l], w=["dsg"])
                    for k2 in range(2):
                        P.tt(lp[:], lv[:, 128 * k2:128 * k2 + 64], lv[:, 128 * k2 + 64:128 * k2 + 128], ALU.mult, r=["lv", "ls"], w=["lp"])
                        P.add("dve", lambda e, k2=k2: e.tensor_reduce(out=ls[:, k2:k2 + 1], in_=lp[:], axis=AX.X, op=ALU.add), r=["lp"], w=["ls"])
                    P.act(ls[:], ls[:], AF.Exp, r=["ls"], w=["ls"])
                    P.stt(nlam[:], ls[:, 1:2], -lam_init, ls[:, 0:1], ALU.add, ALU.subtract, r=["ls"], w=["nlam"])
                    P.ts(dsg[:], dsg[:], 1.0 - lam_init, None, ALU.mult, r=["dsg"], w=["dsg"])
                    cnt = {"s": 0, "o": 0}
                    for h in range(4):
                        P.dma(qT[:], dqT_d[h], w=["qT"])
                        P.dma(kT[:], dkT_d[h], w=["kT"])
                        P.dma(Vt[:], dV_d.rearrange("(kt p) c -> p kt c", p=128)[:, :, h * 128:(h + 1) * 128], w=["Vt"])
                        streams = [dict(kT=kT, qT=qT, rows=slice(c * 64, (c + 1) * 64), kn="kT", qn="qT", vn="Vt",
                                        vals=[(lambda kt: Vt[:, kt, :], 2 * c), (lambda kt: onesb[:], 2 * c + 1)]) for c in range(2)]

                        def epi(qb_, q0, aoff, h=h):
                            P.act(rc0[:], psAcc[1][:], AF.Ln, r=["acc1"], w=["rc0"])
                            P.act(rc1[:], psAcc[3][:], AF.Ln, r=["acc3"], w=["rc1"])
                            P.act(rc0[:], rc0[:], AF.Exp, r=["rc0"], w=["rc0"], scale=-1.0)
                            P.act(rc1[:], rc1[:], AF.Exp, r=["rc1"], w=["rc1"], scale=-1.0)
                            P.tt(o0[:], psAcc[0][:], rc0[:], ALU.mult, r=["acc0", "rc0"], w=["o0"])
                            P.tt(o1[:], psAcc[2][:], rc1[:], ALU.mult, r=["acc2", "rc1"], w=["o1"])
                            P.stt(o0[:], o1[:], nlam[:, 0:1], o0[:], ALU.mult, ALU.add, r=["o1", "o0", "nlam"], w=["o0"])
                            P.act(osq[:], o0[:], AF.Square, r=["o0"], w=["osq"])
                            P.mm(psS[0][:], onesb[:], osq[:], r=["osq"], w=["psS0"])
                            P.act(dl[:], psS[0][:], AF.Ln, r=["psS0"], w=["dl"], scale=1.0 / 128, bias=epsc[:, 0:1])
                            P.act(dr[:], dl[:], AF.Exp, r=["dl"], w=["dr"], scale=-0.5)
                            d_ = do_[cnt["o"] % 2]; dn = f"do{cnt['o'] % 2}"; cnt["o"] += 1
                            P.stt(d_[:], o0[:], dsg[:, 0:1], dr[:], ALU.mult, ALU.mult, r=["o0", "dsg", "dr"], w=[dn])
                            P.dma(mixT_d[256 + h * 128:256 + (h + 1) * 128, q0:q0 + TB], d_[:], r=[dn], w=["mixTd"])

                        attention(P, streams, NB, epi, psS, psAcc, Pt, cnt)
                    P.emit(block)

            if stop == "D":
                return nc
            with ExitStack() as es:
                sb = lambda n, s, d=F32: es.enter_context(nc.sbuf_tensor(U(n), list(s), d))
                qT = sb("mqTs", [96, S], BF16); kT = sb("mkTs", [96, S], BF16)
                Vt = sb("mVt", [128, NKT, 128], BF16)
                onesS = sb("onesS", [96, S], BF16); tmpS = sb("tmpS", [96, S], BF16)
                Pt = [sb(f"Pt{i}", [128, TB], BF16) for i in range(4)]
                mng = sb("mngs", [64, 1])
                osb = sb("osb", [128, TB]); rc0 = sb("mrc", [64, TB]); o0 = sb("mo0", [64, TB])
                osq = sb("mosq", [64, TB], BF16); dl = sb("mdl", [64, TB]); dr = sb("mdr", [64, TB])
                do_ = [sb(f"mdo{i}", [64, TB], BF16) for i in range(2)]
                psS = [es.enter_context(nc.psum_tensor(U(f"psS{i}"), [128, TB], F32)) for i in range(4)]
                psAcc = [es.enter_context(nc.psum_tensor(U(f"acc{i}"), [128, TB], F32)) for i in range(2)]
                pmv = es.enter_context(nc.psum_tensor(U("pmv"), [128, TB], F32))
                pnm = es.enter_context(nc.psum_tensor(U("pnm"), [128, TB], F32))
                with nc.Block() as block:
                    P.dma(mng[:], mng_d[l], w=["mng"])
                    P.memset(Vt[:], 1.0, w=["Vt"], eng="pool")
                    P.memset(onesS[64:96, :], 1.0, w=["onesS"], eng="pool")
                    P.add("pool", lambda e: e.affine_select(out=tmpS[64:96, :], in_=onesS[64:96, :], pattern=[[1, S]], compare_op=ALU.is_ge, fill=0.0, base=0, channel_multiplier=-256), r=["onesS"], w=["tmpS"])
                    P.add("pool", lambda e: e.affine_select(out=kT[64:96, :], in_=tmpS[64:96, :], pattern=[[-1, S]], compare_op=ALU.is_ge, fill=0.0, base=255, channel_multiplier=256), r=["tmpS"], w=["kT1h"])
                    cnt = {"s": 0, "o": 0, "a": 0}
                    for h in range(4):
                        P.dma(qT[:], mqT_d[h], w=["qT"])
                        P.dma(kT[0:64, :], mkT_d[h], r=["kT1h"], w=["kT"])
                        P.dma(Vt[:, :, 0:64], mV_d.rearrange("(kt p) c -> p kt c", p=128)[:, :, h * 64:(h + 1) * 64], w=["Vt"])
                        streams = [dict(kT=kT, qT=qT, rows=slice(0, 96), kn="kT", qn="qT", vn="Vt", vals=[(lambda kt: Vt[:, kt, :], 0)])]

                        def epi(qb_, q0, aoff, h=h):
                            P.copy(osb[:], psAcc[aoff][:], r=[f"acc{aoff}"], w=["osb"], eng="act")
                            P.mm(pmv[0:64, :], ident[:, 64:128], osb[:], r=["osb"], w=["pmv"])
                            P.act(rc0[:], pmv[0:64, :], AF.Ln, r=["pmv"], w=["rc0"])
                            P.act(rc0[:], rc0[:], AF.Exp, r=["rc0"], w=["rc0"], scale=-1.0)
                            P.tt(o0[:], osb[0:64, :], rc0[:], ALU.mult, r=["osb", "rc0"], w=["o0"])
                            P.act(osq[:], o0[:], AF.Square, r=["o0"], w=["osq"])
                            P.mm(pnm[0:64, :], onesb[0:64, 0:64], osq[:], r=["osq"], w=["pnm"])
                            P.act(dl[:], pnm[0:64, :], AF.Ln, r=["pnm"], w=["dl"], scale=1.0 / 64, bias=epsc[0:64, 0:1])
                            P.act(dr[:], dl[:], AF.Exp, r=["dl"], w=["dr"], scale=-0.5)
                            d_ = do_[cnt["o"] % 2]; dn = f"mdo{cnt['o'] % 2}"; cnt["o"] += 1
                            P.stt(d_[:], o0[:], mng[:, 0:1], dr[:], ALU.mult, ALU.mult, r=["o0", "mng", "dr"], w=[dn])
                            P.dma(mixT_d[768 + h * 64:768 + (h + 1) * 64, q0:q0 + TB], d_[:], r=[dn], w=["mixTd"])

                        attention(P, streams, NB, epi, psS, psAcc, Pt, cnt, acc_sets=2)
                    P.emit(block)

            if stop == "M":
                return nc
            with ExitStack() as es:
                sb = lambda n, s, d=F32: es.enter_context(nc.sbuf_tensor(U(n), list(s), d))
                wo = sb("wo", [128, 8, D], BF16)
                xbs = [sb(f"xb{i}", [128, 8, TB]) for i in range(2)]
                mxs = [sb(f"mx{i}", [128, 8, TB], BF16) for i in range(2)]
                pc = [es.enter_context(nc.psum_tensor(U(f"pc{i}"), [128, TB], F32)) for i in range(4)]
                with nc.Block() as block:
                    load_cast(P, wo, wout_d[l], 8, D, "wo", step=1024)
                    mixv = mixT_d.rearrange("(ft p) t -> p ft t", p=128)

                    def ld(i):
                        P.dma(xbs[i % 2][:], xTv[:, :, i * TB:(i + 1) * TB], r=["xTd"], w=[f"xb{i % 2}_{ft}" for ft in range(8)])
                        P.dma(mxs[i % 2][:], mixv[:, :, i * TB:(i + 1) * TB], w=[f"mx{i % 2}"])
                    ld(0)
                    k = 0
                    for i in range(NB):
                        if i + 1 < NB:
                            ld(i + 1)
                        xb = xbs[i % 2]; mx = mxs[i % 2]
                        for fo in range(8):
                            p_ = pc[k % 4]; pn_ = f"pc{k % 4}"; k += 1
                            for kt in range(8):
                                P.mm(p_[:], wo[:, kt, fo * 128:(fo + 1) * 128], mx[:, kt, :], start=(kt == 0), stop=(kt == 7), r=[f"wo_{kt}_0", f"mx{i % 2}"], w=[pn_])
                            P.stt(xb[:, fo, :], p_[:], modv[:, l, 16 + fo:17 + fo], xb[:, fo, :], ALU.mult, ALU.add, r=[pn_, f"xb{i % 2}_{fo}"], w=[f"xb{i % 2}_{fo}"])
                        P.dma(xTv[:, :, i * TB:(i + 1) * TB], xb[:], r=[f"xb{i % 2}_{ft}" for ft in range(8)], w=["xTd"])
                    P.emit(block)

            if stop == "C1":
                return nc
            with ExitStack() as es:
                sb = lambda n, s, d=F32: es.enter_context(nc.sbuf_tensor(U(n), list(s), d))
                w1 = sb("w1", [128, 8, DFF], BF16)
                w2 = sb("w2", [128, 32, D], BF16)
                xb = sb("xb", [128, 8, TB])
                hT = sb("hT", [128, 8, TB], BF16)
                hid = sb("hid", [128, 32, TB], BF16)
                sqs = [sb(f"sq{i}", [128, TB], BF16) for i in range(2)]
                tmps = [sb(f"nt{i}", [128, TB]) for i in range(2)]
                rstd = sb("rstd", [128, TB]); lnv = sb("lnv", [128, TB])
                rl = [sb(f"rl{i}", [128, TB]) for i in range(2)]
                pc = [es.enter_context(nc.psum_tensor(U(f"pc{i}"), [128, TB], F32)) for i in range(5)]
                pd = [es.enter_context(nc.psum_tensor(U(f"pd{i}"), [128, TB], F32)) for i in range(2)]
                pn = es.enter_context(nc.psum_tensor(U("pn"), [128, TB], F32))
                with nc.Block() as block:
                    load_cast(P, w1, w1_d[l], 8, DFF, "w1", step=1024, col_major=True)
                    load_cast(P, w2, w2_d[l], 32, D, "w2", step=1024)
                    k = 0; k2 = 0
                    for i in range(NB):
                        P.dma(xb[:], xTv[:, :, i * TB:(i + 1) * TB], r=["xTd"], w=[f"xb_{ft}" for ft in range(8)] + ["xb"])
                        norm_mod(P, xb, "xb", hT, "hT", A2, lambda kt: modv[:, l, 24 + kt:25 + kt], l, pn, tmps, sqs, rstd, lnv)
                        hres = [f"hT_{kt}" for kt in range(8)]
                        for ft in range(32):
                            p_ = pc[k % 5]; pn_ = f"pc{k % 5}"; r_ = rl[k % 2]; rn = f"rl{k % 2}"; k += 1
                            for kt in range(8):
                                P.mm(p_[:], w1[:, kt, ft * 128:(ft + 1) * 128], hT[:, kt, :], start=(kt == 0), stop=(kt == 7), r=[f"w1_{kt}_{ft // 8}", hres[kt]], w=[pn_])
                            P.act(r_[:], p_[:], AF.Relu, r=[pn_], w=[rn])
                            P.tt(hid[:, ft, :], r_[:], r_[:], ALU.mult, r=[rn], w=[f"hid{ft}"], eng=("pool" if ft % 2 else "dve"))
                        for fo in range(8):
                            p_ = pd[k2 % 2]; pn_ = f"pd{k2 % 2}"; k2 += 1
                            for ft in range(32):
                                P.mm(p_[:], w2[:, ft, fo * 128:(fo + 1) * 128], hid[:, ft, :], start=(ft == 0), stop=(ft == 31), r=[f"w2_{ft}_0", f"hid{ft}"], w=[pn_])
                            P.stt(xb[:, fo, :], p_[:], modv[:, l, 40 + fo:41 + fo], xb[:, fo, :], ALU.mult, ALU.add, r=[pn_, "xb", f"xb_{fo}"], w=[f"xb_{fo}"])
                        P.dma(xTv[:, :, i * TB:(i + 1) * TB], xb[:], r=[f"xb_{ft}" for ft in range(8)], w=["xTd", "xb"])
                    P.emit(block)

        if stop == "C2":
            return nc
        with ExitStack() as es:
            sb = lambda n, s, d=F32: es.enter_context(nc.sbuf_tensor(U(n), list(s), d))
            xbs = [sb(f"xb{i}", [128, 8, TB]) for i in range(2)]
            yn = sb("yn", [128, 8, TB])
            sqs = [sb(f"sq{i}", [128, TB], BF16) for i in range(2)]
            rstd = sb("rstd", [128, TB]); lnv = sb("lnv", [128, TB])
            ost = [sb(f"ost{i}", [128, D]) for i in range(2)]
            pt = [es.enter_context(nc.psum_tensor(U(f"pt{i}"), [128, TB], F32)) for i in range(6)]
            pn = es.enter_context(nc.psum_tensor(U("pn"), [128, TB], F32))
            with nc.Block() as block:
                P.dma(xbs[0][:], xTv[:, :, 0:TB], w=["xb0"])
                k = 0; ko = 0
                for i in range(NB):
                    if i + 1 < NB:
                        P.dma(xbs[(i + 1) % 2][:], xTv[:, :, (i + 1) * TB:(i + 2) * TB], w=[f"xb{(i + 1) % 2}"])
                    xb = xbs[i % 2]; xn = f"xb{i % 2}"
                    for kt in range(8):
                        sq = sqs[kt % 2]; sqn = f"sq{kt % 2}"
                        P.act(sq[:], xb[:, kt, :], AF.Square, r=[xn], w=[sqn])
                        P.mm(pn[:], onesb[:], sq[:], start=(kt == 0), stop=(kt == 7), r=[sqn], w=["pn"])
                    P.act(lnv[:], pn[:], AF.Ln, r=["pn"], w=["lnv"], scale=1.0 / D, bias=epsc[:, 0:1])
                    P.act(rstd[:], lnv[:], AF.Exp, r=["lnv"], w=["rstd"], scale=-0.5)
                    for kt in range(8):
                        P.stt(yn[:, kt, :], xb[:, kt, :], fgT[:, kt:kt + 1], rstd[:], ALU.mult, ALU.mult, r=[xn, "rstd"], w=[f"yn{kt}"])
                    for tt in range(4):
                        o_ = ost[ko % 2]; on = f"ost{ko % 2}"; ko += 1
                        for hf in range(2):
                            p_ = pt[k % 6]; pn_ = f"pt{k % 6}"; k += 1
                            for kk in range(4):
                                kt = hf * 4 + kk
                                P.tr(p_[:, kk * 128:(kk + 1) * 128], yn[:, kt, tt * 128:(tt + 1) * 128], ident[:], r=[f"yn{kt}"], w=[pn_])
                            P.copy(o_[:, hf * 512:(hf + 1) * 512], p_[:], r=[pn_], w=[f"{on}_{hf}"], eng=("act" if hf else "dve"))
                        P.dma(out_d[i * TB + tt * 128:i * TB + (tt + 1) * 128, :], o_[:], r=[f"{on}_0", f"{on}_1"], w=["outd"])
                P.emit(block)
    return nc


def _layout_inputs(inp, S):
    f = lambda a: np.ascontiguousarray(a, dtype=np.float32)
    col8 = lambda v: f(np.asarray(v).reshape(8, 128).T)
    cst = np.zeros((128, 2), np.float32)
    inv = (ROPE_THETA ** (-np.arange(0, 16, 2, dtype=np.float32) / 16)).astype(np.float32)
    for s0 in (0, 64):
        for d in range(16):
            cst[s0 + d, 0] = inv[d % 8]
            cst[s0 + d, 1] = -1.0 if d < 8 else 1.0
    L = DEPTH
    shared = {
        "cst": cst,
        "w_ada": f(inp["w_ada"]),
        "b_adaT": f(np.asarray(inp["b_ada"]).reshape(L, 48, 128).transpose(0, 2, 1)),
        "n1gT": f(np.asarray(inp["norm1_g"]).reshape(L, 8, 128).transpose(0, 2, 1)),
        "n2gT": f(np.asarray(inp["norm2_g"]).reshape(L, 8, 128).transpose(0, 2, 1)),
        "fgT": col8(inp["final_g"]),
        "w_in": f(inp["w_in"]), "w_out": f(inp["w_out"]), "mlp_w1": f(inp["mlp_w1"]), "mlp_w2": f(inp["mlp_w2"]),
        "s_are": f(np.asarray(inp["ssm_a_re"]).reshape(L, 8, 128).transpose(0, 2, 1)),
        "s_aim": f(np.asarray(inp["ssm_a_im"]).reshape(L, 8, 128).transpose(0, 2, 1)),
        "s_ldt": f(np.repeat(np.asarray(inp["ssm_log_dt"]).reshape(L, 8, 2), 64, axis=2).transpose(0, 2, 1)),
        "s_bre": f(np.asarray(inp["ssm_b_re"]).reshape(L, 8, 2, 64, 16).transpose(0, 2, 3, 1, 4).reshape(L, 128, 8, 16)),
        "s_bim": f(np.asarray(inp["ssm_b_im"]).reshape(L, 8, 2, 64, 16).transpose(0, 2, 3, 1, 4).reshape(L, 128, 8, 16)),
        "s_cre": f(np.asarray(inp["ssm_c_re"]).reshape(L, 8, 2, 16, 64).transpose(0, 2, 4, 1, 3).reshape(L, 128, 8, 16)),
        "s_cim": f(np.asarray(inp["ssm_c_im"]).reshape(L, 8, 2, 16, 64).transpose(0, 2, 4, 1, 3).reshape(L, 128, 8, 16)),
        "s_d": f(np.asarray(inp["ssm_d"]).reshape(L, 2, 128).transpose(0, 2, 1)),
        "s_gw": f(inp["ssm_glu_w"]),
        "s_gb": f(np.asarray(inp["ssm_glu_b"]).reshape(L, 2, 128).transpose(0, 2, 1)),
        "s_ng": f(np.asarray(inp["ssm_norm_g"]).reshape(L, 2, 128).transpose(0, 2, 1)),
        "lamv": f(np.concatenate([np.asarray(inp[k]) for k in ("diff_lq1", "diff_lk1", "diff_lq2", "diff_lk2")], axis=1)),
        "dsg": f(np.asarray(inp["diff_subln_g"]).reshape(L, 128, 1)),
        "mng": f(np.asarray(inp["moba_norm_g"]).reshape(L, 64, 1)),
    }
    x = np.asarray(inp["x"]); c = np.asarray(inp["c"]); pos = np.asarray(inp["positions"])
    maps = []
    for core in range(8):
        b = core % x.shape[0]
        m = dict(shared)
        m["x"] = f(x[b, :S])
        m["pos"] = np.ascontiguousarray(pos[b:b + 1, :S], dtype=np.int32)
        m["cT"] = col8(c[b])
        maps.append(m)
    return maps


def kernel(**inputs):
    S = SEQ
    nc = build(S)
    maps = _layout_inputs(inputs, S)
    res = run_bass_kernel_spmd(nc, maps, core_ids=list(range(8)))
    B = np.asarray(inputs["x"]).shape[0]
    return np.stack([np.asarray(res.results[b]["out"], dtype=np.float32) for b in range(B)], axis=0)
```

```python
import math
from contextlib import ExitStack

import numpy as np
import concourse.bass as bass
import concourse.mybir as mybir
from concourse.bass_utils import run_bass_kernel_spmd

F32 = mybir.dt.float32
BF16 = mybir.dt.bfloat16
I32 = mybir.dt.int32
ALU = mybir.AluOpType
AF = mybir.ActivationFunctionType
AX = mybir.AxisListType

D = 1024
SEQ = 8192
DEPTH = 2
DFF = 4096
INW = 2560
TB = 512
EPS = 1e-6
ROPE_THETA = 500000.0

ENGS = ("pe", "act", "dve", "pool", "sp")
PSUM_PREFIX = ("pa", "pb", "pn", "pg", "pst", "modps", "pbu", "py", "psS", "acc", "pmv", "pnm", "pc", "pd", "pt")


class Prog:
    NDMA = 6

    def __init__(self, nc, sems):
        self.nc = nc
        self.sems = sems
        self.cnt = {e: 0 for e in ENGS}
        self.dcnt = {}
        self.drot = {e: 0 for e in ENGS}
        self.reset()

    def reset(self):
        self.ops = {e: [] for e in ENGS}
        self.last_w = {}
        self.readers = {}

    def add(self, eng, fn, r=(), w=(), dma=False):
        deps = []
        for x in r:
            if x in self.last_w:
                deps.append(self.last_w[x])
            if x.startswith(PSUM_PREFIX):
                deps.extend(o for o in self.readers.get(x, ()) if o["eng"] != eng)
        for x in w:
            if x in self.last_w:
                deps.append(self.last_w[x])
            deps.extend(self.readers.get(x, ()))
        op = {"fn": fn, "deps": [], "dma": dma, "sig": False, "eng": eng, "val": None, "sem": None}
        for d in deps:
            if d is op:
                continue
            if d["eng"] == eng and not d["dma"] and not dma and eng == "pe":
                continue
            if not any(d is x for x in op["deps"]):
                op["deps"].append(d)
                d["sig"] = True
        self.ops[eng].append(op)
        for x in r:
            self.readers.setdefault(x, []).append(op)
        for x in w:
            self.last_w[x] = op
            self.readers[x] = []
        return op

    def emit(self, block):
        nc = self.nc
        for e in ENGS:
            for op in self.ops[e]:
                if op["dma"]:
                    k = ("dma", e, self.drot[e] % self.NDMA)
                    self.drot[e] += 1
                    op["sem"] = k
                    op["prev"] = self.dcnt.get(k, 0)
                    self.dcnt[k] = op["prev"] + 16
                    op["val"] = self.dcnt[k]
                elif op["sig"]:
                    self.cnt[e] += 1
                    op["sem"] = e
                    op["val"] = self.cnt[e]
        final_dma = dict(self.dcnt)
        sems = self.sems
        ops = self.ops

        def run(e, eng):
            waited = {}
            for op in ops[e]:
                need = {}
                for d in op["deps"]:
                    k = d["sem"]
                    if d["val"] > need.get(k, 0):
                        need[k] = d["val"]
                if op["dma"] and op["prev"] > 0:
                    k = op["sem"]
                    need[k] = max(need.get(k, 0), op["prev"])
                for k, v in need.items():
                    if waited.get(k, 0) < v:
                        eng.wait_ge(sems[k], v)
                        waited[k] = v
                inst = op["fn"](eng)
                if op["dma"]:
                    inst.then_inc(sems[op["sem"]], 16)
                elif op["sig"]:
                    inst.then_inc(sems[op["sem"]], 1)
            if e == "sp":
                for k, v in final_dma.items():
                    if v > 0 and waited.get(k, 0) < v:
                        eng.wait_ge(sems[k], v)

        for e, starter in (("pe", block.tensor), ("act", block.scalar), ("dve", block.vector), ("pool", block.gpsimd), ("sp", block.sync)):
            if ops[e] or e == "sp":
                starter(lambda eng, e=e: run(e, eng))

        self.reset()

    def dma(self, out, in_, r=(), w=(), q="sp", **kw):
        return self.add(q, lambda eng: eng.dma_start(out=out, in_=in_, **kw), r=r, w=w, dma=True)

    def mm(self, out, lhsT, rhs, start=True, stop=True, r=(), w=(), **kw):
        return self.add("pe", lambda eng: eng.matmul(out, lhsT, rhs, start=start, stop=stop, **kw), r=r, w=w)

    def tr(self, out, in_, ident, r=(), w=()):
        return self.add("pe", lambda eng: eng.transpose(out, in_, ident), r=r, w=w)

    def act(self, out, in_, func, r=(), w=(), **kw):
        return self.add("act", lambda eng: eng.activation(out=out, in_=in_, func=func, **kw), r=r, w=w)

    def tt(self, out, in0, in1, op, r=(), w=(), eng="dve"):
        return self.add(eng, lambda e: e.tensor_tensor(out=out, in0=in0, in1=in1, op=op), r=r, w=w)

    def ts(self, out, in0, s1, s2, op0, op1=None, r=(), w=(), eng="dve", **kw):
        if op1 is None:
            return self.add(eng, lambda e: e.tensor_scalar(out=out, in0=in0, scalar1=s1, scalar2=None, op0=op0, **kw), r=r, w=w)
        return self.add(eng, lambda e: e.tensor_scalar(out=out, in0=in0, scalar1=s1, scalar2=s2, op0=op0, op1=op1, **kw), r=r, w=w)

    def stt(self, out, in0, scalar, in1, op0, op1, r=(), w=()):
        return self.add("dve", lambda e: e.scalar_tensor_tensor(out=out, in0=in0, scalar=scalar, in1=in1, op0=op0, op1=op1), r=r, w=w)

    def copy(self, out, in_, r=(), w=(), eng="dve"):
        if eng == "act":
            return self.add("act", lambda e: e.copy(out=out, in_=in_), r=r, w=w)
        return self.add(eng, lambda e: e.tensor_copy(out=out, in_=in_), r=r, w=w)

    def memset(self, ap, val, r=(), w=(), eng="dve"):
        return self.add(eng, lambda e: e.memset(ap, val), r=r, w=w)


PI = math.pi
SKIP = set()
TWO_PI = 2.0 * math.pi
CW1 = 6.28125
CW2 = TWO_PI - 6.28125
PI_LO = 3.141592


def sincos(P, ang, n, out_sin, out_cos, t1, t2, t3, ti, tag, rsin, rcos, np_=128):
    a = lambda nm: tag + nm
    sl = lambda t: t[0:np_, 0:n]
    P.ts(sl(t1), ang, 1.0 / TWO_PI, None, ALU.mult, r=[a("ang")], w=[a("t1")])
    P.copy(sl(ti), sl(t1), r=[a("t1")], w=[a("ti")])
    P.copy(sl(t1), sl(ti), r=[a("ti")], w=[a("t1")])
    P.stt(sl(t2), sl(t1), -CW1, ang, ALU.mult, ALU.add, r=[a("t1"), a("ang")], w=[a("t2")])
    P.stt(sl(t3), sl(t1), -CW2, sl(t2), ALU.mult, ALU.add, r=[a("t1"), a("t2")], w=[a("t3")])
    P.ts(sl(t1), sl(t3), PI, -TWO_PI, ALU.is_gt, ALU.mult, r=[a("t3")], w=[a("t1")])
    P.tt(sl(t2), sl(t3), sl(t1), ALU.add, r=[a("t3"), a("t1")], w=[a("t2")])
    P.ts(sl(t1), sl(t2), -PI, TWO_PI, ALU.is_lt, ALU.mult, r=[a("t2")], w=[a("t1")])
    P.tt(sl(t3), sl(t2), sl(t1), ALU.add, r=[a("t2"), a("t1")], w=[a("t3")])
    P.ts(sl(t1), sl(t3), PI_LO, -PI_LO, ALU.min, ALU.max, r=[a("t3")], w=[a("t1")])
    P.act(out_sin, sl(t1), AF.Sin, r=[a("t1")], w=[rsin])
    P.ts(sl(t2), sl(t3), PI / 2, None, ALU.add, r=[a("t3")], w=[a("t2")])
    P.ts(sl(t1), sl(t2), PI, -TWO_PI, ALU.is_gt, ALU.mult, r=[a("t2"), a("t1")], w=[a("t1")])
    P.tt(sl(t3), sl(t2), sl(t1), ALU.add, r=[a("t2"), a("t1")], w=[a("t3")])
    P.ts(sl(t2), sl(t3), PI_LO, -PI_LO, ALU.min, ALU.max, r=[a("t3")], w=[a("t2")])
    P.act(out_cos, sl(t2), AF.Sin, r=[a("t2")], w=[rcos])


def build(S, dbg=False, nl=DEPTH, stop=None):
    NB = S // TB
    NKT = S // 128
    nc = bass.Bass("TRN2", target_bir_lowering=False)
    _uid = [0]

    def U(n):
        _uid[0] += 1
        return f"{n}_u{_uid[0]}"

    din = lambda n, s, d=F32: nc.dram_tensor(n, list(s), d, kind="ExternalInput").ap()
    dscr = lambda n, s, d=F32: nc.dram_tensor(n, list(s), d, kind=("ExternalOutput" if dbg else "Internal")).ap()
    x_d = din("x", [S, D])
    out_d = nc.dram_tensor("out", [S, D], F32, kind="ExternalOutput").ap()
    pos_d = din("pos", [1, S], I32)
    cT_d = din("cT", [128, 8])
    cst_d = din("cst", [128, 2])
    wada_d = din("w_ada", [DEPTH, D, 6 * D])
    bada_d = din("b_adaT", [DEPTH, 128, 48])
    n1g_d = din("n1gT", [DEPTH, 128, 8])
    n2g_d = din("n2gT", [DEPTH, 128, 8])
    fg_d = din("fgT", [128, 8])
    win_d = din("w_in", [DEPTH, D, INW])
    wout_d = din("w_out", [DEPTH, D, D])
    w1_d = din("mlp_w1", [DEPTH, D, DFF])
    w2_d = din("mlp_w2", [DEPTH, DFF, D])
    sare_d = din("s_are", [DEPTH, 128, 8])
    saim_d = din("s_aim", [DEPTH, 128, 8])
    sldt_d = din("s_ldt", [DEPTH, 128, 8])
    sbre_d = din("s_bre", [DEPTH, 128, 8, 16])
    sbim_d = din("s_bim", [DEPTH, 128, 8, 16])
    scre_d = din("s_cre", [DEPTH, 128, 8, 16])
    scim_d = din("s_cim", [DEPTH, 128, 8, 16])
    sd_d = din("s_d", [DEPTH, 128, 2])
    sgw_d = din("s_gw", [DEPTH, 256, 256])
    sgb_d = din("s_gb", [DEPTH, 128, 2])
    sng_d = din("s_ng", [DEPTH, 128, 2])
    lam_d = din("lamv", [DEPTH, 256])
    dsg_d = din("dsg", [DEPTH, 128, 1])
    mng_d = din("mng", [DEPTH, 64, 1])
    xT_d = dscr("xT", [D, S])
    cosT_d = dscr("cosT", [128, S])
    sinT_d = dscr("sinT", [128, S])
    uT_d = dscr("uT", [256, S], BF16)
    dqT_d = dscr("dqT", [4, 128, S], BF16)
    dkT_d = dscr("dkT", [4, 128, S], BF16)
    dV_d = dscr("dV", [S, 512], BF16)
    mqT_d = dscr("mqT", [4, 96, S], BF16)
    mkT_d = dscr("mkT", [4, 64, S], BF16)
    mV_d = dscr("mV", [S, 256], BF16)
    mixT_d = dscr("mixT", [D, S], BF16)
    dbgk_d = dscr("dbgk", [128, 32]); dbgg_d = dscr("dbgg", [128, 32]); dbgm_d = dscr("dbgm", [128, 8]); dbgs_d = dscr("dbgs", [128, 32])

    with ExitStack() as top:
        sems = {}
        for e in ENGS:
            sems[e] = top.enter_context(nc.semaphore("s_" + e))
            for i in range(Prog.NDMA):
                sems[("dma", e, i)] = top.enter_context(nc.semaphore(f"d_{e}_{i}"))
        P = Prog(nc, sems)
        gsb = lambda n, s, d=F32: top.enter_context(nc.sbuf_tensor(U(n), list(s), d))
        ident = gsb("ident", [128, 128])
        identb = gsb("identb", [128, 128], BF16)
        onesb = gsb("onesb", [128, 128], BF16)
        tri = gsb("tri", [128, 128], BF16)
        pswap = gsb("pswap", [128, 128], BF16)
        epsc = gsb("epsc", [128, 1])
        cst = gsb("cstc", [128, 2])
        modv = gsb("modv", [128, DEPTH, 48])
        A1 = gsb("A1", [128, DEPTH, 8])
        A2 = gsb("A2", [128, DEPTH, 8])
        fgT = gsb("fgTs", [128, 8])

        with ExitStack() as es:
            sb = lambda n, s, d=F32: es.enter_context(nc.sbuf_tensor(U(n), list(s), d))
            onesf = sb("onesf", [128, 128])
            b1 = sb("b1", [128, 128])
            b2 = sb("b2", [128, 128])
            cT = sb("cTs", [128, 8])
            scT = sb("scT", [128, 8])
            tmp8 = sb("tmp8", [128, 8])
            wa = [sb(f"wa{i}", [128, 8, 512]) for i in range(2)]
            bada = sb("bada", [128, DEPTH, 48])
            ng1 = sb("ng1", [128, DEPTH, 8])
            ng2 = sb("ng2", [128, DEPTH, 8])
            posi = [sb(f"posi{i}", [128, TB], I32) for i in range(2)]
            ang = sb("ang", [128, TB])
            t1 = sb("t1", [128, TB]); t2 = sb("t2", [128, TB]); t3 = sb("t3", [128, TB])
            ti = sb("ti", [128, TB], I32)
            sn = [sb(f"sn{i}", [128, TB]) for i in range(2)]
            cs = [sb(f"cs{i}", [128, TB]) for i in range(2)]
            modps = es.enter_context(nc.psum_tensor(U("modps"), [128, DEPTH * 48], F32))
            with nc.Block() as block:
                P.memset(onesf[:], 1.0, w=["onesf"], eng="pool")
                P.memset(onesb[:], 1.0, w=["onesb"], eng="pool")
                P.memset(epsc[:], EPS, w=["epsc"], eng="pool")
                P.memset(pswap[:], 0.0, w=["pswap"], eng="pool")
                P.add("pool", lambda e: e.affine_select(out=ident[:], in_=onesf[:], pattern=[[-1, 128]], compare_op=ALU.is_equal, fill=0.0, base=0, channel_multiplier=1), r=["onesf"], w=["ident"])
                P.copy(identb[:], ident[:], r=["ident"], w=["identb"], eng="pool")
                P.add("pool", lambda e: e.affine_select(out=tri[:], in_=onesb[:], pattern=[[1, 128]], compare_op=ALU.is_ge, fill=0.0, base=0, channel_multiplier=-1), r=["onesb"], w=["tri"])
                P.add("pool", lambda e: e.affine_select(out=b1[:], in_=onesf[:], pattern=[[-1, 128]], compare_op=ALU.is_equal, fill=0.0, base=-8, channel_multiplier=1), r=["onesf"], w=["b1"])
                P.add("pool", lambda e: e.affine_select(out=b2[:], in_=onesf[:], pattern=[[-1, 128]], compare_op=ALU.is_equal, fill=0.0, base=8, channel_multiplier=1), r=["onesf"], w=["b2"])
                for s0 in (0, 64):
                    P.copy(pswap[:, s0:s0 + 8], b1[:, s0:s0 + 8], r=["b1", "pswap"], w=["pswap"], eng="pool")
                    P.copy(pswap[:, s0 + 8:s0 + 16], b2[:, s0 + 8:s0 + 16], r=["b2", "pswap"], w=["pswap"], eng="pool")
                P.dma(cT[:], cT_d, w=["cT"])
                P.dma(cst[:], cst_d, w=["cst"])
                P.dma(bada[:], bada_d.rearrange("l p j -> p l j"), w=["bada"])
                P.dma(ng1[:], n1g_d.rearrange("l p j -> p l j"), w=["ng1"])
                P.dma(ng2[:], n2g_d.rearrange("l p j -> p l j"), w=["ng2"])
                P.dma(fgT[:], fg_d, w=["fgT"])
                P.act(tmp8[:], cT[:], AF.Exp, r=["cT"], w=["tmp8"], scale=-1.0)
                P.ts(tmp8[:], tmp8[:], 1.0, None, ALU.add, r=["tmp8"], w=["tmp8"])
                P.add("dve", lambda e: e.reciprocal(out=scT[:], in_=tmp8[:]), r=["tmp8"], w=["scT"])
                P.tt(scT[:], scT[:], cT[:], ALU.mult, r=["scT", "cT"], w=["scT"])
                k = 0
                for l in range(DEPTH):
                    wv = wada_d[l].rearrange("(kt p) n -> p kt n", p=128)
                    for cb in range(12):
                        wb_ = wa[k % 2]; wn = f"wa{k % 2}"; k += 1
                        P.dma(wb_[:], wv[:, :, cb * 512:(cb + 1) * 512], w=[wn])
                        for j in range(4):
                            col = l * 48 + cb * 4 + j
                            for kt in range(8):
                                P.mm(modps[:, col:col + 1], wb_[:, kt, j * 128:(j + 1) * 128], scT[:, kt:kt + 1],
                                     start=(kt == 0), stop=(kt == 7), r=[wn, "scT"], w=["modps"])
                P.tt(modv[:].rearrange("p l j -> p (l j)"), modps[:], bada[:].rearrange("p l j -> p (l j)"), ALU.add, r=["modps", "bada"], w=["modv"])
                for l in range(DEPTH):
                    P.stt(A1[:, l, :], modv[:, l, 8:16], 1.0, ng1[:, l, :], ALU.add, ALU.mult, r=["modv", "ng1"], w=["A1"])
                    P.stt(A2[:, l, :], modv[:, l, 32:40], 1.0, ng2[:, l, :], ALU.add, ALU.mult, r=["modv", "ng2"], w=["A2"])
                for i in range(NB):
                    pi_ = posi[i % 2]; pn_ = f"posi{i % 2}"
                    P.dma(pi_[:], pos_d[0:1, i * TB:(i + 1) * TB].to_broadcast([128, TB]), w=[pn_])
                    P.copy(t1[:], pi_[:], r=[pn_], w=["rt1"])
                    P.ts(ang[:], t1[:], cst[:, 0:1], None, ALU.mult, r=["rt1", "cst"], w=["rang"])
                    sincos(P, ang[:], TB, sn[i % 2][:], cs[i % 2][:], t1, t2, t3, ti, "r", f"sn{i % 2}", f"cs{i % 2}")
                    P.ts(sn[i % 2][:], sn[i % 2][:], cst[:, 1:2], None, ALU.mult, r=[f"sn{i % 2}", "cst"], w=[f"sn{i % 2}"])
                    P.dma(sinT_d[:, i * TB:(i + 1) * TB], sn[i % 2][:], r=[f"sn{i % 2}"], w=["sinT"])
                    P.dma(cosT_d[:, i * TB:(i + 1) * TB], cs[i % 2][:], r=[f"cs{i % 2}"], w=["cosT"])
                P.emit(block)

        if stop == "0":
            return nc
        with ExitStack() as es:
            sb = lambda n, s, d=F32: es.enter_context(nc.sbuf_tensor(U(n), list(s), d))
            xin = [sb(f"xin{i}", [128, D]) for i in range(3)]
            xo = [sb(f"xo{i}", [128, 8, TB]) for i in range(2)]
            pst = [es.enter_context(nc.psum_tensor(U(f"pst{i}"), [128, TB], F32)) for i in range(8)]
            with nc.Block() as block:
                k = 0
                for i in range(NB):
                    o_ = xo[i % 2]; on = f"xo{i % 2}"
                    for tt in range(4):
                        xi = xin[k % 3]; xn = f"xin{k % 3}"; k += 1
                        P.dma(xi[:], x_d[i * TB + tt * 128:i * TB + (tt + 1) * 128, :], w=[xn])
                        for ft in range(8):
                            P.tr(pst[ft][:, tt * 128:(tt + 1) * 128], xi[:, ft * 128:(ft + 1) * 128], ident[:], r=[xn], w=[f"pst{ft}"])
                    for ft in range(8):
                        P.copy(o_[:, ft, :], pst[ft][:], r=[f"pst{ft}"], w=[f"{on}_{ft}"], eng=("act" if ft % 2 else "dve"))
                    P.dma(xT_d.rearrange("(ft p) t -> p ft t", p=128)[:, :, i * TB:(i + 1) * TB], o_[:], r=[f"{on}_{ft}" for ft in range(8)], w=["xTd"])
                P.emit(block)

        if stop == "T0":
            return nc
        xTv = xT_d.rearrange("(ft p) t -> p ft t", p=128)

        def load_cast(P, dst3, src2, nk, ncol, name, q="pool", step=2048, col_major=False):
            sv = src2.rearrange("(kt p) n -> p kt n", p=128)
            chunks = [(kt, c0) for kt in range(nk) for c0 in range(0, ncol, step)]
            if col_major:
                chunks.sort(key=lambda t: (t[1], t[0]))
            for kt, c0 in chunks:
                c1 = min(ncol, c0 + step)
                P.dma(dst3[:, kt, c0:c1], sv[:, kt, c0:c1], w=[f"{name}_{kt}_{c0 // step}"], q=q)

        def norm_mod(P, xb, xn, hT, hn, A, B, l, pn, tmps, sqs, rstd, lnv, pool_share=True):
            for kt in range(8):
                sq = sqs[kt % 2]; sqn = f"sq{kt % 2}"
                P.act(sq[:], xb[:, kt, :], AF.Square, r=[xn], w=[sqn])
                P.mm(pn[:], onesb[:], sq[:], start=(kt == 0), stop=(kt == 7), r=[sqn, "onesb"], w=["pn"])
            P.act(lnv[:], pn[:], AF.Ln, r=["pn", "epsc"], w=["lnv"], scale=1.0 / D, bias=epsc[:, 0:1])
            P.act(rstd[:], lnv[:], AF.Exp, r=["lnv"], w=["rstd"], scale=-0.5)
            for kt in range(8):
                tm = tmps[kt % 2]; tn = f"nt{kt % 2}"
                P.tt(tm[:], xb[:, kt, :], rstd[:], ALU.mult, r=[xn, "rstd"], w=[tn], eng=("pool" if (pool_share and kt % 2) else "dve"))
                P.act(hT[:, kt, :], tm[:], AF.Identity, r=[tn, "A", "modv"], w=[f"{hn}_{kt}"], scale=A[:, l, kt:kt + 1], bias=B(kt))

        for l in range(nl):
            with ExitStack() as es:
                sb = lambda n, s, d=F32: es.enter_context(nc.sbuf_tensor(U(n), list(s), d))
                win = sb("win", [128, 8, INW], BF16)
                xbs = [sb(f"xb{i}", [128, 8, TB]) for i in range(2)]
                hTs = [sb(f"hT{i}", [128, 8, TB], BF16) for i in range(2)]
                sqs = [sb(f"sq{i}", [128, TB], BF16) for i in range(2)]
                tmps = [sb(f"nt{i}", [128, TB]) for i in range(2)]
                rstd = sb("rstd", [128, TB]); lnv = sb("lnv", [128, TB])
                cosb = [sb(f"cosb{i}", [128, TB]) for i in range(2)]
                sinb = [sb(f"sinb{i}", [128, TB]) for i in range(2)]
                qb = [sb(f"qb{i}", [128, TB], BF16) for i in range(2)]
                r1 = [sb(f"r1_{i}", [128, TB]) for i in range(2)]
                r2 = [sb(f"r2_{i}", [128, TB]) for i in range(2)]
                rf = [sb(f"rf{i}", [128, TB]) for i in range(2)]
                stg = [sb(f"stg{i}", [128, TB], BF16) for i in range(4)]
                vst = [sb(f"vst{i}", [128, 768], BF16) for i in range(2)]
                kmT = [sb(f"kmT{j}", [128, 32]) for j in range(2)]
                gm8 = [sb(f"gm8_{j}", [128, 8, 32]) for j in range(2)]
                m8a = [sb(f"m8a{j}", [128, 8, 8]) for j in range(2)]
                sel8 = [sb(f"sel8_{j}", [128, 8, 32]) for j in range(2)]
                neg8 = [sb(f"neg8_{j}", [128, 8, 32], BF16) for j in range(2)]
                nst = [sb(f"nst{i}", [32, TB], BF16) for i in range(4)]
                NPA = 3
                pa = [es.enter_context(nc.psum_tensor(U(f"pa{i}"), [128, TB], F32)) for i in range(NPA)]
                pb = [es.enter_context(nc.psum_tensor(U(f"pb{i}"), [128, TB], F32)) for i in range(2)]
                pn = es.enter_context(nc.psum_tensor(U("pn"), [128, TB], F32))
                pgs = [es.enter_context(nc.psum_tensor(U(f"pg{i}"), [128, TB], F32)) for i in range(2)]
                with nc.Block() as block:
                    load_cast(P, win, win_d[l], 8, INW, "win", step=1280, col_major=True)
                    for j in range(2):
                        P.memset(kmT[j][:], 0.0, w=[f"kmT{j}"], eng="pool")
                    cnt = {"pa": 0, "pb": 0, "stg": 0, "qb": 0, "r": 0, "v": 0, "ns": 0}

                    def load_blk(i):
                        P.dma(xbs[i % 2][:], xTv[:, :, i * TB:(i + 1) * TB], w=[f"xb{i % 2}"])
                        P.dma(cosb[i % 2][:], cosT_d[:, i * TB:(i + 1) * TB], w=[f"cosb{i % 2}"])
                        P.dma(sinb[i % 2][:], sinT_d[:, i * TB:(i + 1) * TB], w=[f"sinb{i % 2}"])

                    load_blk(0)
                    for i in range(NB):
                        t0 = i * TB
                        if i + 1 < NB:
                            load_blk(i + 1)
                        xb = xbs[i % 2]; xn = f"xb{i % 2}"; hT = hTs[i % 2]; hn = f"hT{i % 2}"
                        cb_, cbn = cosb[i % 2], f"cosb{i % 2}"
                        sb_, sbn = sinb[i % 2], f"sinb{i % 2}"
                        norm_mod(P, xb, xn, hT, hn, A1, lambda kt: modv[:, l, kt:kt + 1], l, pn, tmps, sqs, rstd, lnv)
                        hres = [f"{hn}_{kt}" for kt in range(8)]

                        def proj(c0):
                            p_ = pa[cnt["pa"] % NPA]; pn_ = f"pa{cnt['pa'] % NPA}"; cnt["pa"] += 1
                            for kt in range(8):
                                P.mm(p_[:], win[:, kt, c0:c0 + 128], hT[:, kt, :], start=(kt == 0), stop=(kt == 7), r=[f"win_{kt}_{c0 // 1280}", hres[kt]], w=[pn_])
                            return p_, pn_

                        RENG = "dve" if "nopool" in SKIP else "pool"

                        def rope(p_, pn_, want_f32):
                            q_ = qb[cnt["qb"] % 2]; qn = f"qb{cnt['qb'] % 2}"; cnt["qb"] += 1
                            P.copy(q_[:], p_[:], r=[pn_], w=[qn], eng="act")
                            s_ = pb[cnt["pb"] % 2]; sn_ = f"pb{cnt['pb'] % 2}"; cnt["pb"] += 1
                            if "nosw" not in SKIP:
                                P.mm(s_[:], pswap[:], q_[:], r=[qn, "pswap"], w=[sn_])
                            else:
                                s_, sn_ = p_, pn_
                            k_ = cnt["r"] % 2; cnt["r"] += 1
                            P.tt(r1[k_][:], p_[:], cb_[:], ALU.mult, r=[pn_, cbn, qn], w=[f"r1_{k_}"])
                            P.tt(r2[k_][:], s_[:], sb_[:], ALU.mult, r=[sn_, sbn], w=[f"r2_{k_}"])
                            g_ = stg[cnt["stg"] % 4]; gn = f"stg{cnt['stg'] % 4}"; cnt["stg"] += 1
                            if want_f32:
                                P.tt(rf[k_][:], r1[k_][:], r2[k_][:], ALU.add, r=[f"r1_{k_}", f"r2_{k_}"], w=[f"rf{k_}"], eng=RENG)
                                P.copy(g_[:], rf[k_][:], r=[f"rf{k_}"], w=[gn], eng="act")
                                return rf[k_], f"rf{k_}", g_, gn
                            P.tt(g_[:], r1[k_][:], r2[k_][:], ALU.add, r=[f"r1_{k_}", f"r2_{k_}"], w=[gn], eng=RENG)
                            return None, None, g_, gn

                        for j in range(2):
                            p_, pn_ = proj(j * 128)
                            g_ = stg[cnt["stg"] % 4]; gn = f"stg{cnt['stg'] % 4}"; cnt["stg"] += 1
                            P.copy(g_[:], p_[:], r=[pn_], w=[gn], eng="act")
                            P.dma(uT_d[j * 128:(j + 1) * 128, t0:t0 + TB], g_[:], r=[gn], w=["uTd"])
                        for h in range(0 if "rope" in SKIP else 4):
                            p_, pn_ = proj(256 + h * 128)
                            _, _, g_, gn = rope(p_, pn_, False)
                            if "nodma" not in SKIP:
                                P.dma(dqT_d[h, :, t0:t0 + TB], g_[:], r=[gn], w=["dqTd"])
                        for h in range(0 if "rope" in SKIP else 4):
                            p_, pn_ = proj(768 + h * 128)
                            _, _, g_, gn = rope(p_, pn_, False)
                            if "nodma" not in SKIP:
                                P.dma(dkT_d[h, :, t0:t0 + TB], g_[:], r=[gn], w=["dkTd"])
                        for j in range(0 if "mk" in SKIP else 2):
                            p_, pn_ = proj(2048 + j * 128)
                            f_, fn_, g_, gn = rope(p_, pn_, True)
                            P.add("dve", lambda e, f_=f_, j=j, i=i: e.tensor_reduce(out=kmT[j][:, 2 * i:2 * i + 2], in_=f_[:].rearrange("p (b k) -> p b k", k=256), axis=AX.X, op=ALU.add),
                                  r=[fn_], w=[f"kmT{j}"])
                            P.ts(kmT[j][:, 2 * i:2 * i + 2], kmT[j][:, 2 * i:2 * i + 2], 1.0 / 256, None, ALU.mult, r=[f"kmT{j}"], w=[f"kmT{j}"])
                            for hh in range(2):
                                P.dma(mkT_d[2 * j + hh, :, t0:t0 + TB], g_[hh * 64:(hh + 1) * 64, :], r=[gn], w=["mkTd"])
                        pend = []
                        for j in range(0 if "mq" in SKIP else 2):
                            p_, pn_ = proj(1792 + j * 128)
                            f_, fn_, g_, gn = rope(p_, pn_, True)
                            for hh in range(2):
                                P.dma(mqT_d[2 * j + hh, 0:64, t0:t0 + TB], g_[hh * 64:(hh + 1) * 64, :], r=[gn], w=["mqTd"])
                            for hh in range(0 if "nogate" in SKIP else 2):
                                hs = slice(hh * 64, (hh + 1) * 64)
                                for tt in range(4):
                                    gcol = j * 128 + tt * 32
                                    P.mm(pgs[hh][:, gcol:gcol + 32], f_[hs, tt * 128:(tt + 1) * 128], kmT[j][hs, :], r=[fn_, f"kmT{j}"], w=[f"pg{hh}"])
                            gm_ = gm8[j]; gmn = f"gm8_{j}"
                            P.memset(gm_[:], -1e30, w=[gmn])
                            for hh in range(2):
                                for tt in range(4):
                                    k8 = hh * 4 + tt
                                    own = (t0 + tt * 128) // 256
                                    gcol = j * 128 + tt * 32
                                    if own > 0:
                                        P.copy(gm_[:, k8, 0:own], pgs[hh][:, gcol:gcol + own], r=[f"pg{hh}", gmn], w=[gmn + f"_{k8}"])
                            for hh in range(0 if "nochain" in SKIP else 2):
                                for tt in range(4):
                                    k8 = hh * 4 + tt
                                    own = (t0 + tt * 128) // 256
                                    P.add("dve", lambda e, k8=k8, j=j: e.max(out=m8a[j][:, k8, :], in_=gm8[j][:, k8, :]), r=[gmn, gmn + f"_{k8}"], w=[f"m8a{j}_{k8}"])
                                    P.tt(sel8[j][:, k8, :], gm_[:, k8, :], m8a[j][:, k8, 2:3].to_broadcast([128, 32]), ALU.is_ge, r=[gmn + f"_{k8}", f"m8a{j}_{k8}"], w=[f"sel8_{j}_{k8}"])
                                    P.ts(neg8[j][:, k8, :], sel8[j][:, k8, :], 30000.0, -30000.0, ALU.mult, ALU.add, r=[f"sel8_{j}_{k8}"], w=[f"neg8_{j}_{k8}"])
                                    P.memset(neg8[j][:, k8, own:own + 1], 0.0, r=[f"neg8_{j}_{k8}"], w=[f"neg8_{j}_{k8}"])
                            pend.append(j)
                        for tt in range(0 if "v" in SKIP else 4):
                            v_ = vst[cnt["v"] % 2]; vn = f"vst{cnt['v'] % 2}"; cnt["v"] += 1
                            p_ = pa[cnt["pa"] % NPA]; pn_ = f"pa{cnt['pa'] % NPA}"; cnt["pa"] += 1
                            for kt in range(8):
                                P.mm(p_[:], hT[:, kt, tt * 128:(tt + 1) * 128], win[:, kt, 1280:1792], start=(kt == 0), stop=(kt == 7), r=[f"win_{kt}_1", hres[kt]], w=[pn_])
                            P.copy(v_[:, 0:512], p_[:], r=[pn_], w=[vn + "a"], eng="act")
                            p2 = pa[cnt["pa"] % NPA]; pn2 = f"pa{cnt['pa'] % NPA}"; cnt["pa"] += 1
                            for kt in range(8):
                                P.mm(p2[:, 0:256], hT[:, kt, tt * 128:(tt + 1) * 128], win[:, kt, 2304:2560], start=(kt == 0), stop=(kt == 7), r=[f"win_{kt}_1", hres[kt]], w=[pn2])
                            P.copy(v_[:, 512:768], p2[:, 0:256], r=[pn2], w=[vn + "b"], eng=("act" if "vact" in SKIP else "dve"))
                            P.dma(dV_d[t0 + tt * 128:t0 + (tt + 1) * 128, :], v_[:, 0:512], r=[vn + "a"], w=["dVd"])
                            P.dma(mV_d[t0 + tt * 128:t0 + (tt + 1) * 128, :], v_[:, 512:768], r=[vn + "b"], w=["mVd"])
                        for j in ([] if "notr" in SKIP else pend):
                            for hh in range(2):
                                pgb = pgs[hh][:].bitcast(BF16)
                                for tt in range(4):
                                    tcol = 512 + tt * 128
                                    P.tr(pgb[0:32, tcol:tcol + 128], neg8[j][:, hh * 4 + tt, :], identb[:], r=[f"neg8_{j}_{hh * 4 + tt}", "identb"], w=[f"pg{hh}"])
                            for hh in range(2):
                                pgb = pgs[hh][:].bitcast(BF16)
                                ns_ = nst[cnt["ns"] % 4]; nsn = f"nst{cnt['ns'] % 4}"; cnt["ns"] += 1
                                P.copy(ns_[:], pgb[0:32, 512:1024], r=[f"pg{hh}"], w=[nsn])
                                P.dma(mqT_d[2 * j + hh, 64:96, t0:t0 + TB], ns_[:], r=[nsn], w=["mqTd"])
                    P.emit(block)

            if stop == "A":
                return nc
            with ExitStack() as es:
                sb = lambda n, s, d=F32: es.enter_context(nc.sbuf_tensor(U(n), list(s), d))
                uTs = [sb(f"uTs{h}", [128, S], BF16) for h in range(2)]
                are = sb("are", [128, 8]); aim = sb("aim", [128, 8]); ldt = sb("ldt", [128, 8])
                dt_ = sb("dt_", [128, 8]); mag = sb("mag", [128, 8]); th = sb("th", [128, 8])
                sth = sb("sth", [128, 8]); cth = sb("cth", [128, 8])
                s1 = sb("s1", [128, 8]); s2 = sb("s2", [128, 8]); s3 = sb("s3", [128, 8]); si = sb("si", [128, 8], I32)
                abr = sb("abr", [128, 8]); abi = sb("abi", [128, 8]); rden = sb("rden", [128, 8])
                kr = sb("kr", [128, 8]); ki = sb("ki", [128, 8]); nki = sb("nki", [128, 8])
                bre = sb("bre", [128, 8, 16]); bim = sb("bim", [128, 8, 16])
                cre = sb("cre", [128, 8, 16]); cim = sb("cim", [128, 8, 16])
                bbr = sb("bbr", [128, 8, 16]); bbi = sb("bbi", [128, 8, 16]); tma = sb("tma", [128, 16])
                X = [sb(f"X{i}", [128, 128]) for i in range(2)]
                BD = sb("BD", [128, 16, 128], BF16)
                CDm = sb("CD", [128, 16, 128], BF16)
                sdv = sb("sdv", [128, 2]); sgb = sb("sgb", [128, 2]); nsgb = sb("nsgb", [128, 2]); sng = sb("sng", [128, 2])
                gw = sb("gw", [128, 2, 256], BF16)
                jidi = sb("jidi", [128, TB], I32); jid = sb("jid", [128, TB])
                tA = sb("tA", [128, TB]); q1 = sb("q1", [128, TB]); q2 = sb("q2", [128, TB]); q3 = sb("q3", [128, TB]); qi = sb("qi", [128, TB], I32)
                Cs = sb("Cs", [128, 8, TB]); Sn = sb("Sn", [128, 8, TB])
                Hre = sb("Hre", [128, 8]); Him = sb("Him", [128, 8])
                NW = 3
                mt = [[sb(f"m{k}_{w}", [128, TB]) for k in range(4)] for w in range(NW)]
                bp = [[sb(f"bp{k}_{w}", [128, TB]) for k in range(2)] for w in range(NW)]
                gg = [[sb(f"g{k}_{w}", [128, TB]) for k in range(2)] for w in range(NW)]
                pp = [[sb(f"p{k}_{w}", [128, TB]) for k in range(4)] for w in range(NW)]
                hb = [[sb(f"hb{k}_{w}", [128, TB], BF16) for k in range(2)] for w in range(NW)]
                yv = [sb(f"yv{h}", [128, TB]) for h in range(2)]
                ya = sb("ya", [128, TB]); yb = sb("yb", [128, TB]); yc = sb("yc", [128, TB])
                gl = [sb(f"gl{h}", [128, TB]) for h in range(2)]
                glb = [sb(f"glb{h}", [128, TB], BF16) for h in range(2)]
                zz = [sb(f"zz{h}", [128, TB]) for h in range(2)]
                zsq = [sb(f"zsq{h}", [128, TB], BF16) for h in range(2)]
                rs_ = sb("rs_", [128, TB]); ln_ = sb("ln_", [128, TB])
                mo = [sb(f"mo{h}", [128, TB], BF16) for h in range(2)]
                pbu = [es.enter_context(nc.psum_tensor(U(f"pbu{i}"), [128, TB], F32)) for i in range(4)]
                py = [es.enter_context(nc.psum_tensor(U(f"py{i}"), [128, TB], F32)) for i in range(2)]
                pgx = es.enter_context(nc.psum_tensor(U("pgx"), [128, TB], F32))
                pn2 = es.enter_context(nc.psum_tensor(U("pn2"), [128, TB], F32))
                with nc.Block() as block:
                    for h in range(2):
                        P.dma(uTs[h][:], uT_d[h * 128:(h + 1) * 128, :], w=[f"uTs{h}"])
                    for t_, d_, n_ in ((are, sare_d, "are"), (aim, saim_d, "aim"), (ldt, sldt_d, "ldt"), (bre, sbre_d, "bre"), (bim, sbim_d, "bim"),
                                       (cre, scre_d, "cre"), (cim, scim_d, "cim"), (sdv, sd_d, "sdv"), (sgb, sgb_d, "sgb"), (sng, sng_d, "sng")):
                        P.dma(t_[:], d_[l], w=[n_])
                    load_cast(P, gw, sgw_d[l], 2, 256, "gw")
                    P.act(dt_[:], ldt[:], AF.Exp, r=["ldt"], w=["dt_"])
                    P.tt(s1[:], are[:], dt_[:], ALU.mult, r=["are", "dt_"], w=["s1"])
                    P.act(mag[:], s1[:], AF.Exp, r=["s1"], w=["mag"])
                    P.tt(th[:], aim[:], dt_[:], ALU.mult, r=["aim", "dt_"], w=["sang"])
                    sincos(P, th[:], 8, sth[:], cth[:], s1, s2, s3, si, "s", "sth", "cth")
                    P.tt(abr[:], mag[:], cth[:], ALU.mult, r=["mag", "cth"], w=["abr"])
                    P.tt(abi[:], mag[:], sth[:], ALU.mult, r=["mag", "sth"], w=["abi"])
                    P.tt(s1[:], are[:], are[:], ALU.mult, r=["are", "st1"], w=["s1b"])
                    P.tt(s2[:], aim[:], aim[:], ALU.mult, r=["aim", "st2"], w=["s2b"])
                    P.tt(s1[:], s1[:], s2[:], ALU.add, r=["s1b", "s2b"], w=["s1b"])
                    P.add("dve", lambda e: e.reciprocal(out=rden[:], in_=s1[:]), r=["s1b"], w=["rden"])
                    P.ts(s3[:], abr[:], -1.0, None, ALU.add, r=["abr", "st3"], w=["nr"])
                    P.tt(s1[:], s3[:], are[:], ALU.mult, r=["nr", "are", "rden"], w=["s1c"])
                    P.tt(s2[:], abi[:], aim[:], ALU.mult, r=["abi", "aim", "s2b"], w=["s2c"])
                    P.tt(s1[:], s1[:], s2[:], ALU.add, r=["s1c", "s2c"], w=["s1c"])
                    P.tt(kr[:], s1[:], rden[:], ALU.mult, r=["s1c", "rden"], w=["kr"])
                    P.tt(s1[:], abi[:], are[:], ALU.mult, r=["abi", "are", "kr"], w=["s1d"])
                    P.tt(s2[:], s3[:], aim[:], ALU.mult, r=["nr", "aim", "s1c"], w=["s2d"])
                    P.tt(s1[:], s1[:], s2[:], ALU.subtract, r=["s1d", "s2d"], w=["s1d"])
                    P.tt(ki[:], s1[:], rden[:], ALU.mult, r=["s1d", "rden"], w=["ki"])
                    P.ts(nki[:], ki[:], -1.0, None, ALU.mult, r=["ki"], w=["nki"])
                    P.ts(nsgb[:], sgb[:], -1.0, None, ALU.mult, r=["sgb"], w=["nsgb"])
                    P.memset(BD[:], 0.0, w=["BD"], eng="pool")
                    P.memset(CDm[:], 0.0, w=["CD"], eng="pool")
                    P.memset(Hre[:], 0.0, w=["Hre"], eng="pool")
                    P.memset(Him[:], 0.0, w=["Him"], eng="pool")
                    for st in range(8):
                        P.ts(tma[:], bre[:, st, :], kr[:, st:st + 1], None, ALU.mult, r=["bre", "kr", "bbr"], w=["tma"])
                        P.stt(bbr[:, st, :], bim[:, st, :], nki[:, st:st + 1], tma[:], ALU.mult, ALU.add, r=["bim", "nki", "tma"], w=["bbr"])
                        P.ts(tma[:], bim[:, st, :], kr[:, st:st + 1], None, ALU.mult, r=["bim", "kr", "bbr"], w=["tma"])
                        P.stt(bbi[:, st, :], bre[:, st, :], ki[:, st:st + 1], tma[:], ALU.mult, ALU.add, r=["bre", "ki", "tma"], w=["bbi"])
                    kx = 0
                    for st in range(8):
                        gl0 = (2 * st) % 8
                        for ri, bsrc, bn in ((0, bbr, "bbr"), (1, bbi, "bbi")):
                            X_ = X[kx % 2]; Xn = f"X{kx % 2}"; kx += 1
                            P.memset(X_[:], 0.0, w=[Xn], eng="pool")
                            P.copy(X_[0:64, gl0 * 16:(gl0 + 1) * 16], bsrc[0:64, st, :], r=[bn, Xn], w=[Xn])
                            P.copy(X_[64:128, (gl0 + 1) * 16:(gl0 + 2) * 16], bsrc[64:128, st, :], r=[bn, Xn], w=[Xn])
                            pt_ = pbu[kx % 4]; ptn = f"pbu{kx % 4}"
                            P.tr(pt_[:, 0:128], X_[:], ident[:], r=[Xn], w=[ptn])
                            P.copy(BD[:, st * 2 + ri, :], pt_[:, 0:128], r=[ptn, "BD"], w=["BD"], eng="act")
                        P.copy(CDm[0:64, st * 2, gl0 * 16:(gl0 + 1) * 16], cre[0:64, st, :], r=["cre", "CD"], w=["CD"])
                        P.copy(CDm[64:128, st * 2, (gl0 + 1) * 16:(gl0 + 2) * 16], cre[64:128, st, :], r=["cre", "CD"], w=["CD"])
                        P.ts(CDm[0:64, st * 2 + 1, gl0 * 16:(gl0 + 1) * 16], cim[0:64, st, :], -1.0, None, ALU.mult, r=["cim", "CD"], w=["CD"])
                        P.ts(CDm[64:128, st * 2 + 1, (gl0 + 1) * 16:(gl0 + 2) * 16], cim[64:128, st, :], -1.0, None, ALU.mult, r=["cim", "CD"], w=["CD"])
                    P.add("pool", lambda e: e.iota(jidi[:], pattern=[[1, TB]], base=1, channel_multiplier=0), w=["jidi"])
                    P.copy(jid[:], jidi[:], r=["jidi"], w=["jid"])
                    for st in range(8):
                        P.ts(tA[:], jid[:], th[:, st:st + 1], None, ALU.mult, r=["jid", "sang", "tsin", "tcos"], w=["tang"])
                        sincos(P, tA[:], TB, Sn[:, st, :], Cs[:, st, :], q1, q2, q3, qi, "t", "Sn", "Cs")
                    it = 0
                    for i in range(NB):
                        t0 = i * TB
                        for st in range(8):
                            w_ = it % NW; it += 1
                            half = st // 4
                            m = mt[w_]; mn = [f"m{k}_{w_}" for k in range(4)]
                            b_ = bp[w_]; bn_ = [f"bp{k}_{w_}" for k in range(2)]
                            g_ = gg[w_]; gn_ = [f"g{k}_{w_}" for k in range(2)]
                            p_ = pp[w_]; pn_ = [f"p{k}_{w_}" for k in range(4)]
                            h_ = hb[w_]; hn_ = [f"hb{k}_{w_}" for k in range(2)]
                            pr, prn = pbu[(2 * it) % 4], f"pbu{(2 * it) % 4}"
                            pi2, pin = pbu[(2 * it + 1) % 4], f"pbu{(2 * it + 1) % 4}"
                            P.mm(pr[:], BD[:, st * 2, :], uTs[half][:, t0:t0 + TB], r=["BD", f"uTs{half}"], w=[prn])
                            P.mm(pi2[:], BD[:, st * 2 + 1, :], uTs[half][:, t0:t0 + TB], r=["BD", f"uTs{half}"], w=[pin])
                            cs_, sn_ = Cs[:, st, :], Sn[:, st, :]
                            P.tt(m[0][:], pr[:], cs_, ALU.mult, r=[prn, "Cs"], w=[mn[0]])
                            P.tt(m[1][:], pi2[:], sn_, ALU.mult, r=[pin, "Sn"], w=[mn[1]])
                            P.tt(m[2][:], pi2[:], cs_, ALU.mult, r=[pin, "Cs"], w=[mn[2]])
                            P.tt(m[3][:], pr[:], sn_, ALU.mult, r=[prn, "Sn"], w=[mn[3]])
                            P.tt(b_[0][:], m[0][:], m[1][:], ALU.add, r=[mn[0], mn[1]], w=[bn_[0]], eng="pool")
                            P.tt(b_[1][:], m[2][:], m[3][:], ALU.subtract, r=[mn[2], mn[3]], w=[bn_[1]], eng="pool")
                            rbc = mag[:, st:st + 1].to_broadcast([128, TB])
                            P.add("dve", lambda e, o=g_[0], d1=b_[0], ini=Hre[:, st:st + 1], rbc=rbc: e.tensor_tensor_scan(out=o[:], data0=rbc, data1=d1[:], initial=ini, op0=ALU.mult, op1=ALU.add),
                                  r=[bn_[0], "mag", "Hre"], w=[gn_[0]])
                            P.add("dve", lambda e, o=g_[1], d1=b_[1], ini=Him[:, st:st + 1], rbc=rbc: e.tensor_tensor_scan(out=o[:], data0=rbc, data1=d1[:], initial=ini, op0=ALU.mult, op1=ALU.add),
                                  r=[bn_[1], "mag", "Him"], w=[gn_[1]])
                            P.tt(p_[0][:], g_[0][:], cs_, ALU.mult, r=[gn_[0], "Cs"], w=[pn_[0]])
                            P.tt(p_[1][:], g_[1][:], sn_, ALU.mult, r=[gn_[1], "Sn"], w=[pn_[1]])
                            P.tt(p_[2][:], g_[0][:], sn_, ALU.mult, r=[gn_[0], "Sn"], w=[pn_[2]], eng="pool")
                            P.tt(p_[3][:], g_[1][:], cs_, ALU.mult, r=[gn_[1], "Cs"], w=[pn_[3]], eng="pool")
                            P.tt(h_[0][:], p_[0][:], p_[1][:], ALU.subtract, r=[pn_[0], pn_[1]], w=[hn_[0]], eng="pool")
                            P.tt(h_[1][:], p_[2][:], p_[3][:], ALU.add, r=[pn_[2], pn_[3]], w=[hn_[1]], eng="pool")
                            P.tt(Hre[:, st:st + 1], p_[0][:, TB - 1:TB], p_[1][:, TB - 1:TB], ALU.subtract, r=[pn_[0], pn_[1], "Hre"], w=["Hre"])
                            P.tt(Him[:, st:st + 1], p_[2][:, TB - 1:TB], p_[3][:, TB - 1:TB], ALU.add, r=[pn_[2], pn_[3], "Him"], w=["Him"])
                            P.mm(py[half][:], CDm[:, st * 2, :], h_[0][:], start=(st % 4 == 0), stop=False, r=["CD", hn_[0]], w=[f"py{half}"])
                            P.mm(py[half][:], CDm[:, st * 2 + 1, :], h_[1][:], start=False, stop=(st % 4 == 3), r=["CD", hn_[1]], w=[f"py{half}"])
                        for h in range(2):
                            P.stt(yv[h][:], uTs[h][:, t0:t0 + TB], sdv[:, h:h + 1], py[h][:], ALU.mult, ALU.add, r=[f"uTs{h}", "sdv", f"py{h}"], w=[f"yv{h}"])
                            P.act(ya[:], yv[h][:], AF.Square, r=[f"yv{h}"], w=["ya"])
                            P.ts(yb[:], ya[:], 0.044715, 1.0, ALU.mult, ALU.add, r=["ya"], w=["yb"])
                            P.tt(yc[:], yb[:], yv[h][:], ALU.mult, r=["yb", f"yv{h}"], w=["yc"], eng="pool")
                            P.act(ya[:], yc[:], AF.Sigmoid, r=["yc"], w=["ya"], scale=1.5957691216)
                            P.tt(gl[h][:], yv[h][:], ya[:], ALU.mult, r=[f"yv{h}", "ya"], w=[f"gl{h}"], eng="pool")
                            P.copy(glb[h][:], gl[h][:], r=[f"gl{h}"], w=[f"glb{h}"], eng="act")
                        for oh in range(2):
                            for kh in range(2):
                                P.mm(pgx[:], gw[:, kh, oh * 128:(oh + 1) * 128], glb[kh][:], start=(kh == 0), stop=(kh == 1), r=[f"gw_{kh}_0", f"glb{kh}"], w=["pgx"])
                            P.act(yc[:], pgx[:], AF.Sigmoid, r=["pgx", "sgb"], w=["yc"], bias=sgb[:, oh:oh + 1])
                            P.tt(zz[oh][:], gl[oh][:], yc[:], ALU.mult, r=[f"gl{oh}", "yc"], w=[f"zz{oh}"], eng="pool")
                            P.act(zsq[oh][:], zz[oh][:], AF.Square, r=[f"zz{oh}"], w=[f"zsq{oh}"])
                            P.mm(pn2[:], onesb[:], zsq[oh][:], start=(oh == 0), stop=(oh == 1), r=[f"zsq{oh}"], w=["pn2"])
                        P.act(ln_[:], pn2[:], AF.Ln, r=["pn2"], w=["ln_"], scale=1.0 / 256, bias=epsc[:, 0:1])
                        P.act(rs_[:], ln_[:], AF.Exp, r=["ln_"], w=["rs_"], scale=-0.5)
                        for oh in range(2):
                            P.stt(mo[oh][:], zz[oh][:], sng[:, oh:oh + 1], rs_[:], ALU.mult, ALU.mult, r=[f"zz{oh}", "sng", "rs_"], w=[f"mo{oh}"])
                            P.dma(mixT_d[oh * 128:(oh + 1) * 128, t0:t0 + TB], mo[oh][:], r=[f"mo{oh}"], w=["mixTd"])
                    P.emit(block)

            if stop == "S":
                return nc
            lam_init = 0.8 - 0.6 * math.exp(-0.3 * l)

            def attention(P, streams, nq_blocks, epilogue, psS, psAcc, Pt, cnt, acc_sets=1):
                ns = len(streams)
                LA = max(1, len(psS) // ns - 1)
                for qb_ in range(nq_blocks):
                    q0 = qb_ * TB
                    nkt = (q0 + TB) // 128
                    aoff = (qb_ % acc_sets) * (len(psAcc) // acc_sets)
                    held = {}

                    def qk(kt):
                        a_ = kt - q0 // 128
                        qs = max(a_, 0) * 128
                        for si_, st_ in enumerate(streams):
                            bi = cnt["s"] % len(psS); cnt["s"] += 1
                            ps_, psn = psS[bi], f"psS{bi}"
                            pt_, ptn = Pt[bi], f"Pt{bi}"
                            rows = st_["rows"]
                            P.mm(ps_[:, qs:TB], st_["kT"][rows, kt * 128:(kt + 1) * 128], st_["qT"][rows, q0 + qs:q0 + TB], r=[st_["kn"], st_["qn"]], w=[psn])
                            P.act(pt_[:, qs:TB], ps_[:, qs:TB], AF.Exp, r=[psn], w=[ptn], scale=0.125)
                            if a_ >= 0:
                                P.tt(pt_[:, qs:qs + 128], pt_[:, qs:qs + 128], tri[:], ALU.mult, r=[ptn, "tri"], w=[ptn], eng=("pool" if si_ % 2 else "dve"))
                            held[(kt, si_)] = (pt_, ptn, qs)

                    def av(kt):
                        for si_, st_ in enumerate(streams):
                            pt_, ptn, qs = held.pop((kt, si_))
                            for lf, ai in st_["vals"]:
                                P.mm(psAcc[aoff + ai][:, qs:TB], lf(kt), pt_[:, qs:TB], start=(kt == 0), stop=(kt == nkt - 1), r=[ptn, st_["vn"]], w=[f"acc{aoff + ai}"])

                    for kt in range(min(LA, nkt)):
                        qk(kt)
                    for kt in range(nkt):
                        if kt + LA < nkt:
                            qk(kt + LA)
                        av(kt)
                    epilogue(qb_, q0, aoff)

            with ExitStack() as es:
                sb = lambda n, s, d=F32: es.enter_context(nc.sbuf_tensor(U(n), list(s), d))
                qT = sb("qT", [128, S], BF16); kT = sb("kT", [128, S], BF16)
                Vt = sb("Vt", [128, NKT, 128], BF16)
                Pt = [sb(f"Pt{i}", [128, TB], BF16) for i in range(4)]
                lv = sb("lv", [128, 256]); lp = sb("lp", [128, 64]); ls = sb("ls", [128, 2]); nlam = sb("nlam", [128, 1])
                dsg = sb("dsg", [128, 1])
                rc0 = sb("rc0", [128, TB]); rc1 = sb("rc1", [128, TB]); o0 = sb("o0", [128, TB]); o1 = sb("o1", [128, TB])
                osq = sb("osq", [128, TB], BF16); dl = sb("dl", [128, TB]); dr = sb("dr", [128, TB])
                do_ = [sb(f"do{i}", [128, TB], BF16) for i in range(2)]
                psS = [es.enter_context(nc.psum_tensor(U(f"psS{i}"), [128, TB], F32)) for i in range(4)]
                psAcc = [es.enter_context(nc.psum_tensor(U(f"acc{i}"), [128, TB], F32)) for i in range(4)]
                with nc.Block() as block:
                    P.dma(lv[:], lam_d[l:l + 1, :].to_broadcast([128, 256]), w=["lv"])
                    P.dma(dsg[:], dsg_d[l], w=["dsg"])
                    for k2 in range(2):
                        P.tt(lp[:], lv[:, 128 * k2:128 * k2 + 64], lv[:, 128 * k2 + 64:128 * k2 + 128], ALU.mult, r=["lv", "ls"], w=["lp"])
                        P.add("dve", lambda e, k2=k2: e.tensor_reduce(out=ls[:, k2:k2 + 1], in_=lp[:], axis=AX.X, op=ALU.add), r=["lp"], w=["ls"])
                    P.act(ls[:], ls[:], AF.Exp, r=["ls"], w=["ls"])
                    P.stt(nlam[:], ls[:, 1:2], -lam_init, ls[:, 0:1], ALU.add, ALU.subtract, r=["ls"], w=["nlam"])
                    P.ts(dsg[:], dsg[:], 1.0 - lam_init, None, ALU.mult, r=["dsg"], w=["dsg"])
                    cnt = {"s": 0, "o": 0}
                    for h in range(4):
                        P.dma(qT[:], dqT_d[h], w=["qT"])
                        P.dma(kT[:], dkT_d[h], w=["kT"])
                        P.dma(Vt[:], dV_d.rearrange("(kt p) c -> p kt c", p=128)[:, :, h * 128:(h + 1) * 128], w=["Vt"])
                        streams = [dict(kT=kT, qT=qT, rows=slice(c * 64, (c + 1) * 64), kn="kT", qn="qT", vn="Vt",
                                        vals=[(lambda kt: Vt[:, kt, :], 2 * c), (lambda kt: onesb[:], 2 * c + 1)]) for c in range(2)]

                        def epi(qb_, q0, aoff, h=h):
                            P.act(rc0[:], psAcc[1][:], AF.Ln, r=["acc1"], w=["rc0"])
                            P.act(rc1[:], psAcc[3][:], AF.Ln, r=["acc3"], w=["rc1"])
                            P.act(rc0[:], rc0[:], AF.Exp, r=["rc0"], w=["rc0"], scale=-1.0)
                            P.act(rc1[:], rc1[:], AF.Exp, r=["rc1"], w=["rc1"], scale=-1.0)
                            P.tt(o0[:], psAcc[0][:], rc0[:], ALU.mult, r=["acc0", "rc0"], w=["o0"])
                            P.tt(o1[:], psAcc[2][:], rc1[:], ALU.mult, r=["acc2", "rc1"], w=["o1"])
                            P.stt(o0[:], o1[:], nlam[:, 0:1], o0[:], ALU.mult, ALU.add, r=["o1", "o0", "nlam"], w=["o0"])
                            P.act(osq[:], o0[:], AF.Square, r=["o0"], w=["osq"])
                            P.mm(psS[0][:], onesb[:], osq[:], r=["osq"], w=["psS0"])
                            P.act(dl[:], psS[0][:], AF.Ln, r=["psS0"], w=["dl"], scale=1.0 / 128, bias=epsc[:, 0:1])
                            P.act(dr[:], dl[:], AF.Exp, r=["dl"], w=["dr"], scale=-0.5)
                            d_ = do_[cnt["o"] % 2]; dn = f"do{cnt['o'] % 2}"; cnt["o"] += 1
                            P.stt(d_[:], o0[:], dsg[:, 0:1], dr[:], ALU.mult, ALU.mult, r=["o0", "dsg", "dr"], w=[dn])
                            P.dma(mixT_d[256 + h * 128:256 + (h + 1) * 128, q0:q0 + TB], d_[:], r=[dn], w=["mixTd"])

                        attention(P, streams, NB, epi, psS, psAcc, Pt, cnt)
                    P.emit(block)

            if stop == "D":
                return nc
            with ExitStack() as es:
                sb = lambda n, s, d=F32: es.enter_context(nc.sbuf_tensor(U(n), list(s), d))
                qT = sb("mqTs", [96, S], BF16); kT = sb("mkTs", [96, S], BF16)
                Vt = sb("mVt", [128, NKT, 128], BF16)
                onesS = sb("onesS", [96, S], BF16); tmpS = sb("tmpS", [96, S], BF16)
                Pt = [sb(f"Pt{i}", [128, TB], BF16) for i in range(4)]
                mng = sb("mngs", [64, 1])
                osb = sb("osb", [128, TB]); rc0 = sb("mrc", [64, TB]); o0 = sb("mo0", [64, TB])
                osq = sb("mosq", [64, TB], BF16); dl = sb("mdl", [64, TB]); dr = sb("mdr", [64, TB])
                do_ = [sb(f"mdo{i}", [64, TB], BF16) for i in range(2)]
                psS = [es.enter_context(nc.psum_tensor(U(f"psS{i}"), [128, TB], F32)) for i in range(4)]
                psAcc = [es.enter_context(nc.psum_tensor(U(f"acc{i}"), [128, TB], F32)) for i in range(2)]
                pmv = es.enter_context(nc.psum_tensor(U("pmv"), [128, TB], F32))
                pnm = es.enter_context(nc.psum_tensor(U("pnm"), [128, TB], F32))
                with nc.Block() as block:
                    P.dma(mng[:], mng_d[l], w=["mng"])
                    P.memset(Vt[:], 1.0, w=["Vt"], eng="pool")
                    P.memset(onesS[64:96, :], 1.0, w=["onesS"], eng="pool")
                    P.add("pool", lambda e: e.affine_select(out=tmpS[64:96, :], in_=onesS[64:96, :], pattern=[[1, S]], compare_op=ALU.is_ge, fill=0.0, base=0, channel_multiplier=-256), r=["onesS"], w=["tmpS"])
                    P.add("pool", lambda e: e.affine_select(out=kT[64:96, :], in_=tmpS[64:96, :], pattern=[[-1, S]], compare_op=ALU.is_ge, fill=0.0, base=255, channel_multiplier=256), r=["tmpS"], w=["kT1h"])
                    cnt = {"s": 0, "o": 0, "a": 0}
                    for h in range(4):
                        P.dma(qT[:], mqT_d[h], w=["qT"])
                        P.dma(kT[0:64, :], mkT_d[h], r=["kT1h"], w=["kT"])
                        P.dma(Vt[:, :, 0:64], mV_d.rearrange("(kt p) c -> p kt c", p=128)[:, :, h * 64:(h + 1) * 64], w=["Vt"])
                        streams = [dict(kT=kT, qT=qT, rows=slice(0, 96), kn="kT", qn="qT", vn="Vt", vals=[(lambda kt: Vt[:, kt, :], 0)])]

                        def epi(qb_, q0, aoff, h=h):
                            P.copy(osb[:], psAcc[aoff][:], r=[f"acc{aoff}"], w=["osb"], eng="act")
                            P.mm(pmv[0:64, :], ident[:, 64:128], osb[:], r=["osb"], w=["pmv"])
                            P.act(rc0[:], pmv[0:64, :], AF.Ln, r=["pmv"], w=["rc0"])
                            P.act(rc0[:], rc0[:], AF.Exp, r=["rc0"], w=["rc0"], scale=-1.0)
                            P.tt(o0[:], osb[0:64, :], rc0[:], ALU.mult, r=["osb", "rc0"], w=["o0"])
                            P.act(osq[:], o0[:], AF.Square, r=["o0"], w=["osq"])
                            P.mm(pnm[0:64, :], onesb[0:64, 0:64], osq[:], r=["osq"], w=["pnm"])
                            P.act(dl[:], pnm[0:64, :], AF.Ln, r=["pnm"], w=["dl"], scale=1.0 / 64, bias=epsc[0:64, 0:1])
                            P.act(dr[:], dl[:], AF.Exp, r=["dl"], w=["dr"], scale=-0.5)
                            d_ = do_[cnt["o"] % 2]; dn = f"mdo{cnt['o'] % 2}"; cnt["o"] += 1
                            P.stt(d_[:], o0[:], mng[:, 0:1], dr[:], ALU.mult, ALU.mult, r=["o0", "mng", "dr"], w=[dn])
                            P.dma(mixT_d[768 + h * 64:768 + (h + 1) * 64, q0:q0 + TB], d_[:], r=[dn], w=["mixTd"])

                        attention(P, streams, NB, epi, psS, psAcc, Pt, cnt, acc_sets=2)
                    P.emit(block)

            if stop == "M":
                return nc
            with ExitStack() as es:
                sb = lambda n, s, d=F32: es.enter_context(nc.sbuf_tensor(U(n), list(s), d))
                wo = sb("wo", [128, 8, D], BF16)
                xbs = [sb(f"xb{i}", [128, 8, TB]) for i in range(2)]
                mxs = [sb(f"mx{i}", [128, 8, TB], BF16) for i in range(2)]
                pc = [es.enter_context(nc.psum_tensor(U(f"pc{i}"), [128, TB], F32)) for i in range(4)]
                with nc.Block() as block:
                    load_cast(P, wo, wout_d[l], 8, D, "wo", step=1024)
                    mixv = mixT_d.rearrange("(ft p) t -> p ft t", p=128)

                    def ld(i):
                        P.dma(xbs[i % 2][:], xTv[:, :, i * TB:(i + 1) * TB], r=["xTd"], w=[f"xb{i % 2}_{ft}" for ft in range(8)])
                        P.dma(mxs[i % 2][:], mixv[:, :, i * TB:(i + 1) * TB], w=[f"mx{i % 2}"])
                    ld(0)
                    k = 0
                    for i in range(NB):
                        if i + 1 < NB:
                            ld(i + 1)
                        xb = xbs[i % 2]; mx = mxs[i % 2]
                        for fo in range(8):
                            p_ = pc[k % 4]; pn_ = f"pc{k % 4}"; k += 1
                            for kt in range(8):
                                P.mm(p_[:], wo[:, kt, fo * 128:(fo + 1) * 128], mx[:, kt, :], start=(kt == 0), stop=(kt == 7), r=[f"wo_{kt}_0", f"mx{i % 2}"], w=[pn_])
                            P.stt(xb[:, fo, :], p_[:], modv[:, l, 16 + fo:17 + fo], xb[:, fo, :], ALU.mult, ALU.add, r=[pn_, f"xb{i % 2}_{fo}"], w=[f"xb{i % 2}_{fo}"])
                        P.dma(xTv[:, :, i * TB:(i + 1) * TB], xb[:], r=[f"xb{i % 2}_{ft}" for ft in range(8)], w=["xTd"])
                    P.emit(block)

            if stop == "C1":
                return nc
            with ExitStack() as es:
                sb = lambda n, s, d=F32: es.enter_context(nc.sbuf_tensor(U(n), list(s), d))
                w1 = sb("w1", [128, 8, DFF], BF16)
                w2 = sb("w2", [128, 32, D], BF16)
                xb = sb("xb", [128, 8, TB])
                hT = sb("hT", [128, 8, TB], BF16)
                hid = sb("hid", [128, 32, TB], BF16)
                sqs = [sb(f"sq{i}", [128, TB], BF16) for i in range(2)]
                tmps = [sb(f"nt{i}", [128, TB]) for i in range(2)]
                rstd = sb("rstd", [128, TB]); lnv = sb("lnv", [128, TB])
                rl = [sb(f"rl{i}", [128, TB]) for i in range(2)]
                pc = [es.enter_context(nc.psum_tensor(U(f"pc{i}"), [128, TB], F32)) for i in range(5)]
                pd = [es.enter_context(nc.psum_tensor(U(f"pd{i}"), [128, TB], F32)) for i in range(2)]
                pn = es.enter_context(nc.psum_tensor(U("pn"), [128, TB], F32))
                with nc.Block() as block:
                    load_cast(P, w1, w1_d[l], 8, DFF, "w1", step=1024, col_major=True)
                    load_cast(P, w2, w2_d[l], 32, D, "w2", step=1024)
                    k = 0; k2 = 0
                    for i in range(NB):
                        P.dma(xb[:], xTv[:, :, i * TB:(i + 1) * TB], r=["xTd"], w=[f"xb_{ft}" for ft in range(8)] + ["xb"])
                        norm_mod(P, xb, "xb", hT, "hT", A2, lambda kt: modv[:, l, 24 + kt:25 + kt], l, pn, tmps, sqs, rstd, lnv)
                        hres = [f"hT_{kt}" for kt in range(8)]
                        for ft in range(32):
                            p_ = pc[k % 5]; pn_ = f"pc{k % 5}"; r_ = rl[k % 2]; rn = f"rl{k % 2}"; k += 1
                            for kt in range(8):
                                P.mm(p_[:], w1[:, kt, ft * 128:(ft + 1) * 128], hT[:, kt, :], start=(kt == 0), stop=(kt == 7), r=[f"w1_{kt}_{ft // 8}", hres[kt]], w=[pn_])
                            P.act(r_[:], p_[:], AF.Relu, r=[pn_], w=[rn])
                            P.tt(hid[:, ft, :], r_[:], r_[:], ALU.mult, r=[rn], w=[f"hid{ft}"], eng=("pool" if ft % 2 else "dve"))
                        for fo in range(8):
                            p_ = pd[k2 % 2]; pn_ = f"pd{k2 % 2}"; k2 += 1
                            for ft in range(32):
                                P.mm(p_[:], w2[:, ft, fo * 128:(fo + 1) * 128], hid[:, ft, :], start=(ft == 0), stop=(ft == 31), r=[f"w2_{ft}_0", f"hid{ft}"], w=[pn_])
                            P.stt(xb[:, fo, :], p_[:], modv[:, l, 40 + fo:41 + fo], xb[:, fo, :], ALU.mult, ALU.add, r=[pn_, "xb", f"xb_{fo}"], w=[f"xb_{fo}"])
                        P.dma(xTv[:, :, i * TB:(i + 1) * TB], xb[:], r=[f"xb_{ft}" for ft in range(8)], w=["xTd", "xb"])
                    P.emit(block)

        if stop == "C2":
            return nc
        with ExitStack() as es:
            sb = lambda n, s, d=F32: es.enter_context(nc.sbuf_tensor(U(n), list(s), d))
            xbs = [sb(f"xb{i}", [128, 8, TB]) for i in range(2)]
            yn = sb("yn", [128, 8, TB])
            sqs = [sb(f"sq{i}", [128, TB], BF16) for i in range(2)]
            rstd = sb("rstd", [128, TB]); lnv = sb("lnv", [128, TB])
            ost = [sb(f"ost{i}", [128, D]) for i in range(2)]
            pt = [es.enter_context(nc.psum_tensor(U(f"pt{i}"), [128, TB], F32)) for i in range(6)]
            pn = es.enter_context(nc.psum_tensor(U("pn"), [128, TB], F32))
            with nc.Block() as block:
                P.dma(xbs[0][:], xTv[:, :, 0:TB], w=["xb0"])
                k = 0; ko = 0
                for i in range(NB):
                    if i + 1 < NB:
                        P.dma(xbs[(i + 1) % 2][:], xTv[:, :, (i + 1) * TB:(i + 2) * TB], w=[f"xb{(i + 1) % 2}"])
                    xb = xbs[i % 2]; xn = f"xb{i % 2}"
                    for kt in range(8):
                        sq = sqs[kt % 2]; sqn = f"sq{kt % 2}"
                        P.act(sq[:], xb[:, kt, :], AF.Square, r=[xn], w=[sqn])
                        P.mm(pn[:], onesb[:], sq[:], start=(kt == 0), stop=(kt == 7), r=[sqn], w=["pn"])
                    P.act(lnv[:], pn[:], AF.Ln, r=["pn"], w=["lnv"], scale=1.0 / D, bias=epsc[:, 0:1])
                    P.act(rstd[:], lnv[:], AF.Exp, r=["lnv"], w=["rstd"], scale=-0.5)
                    for kt in range(8):
                        P.stt(yn[:, kt, :], xb[:, kt, :], fgT[:, kt:kt + 1], rstd[:], ALU.mult, ALU.mult, r=[xn, "rstd"], w=[f"yn{kt}"])
                    for tt in range(4):
                        o_ = ost[ko % 2]; on = f"ost{ko % 2}"; ko += 1
                        for hf in range(2):
                            p_ = pt[k % 6]; pn_ = f"pt{k % 6}"; k += 1
                            for kk in range(4):
                                kt = hf * 4 + kk
                                P.tr(p_[:, kk * 128:(kk + 1) * 128], yn[:, kt, tt * 128:(tt + 1) * 128], ident[:], r=[f"yn{kt}"], w=[pn_])
                            P.copy(o_[:, hf * 512:(hf + 1) * 512], p_[:], r=[pn_], w=[f"{on}_{hf}"], eng=("act" if hf else "dve"))
                        P.dma(out_d[i * TB + tt * 128:i * TB + (tt + 1) * 128, :], o_[:], r=[f"{on}_0", f"{on}_1"], w=["outd"])
                P.emit(block)
    return nc


def _layout_inputs(inp, S):
    f = lambda a: np.ascontiguousarray(a, dtype=np.float32)
    col8 = lambda v: f(np.asarray(v).reshape(8, 128).T)
    cst = np.zeros((128, 2), np.float32)
    inv = (ROPE_THETA ** (-np.arange(0, 16, 2, dtype=np.float32) / 16)).astype(np.float32)
    for s0 in (0, 64):
        for d in range(16):
            cst[s0 + d, 0] = inv[d % 8]
            cst[s0 + d, 1] = -1.0 if d < 8 else 1.0
    L = DEPTH
    shared = {
        "cst": cst,
        "w_ada": f(inp["w_ada"]),
        "b_adaT": f(np.asarray(inp["b_ada"]).reshape(L, 48, 128).transpose(0, 2, 1)),
        "n1gT": f(np.asarray(inp["norm1_g"]).reshape(L, 8, 128).transpose(0, 2, 1)),
        "n2gT": f(np.asarray(inp["norm2_g"]).reshape(L, 8, 128).transpose(0, 2, 1)),
        "fgT": col8(inp["final_g"]),
        "w_in": f(inp["w_in"]), "w_out": f(inp["w_out"]), "mlp_w1": f(inp["mlp_w1"]), "mlp_w2": f(inp["mlp_w2"]),
        "s_are": f(np.asarray(inp["ssm_a_re"]).reshape(L, 8, 128).transpose(0, 2, 1)),
        "s_aim": f(np.asarray(inp["ssm_a_im"]).reshape(L, 8, 128).transpose(0, 2, 1)),
        "s_ldt": f(np.repeat(np.asarray(inp["ssm_log_dt"]).reshape(L, 8, 2), 64, axis=2).transpose(0, 2, 1)),
        "s_bre": f(np.asarray(inp["ssm_b_re"]).reshape(L, 8, 2, 64, 16).transpose(0, 2, 3, 1, 4).reshape(L, 128, 8, 16)),
        "s_bim": f(np.asarray(inp["ssm_b_im"]).reshape(L, 8, 2, 64, 16).transpose(0, 2, 3, 1, 4).reshape(L, 128, 8, 16)),
        "s_cre": f(np.asarray(inp["ssm_c_re"]).reshape(L, 8, 2, 16, 64).transpose(0, 2, 4, 1, 3).reshape(L, 128, 8, 16)),
        "s_cim": f(np.asarray(inp["ssm_c_im"]).reshape(L, 8, 2, 16, 64).transpose(0, 2, 4, 1, 3).reshape(L, 128, 8, 16)),
        "s_d": f(np.asarray(inp["ssm_d"]).reshape(L, 2, 128).transpose(0, 2, 1)),
        "s_gw": f(inp["ssm_glu_w"]),
        "s_gb": f(np.asarray(inp["ssm_glu_b"]).reshape(L, 2, 128).transpose(0, 2, 1)),
        "s_ng": f(np.asarray(inp["ssm_norm_g"]).reshape(L, 2, 128).transpose(0, 2, 1)),
        "lamv": f(np.concatenate([np.asarray(inp[k]) for k in ("diff_lq1", "diff_lk1", "diff_lq2", "diff_lk2")], axis=1)),
        "dsg": f(np.asarray(inp["diff_subln_g"]).reshape(L, 128, 1)),
        "mng": f(np.asarray(inp["moba_norm_g"]).reshape(L, 64, 1)),
    }
    x = np.asarray(inp["x"]); c = np.asarray(inp["c"]); pos = np.asarray(inp["positions"])
    idle = dict(shared)
    for k in ("w_ada", "w_in", "w_out", "mlp_w1", "mlp_w2"):
        idle[k] = np.zeros_like(shared[k])
    zx = np.zeros((S, D), np.float32)
    maps = []
    for core in range(8):
        b = (core // 2) % x.shape[0]
        if core % 2 == 0:
            m = dict(shared)
            m["x"] = f(x[b, :S])
        else:
            m = dict(idle)
            m["x"] = zx
        m["pos"] = np.ascontiguousarray(pos[b:b + 1, :S], dtype=np.int32)
        m["cT"] = col8(c[b])
        maps.append(m)
    return maps


def kernel(**inputs):
    S = SEQ
    nc = build(S)
    maps = _layout_inputs(inputs, S)
    res = run_bass_kernel_spmd(nc, maps, core_ids=list(range(8)))
    B = np.asarray(inputs["x"]).shape[0]
    return np.stack([np.asarray(res.results[2 * b]["out"], dtype=np.float32) for b in range(B)], axis=0)
```
